# Optimizing a Trainium2 kernel written in Bass

```python
import math, functools
import jax, jax.numpy as jnp
from jax import lax
import numpy as np

D_MODEL = 2048
BATCH = 4
SEQ = 8192
DEPTH = 1
DEC_BATCH = 1
DEC_SEQ = 16384
PAST_LEN = 128

SSM_WIDTH = D_MODEL // 2
SSM_GROUP = 16
SSM_GROUPS = SSM_WIDTH // SSM_GROUP
SSM_STATE = 64
HEAD_DIM = 128
HEADS_PER_GROUP = 4
DILATED_PATTERNS = ((128, 1), (512, 4), (2048, 16))
N_ATTN_GROUPS = len(DILATED_PATTERNS)
ATTN_HEADS = N_ATTN_GROUPS * HEADS_PER_GROUP
ATTN_WIDTH = ATTN_HEADS * HEAD_DIM
ATTN_OUT_WIDTH = HEADS_PER_GROUP * HEAD_DIM
REL_BUCKETS = 32
REL_MAX_DIST = 1024
D_FF = 4 * D_MODEL
Q_OFF = SSM_WIDTH
K_OFF = Q_OFF + ATTN_WIDTH
V_OFF = K_OFF + ATTN_WIDTH
G_OFF = V_OFF + ATTN_WIDTH
IN_COLS = G_OFF + 2 * D_MODEL
RMS_EPS = 1e-6
NEG_INF = -1e30

kernel_name = 'hybrid_s5_dilated_attn_encoder'


def rms_norm(x, g):
    x32 = x.astype(jnp.float32)
    y = x32 * lax.rsqrt(jnp.mean(x32 * x32, axis=-1, keepdims=True) + RMS_EPS)
    return (y * g.astype(jnp.float32)).astype(x.dtype)


def t5_bucket(rel):
    nb = REL_BUCKETS // 2
    max_exact = nb // 2
    ret = jnp.where(rel > 0, nb, 0)
    n = jnp.abs(rel)
    nf = jnp.maximum(n, 1).astype(jnp.float32)
    large = max_exact + (jnp.log(nf / max_exact) / math.log(REL_MAX_DIST / max_exact) * (nb - max_exact)).astype(jnp.int32)
    large = jnp.minimum(large, nb - 1)
    return ret + jnp.where(n < max_exact, n, large)


def s5_direction(u, a_re, a_im, log_dt, b_re, b_im, c_re, c_im, reverse):
    lam = lax.complex(a_re.astype(jnp.float32), a_im.astype(jnp.float32))
    dt = jnp.exp(log_dt.astype(jnp.float32))[:, None]
    a_bar = jnp.exp(lam * dt)
    b = lax.complex(b_re.astype(jnp.float32), b_im.astype(jnp.float32))
    b_bar = ((a_bar - 1.0) / lam)[..., None] * b
    bu = lax.complex(jnp.einsum('blgc,gpc->blgp', u, jnp.real(b_bar)),
                     jnp.einsum('blgc,gpc->blgp', u, jnp.imag(b_bar)))
    a_seq = jnp.broadcast_to(a_bar, bu.shape)

    def combine(left, right):
        return (left[0] * right[0], right[0] * left[1] + right[1])

    _, h = lax.associative_scan(combine, (a_seq, bu), axis=1, reverse=reverse)
    return (jnp.einsum('blgp,gcp->blgc', jnp.real(h), c_re.astype(jnp.float32))
            - jnp.einsum('blgp,gcp->blgc', jnp.imag(h), c_im.astype(jnp.float32)))


def dilated_window_attention(q, k, v, bias_table, window, dilation):
    bsz, seqlen, hg, hd = q.shape
    half = window // (2 * dilation)
    m = seqlen // dilation
    nblk = -(-m // half)
    mp = nblk * half

    def to_sub(t):
        return t.reshape(bsz, m, dilation, hg, hd).transpose(0, 2, 1, 3, 4).reshape(bsz * dilation, m, hg, hd)

    def from_sub(t):
        rest = t.shape[3:]
        t = t.reshape((bsz * dilation, mp) + rest)[:, :m]
        t = t.reshape((bsz, dilation, m) + rest)
        return jnp.swapaxes(t, 1, 2).reshape((bsz, seqlen) + rest)

    def key_blocks(t):
        tp = jnp.pad(to_sub(t), ((0, 0), (half, mp - m + half), (0, 0), (0, 0)))
        tp = tp.reshape(-1, nblk + 2, half, hg, hd)
        return jnp.concatenate([tp[:, :-2], tp[:, 1:-1], tp[:, 2:]], axis=2)

    qs = jnp.pad(to_sub(q), ((0, 0), (0, mp - m), (0, 0), (0, 0))).reshape(-1, nblk, half, hg, hd)
    kb = key_blocks(k)
    vb = key_blocks(v)

    qi = jnp.arange(half)[:, None]
    ki = jnp.arange(3 * half)[None, :]
    rel = ki - half - qi
    bias = bias_table[t5_bucket(rel * dilation)].astype(jnp.float32).transpose(2, 0, 1)
    key_idx = jnp.arange(nblk)[:, None, None] * half + (ki - half)[None]
    valid = (jnp.abs(rel) <= half)[None] & (key_idx >= 0) & (key_idx < m)

    s = jnp.einsum('bnqhd,bnkhd->bnhqk', qs, kb) * (hd ** -0.5) + bias[None, None]
    s = jnp.where(valid[None, :, None], s, NEG_INF)
    smax = jnp.max(s, axis=-1, keepdims=True)
    p = jnp.exp(s - smax)
    den = jnp.sum(p, axis=-1, keepdims=True)
    o = jnp.einsum('bnhqk,bnkhd->bnhqd', p, vb) / den
    lse = (smax + jnp.log(den))[..., 0]
    o = from_sub(o.transpose(0, 1, 3, 2, 4))
    lse = from_sub(lse.transpose(0, 1, 3, 2))
    return o, lse


def encoder_layer(x, norm1_g, w_in, ssm_a_re, ssm_a_im, ssm_log_dt, ssm_b_re, ssm_b_im, ssm_c_re, ssm_c_im,
                  ssm_d, w_glu, b_glu, w_br_ssm, w_br_attn, w_out, norm2_g, w_ff1, w_ff2, rel_bias):
    bsz, seqlen, _ = x.shape
    xn = rms_norm(x, norm1_g)
    proj = jnp.einsum('bld,dc->blc', xn, w_in)
    u = proj[..., :Q_OFF].astype(jnp.float32).reshape(bsz, seqlen, SSM_GROUPS, SSM_GROUP)
    q = proj[..., Q_OFF:K_OFF].astype(jnp.float32).reshape(bsz, seqlen, ATTN_HEADS, HEAD_DIM)
    k = proj[..., K_OFF:V_OFF].astype(jnp.float32).reshape(bsz, seqlen, ATTN_HEADS, HEAD_DIM)
    v = proj[..., V_OFF:G_OFF].astype(jnp.float32).reshape(bsz, seqlen, ATTN_HEADS, HEAD_DIM)
    g_ssm = jax.nn.sigmoid(proj[..., G_OFF:G_OFF + D_MODEL].astype(jnp.float32))
    g_attn = jax.nn.sigmoid(proj[..., G_OFF + D_MODEL:].astype(jnp.float32))

    y = ssm_d.astype(jnp.float32).reshape(SSM_GROUPS, SSM_GROUP) * u
    for direction in range(2):
        y = y + s5_direction(u, ssm_a_re[direction], ssm_a_im[direction], ssm_log_dt[direction],
                             ssm_b_re[direction], ssm_b_im[direction], ssm_c_re[direction], ssm_c_im[direction],
                             reverse=bool(direction))
    z = jax.nn.gelu(y.reshape(bsz, seqlen, SSM_WIDTH))
    y_ssm = z * jax.nn.sigmoid(jnp.einsum('blc,ce->ble', z, w_glu) + b_glu)
    branch_ssm = jnp.einsum('blc,cd->bld', y_ssm, w_br_ssm)

    outs, lses = [], []
    for gi, (window, dilation) in enumerate(DILATED_PATTERNS):
        hs = slice(gi * HEADS_PER_GROUP, (gi + 1) * HEADS_PER_GROUP)
        o, l = dilated_window_attention(q[:, :, hs], k[:, :, hs], v[:, :, hs], rel_bias[:, hs], window, dilation)
        outs.append(o)
        lses.append(l)
    outs = jnp.stack(outs, axis=0)
    weights = jax.nn.softmax(jnp.stack(lses, axis=0), axis=0)
    y_attn = jnp.sum(weights[..., None] * outs, axis=0).reshape(bsz, seqlen, ATTN_OUT_WIDTH)
    branch_attn = jnp.einsum('blc,cd->bld', y_attn, w_br_attn)

    mixed = g_ssm * branch_ssm + g_attn * branch_attn
    h = x + jnp.einsum('bld,de->ble', mixed, w_out)

    hn = rms_norm(h, norm2_g)
    act = jnp.square(jax.nn.relu(jnp.einsum('bld,df->blf', hn, w_ff1)))
    return h + jnp.einsum('blf,fd->bld', act, w_ff2)


def encoder_trunk(x, norm1_g, w_in, ssm_a_re, ssm_a_im, ssm_log_dt, ssm_b_re, ssm_b_im, ssm_c_re, ssm_c_im,
                  ssm_d, w_glu, b_glu, w_br_ssm, w_br_attn, w_out, norm2_g, w_ff1, w_ff2, rel_bias, final_g):
    for layer in range(DEPTH):
        x = encoder_layer(x, norm1_g[layer], w_in[layer], ssm_a_re[layer], ssm_a_im[layer], ssm_log_dt[layer],
                          ssm_b_re[layer], ssm_b_im[layer], ssm_c_re[layer], ssm_c_im[layer], ssm_d[layer],
                          w_glu[layer], b_glu[layer], w_br_ssm[layer], w_br_attn[layer], w_out[layer],
                          norm2_g[layer], w_ff1[layer], w_ff2[layer], rel_bias)
    return rms_norm(x, final_g)


def setup_inputs(seed: int = 0) -> dict:
    key = jax.random.key(seed)
    ks = jax.random.split(key, 24)
    f32 = jnp.float32

    def nrm(k, shape, scale):
        return jax.random.normal(k, shape, f32) * scale

    g2 = (DEPTH, 2, SSM_GROUPS)
    a_im_base = jnp.pi * jnp.arange(SSM_STATE, dtype=f32)
    return {
        'x_prompt': nrm(ks[0], (BATCH, SEQ, D_MODEL), 1.0),
        'x_sample': nrm(ks[1], (DEC_BATCH, DEC_SEQ, D_MODEL), 1.0),
        'norm1_g': 1.0 + nrm(ks[2], (DEPTH, D_MODEL), 0.01),
        'w_in': nrm(ks[3], (DEPTH, D_MODEL, IN_COLS), D_MODEL ** -0.5),
        'ssm_a_re': -0.5 + nrm(ks[4], g2 + (SSM_STATE,), 0.02),
        'ssm_a_im': a_im_base + nrm(ks[5], g2 + (SSM_STATE,), 0.02),
        'ssm_log_dt': jax.random.uniform(ks[6], g2, f32, math.log(1e-3), math.log(1e-1)),
        'ssm_b_re': nrm(ks[7], g2 + (SSM_STATE, SSM_GROUP), (2 * SSM_GROUP) ** -0.5),
        'ssm_b_im': nrm(ks[8], g2 + (SSM_STATE, SSM_GROUP), (2 * SSM_GROUP) ** -0.5),
        'ssm_c_re': nrm(ks[9], g2 + (SSM_GROUP, SSM_STATE), SSM_STATE ** -0.5),
        'ssm_c_im': nrm(ks[10], g2 + (SSM_GROUP, SSM_STATE), SSM_STATE ** -0.5),
        'ssm_d': nrm(ks[11], (DEPTH, SSM_WIDTH), 1.0),
        'w_glu': nrm(ks[12], (DEPTH, SSM_WIDTH, SSM_WIDTH), SSM_WIDTH ** -0.5),
        'b_glu': nrm(ks[13], (DEPTH, SSM_WIDTH), 0.01),
        'w_br_ssm': nrm(ks[14], (DEPTH, SSM_WIDTH, D_MODEL), SSM_WIDTH ** -0.5),
        'w_br_attn': nrm(ks[15], (DEPTH, ATTN_OUT_WIDTH, D_MODEL), ATTN_OUT_WIDTH ** -0.5),
        'w_out': nrm(ks[16], (DEPTH, D_MODEL, D_MODEL), D_MODEL ** -0.5),
        'norm2_g': 1.0 + nrm(ks[17], (DEPTH, D_MODEL), 0.01),
        'w_ff1': nrm(ks[18], (DEPTH, D_MODEL, D_FF), D_MODEL ** -0.5),
        'w_ff2': nrm(ks[19], (DEPTH, D_FF, D_MODEL), D_FF ** -0.5),
        'rel_bias': nrm(ks[20], (REL_BUCKETS, ATTN_HEADS), 0.5),
        'final_g': 1.0 + nrm(ks[21], (D_MODEL,), 0.01),
    }


def reference(x_prompt, x_sample, norm1_g, w_in, ssm_a_re, ssm_a_im, ssm_log_dt, ssm_b_re, ssm_b_im, ssm_c_re,
              ssm_c_im, ssm_d, w_glu, b_glu, w_br_ssm, w_br_attn, w_out, norm2_g, w_ff1, w_ff2, rel_bias, final_g):
    run = functools.partial(encoder_trunk, norm1_g=norm1_g, w_in=w_in, ssm_a_re=ssm_a_re, ssm_a_im=ssm_a_im,
                            ssm_log_dt=ssm_log_dt, ssm_b_re=ssm_b_re, ssm_b_im=ssm_b_im, ssm_c_re=ssm_c_re,
                            ssm_c_im=ssm_c_im, ssm_d=ssm_d, w_glu=w_glu, b_glu=b_glu, w_br_ssm=w_br_ssm,
                            w_br_attn=w_br_attn, w_out=w_out, norm2_g=norm2_g, w_ff1=w_ff1, w_ff2=w_ff2,
                            rel_bias=rel_bias, final_g=final_g)
    y_prompt = run(x_prompt)
    y_sample = run(x_sample)
    return (y_prompt, y_sample)
```

```python
import math
from contextlib import ExitStack

import numpy as np
import concourse.bass as bass
import concourse.mybir as mybir
from concourse.bass_utils import run_bass_kernel_spmd

F32 = mybir.dt.float32
BF16 = mybir.dt.bfloat16
AF = mybir.ActivationFunctionType
ALU = mybir.AluOpType

D = 2048
KT = 16
SSMW = 1024
NG = 64
NS = 64
NH = 12
HD = 128
INC = 9728
Q_OFF, K_OFF, V_OFF, G_OFF = 1024, 2560, 4096, 5632
DFF = 8192
NB = 512
HALO = 1024
NEG = -30000.0
EPS = 1e-6
DIL = (1, 4, 16)
GELU_C = 2.0 * math.sqrt(2.0 / math.pi)


class Buf:
    __slots__ = ("w", "r", "name")

    def __init__(self, name=""):
        self.w = {}
        self.r = {}
        self.name = name


class Eng:
    def __init__(self, eng, sem, is_pe=False):
        self.eng = eng
        self.sem = sem
        self.cnt = 0
        self.known = {}
        self.is_pe = is_pe


class Sync:
    NR = 16

    def __init__(self, nc, es):
        self.nc = nc
        self.sems = {}

        def mk(name):
            s = es.enter_context(nc.semaphore(name))
            self.sems[id(s)] = s
            return s

        self.pe = Eng(nc.tensor, mk("s_pe"), is_pe=True)
        self.act = Eng(nc.scalar, mk("s_act"))
        self.dve = Eng(nc.vector, mk("s_dve"))
        self.pool = Eng(nc.gpsimd, mk("s_pool"))
        self.sp = Eng(nc.sync, mk("s_sp"))
        self.ring = [mk("s_dma%d" % i) for i in range(self.NR)]
        self.ring_cnt = [0] * self.NR
        self.n_dma = 0

    def _waits(self, E, reads, writes):
        need = {}
        for b in reads:
            for k, v in b.w.items():
                if need.get(k, 0) < v:
                    need[k] = v
        for b in writes:
            for k, v in b.w.items():
                if need.get(k, 0) < v:
                    need[k] = v
            for k, v in b.r.items():
                if need.get(k, 0) < v:
                    need[k] = v
        for k, v in need.items():
            if E.is_pe and k == id(E.sem):
                continue
            if E.known.get(k, 0) >= v:
                continue
            E.eng.wait_ge(self.sems[k], v)
            E.known[k] = v

    def op(self, E, fn, reads=(), writes=()):
        self._waits(E, reads, writes)
        inst = fn()
        E.cnt += 1
        inst.then_inc(E.sem, 1)
        k = id(E.sem)
        for b in writes:
            b.w = {k: E.cnt}
            b.r = {}
        for b in reads:
            if b.r.get(k, 0) < E.cnt:
                b.r[k] = E.cnt

    def dma(self, out, in_, reads=(), writes=(), q=None, **kw):
        E = q if q is not None else self.sp
        i = self.n_dma % self.NR
        self.n_dma += 1
        sem = self.ring[i]
        k = id(sem)
        prev = self.ring_cnt[i]
        if prev > 0 and E.known.get(k, 0) < prev:
            E.eng.wait_ge(sem, prev)
            E.known[k] = prev
        self._waits(E, reads, writes)
        E.eng.dma_start(out=out, in_=in_, **kw).then_inc(sem, 16)
        self.ring_cnt[i] = prev + 16
        for b in writes:
            b.w = {k: prev + 16}
            b.r = {}
        for b in reads:
            if b.r.get(k, 0) < prev + 16:
                b.r[k] = prev + 16

    def barrier(self):
        engs = (self.pe, self.act, self.dve, self.pool, self.sp)
        for E in engs:
            for X in engs:
                if X is E or X.cnt == 0:
                    continue
                k = id(X.sem)
                if E.known.get(k, 0) < X.cnt:
                    E.eng.wait_ge(X.sem, X.cnt)
                    E.known[k] = X.cnt
            for i in range(self.NR):
                c = self.ring_cnt[i]
                k = id(self.ring[i])
                if c > 0 and E.known.get(k, 0) < c:
                    E.eng.wait_ge(self.ring[i], c)
                    E.known[k] = c


def _t5_onehot():
    oh = np.zeros((3, 33, 384), np.float32)
    for gi, d in enumerate(DIL):
        for m in range(384):
            rel = 191 - m
            if abs(rel) > 64:
                oh[gi, 32, m] = 1.0
                continue
            r = rel * d
            ret = 16 if r > 0 else 0
            n = abs(r)
            nf = np.float32(max(n, 1))
            large = 8 + int(np.float32(np.log(nf / np.float32(8.0)) / np.float32(math.log(128.0)) * np.float32(8.0)))
            large = min(large, 15)
            b = ret + (n if n < 8 else large)
            oh[gi, b, m] = 1.0
    return oh


def build_program(L, dbg=False, CTX=0):
    assert L % NB == 0 and CTX % NB == 0 and (CTX == 0 or CTX >= 2 * HALO)
    NBLK = L // NB
    LP = L + 2 * HALO
    nc = bass.Bass("TRN2", target_bir_lowering=False)

    def din(name, shape, dt=F32):
        return nc.dram_tensor(name, list(shape), dt, kind="ExternalInput")

    def dscr_early(name, shape, dt):
        return nc.dram_tensor(name, list(shape), dt, kind="Internal")

    xs = din("xs", [L, D]).ap()
    kmask_h = din("kmask", [LP, 1])
    if CTX:
        xc = din("xc", [CTX, D]).ap()
        flags_in = din("flags", [128, 2]).ap()
        uc_scr = dscr_early("uc_scr", [SSMW, CTX], BF16).ap()
    norm1_g = din("norm1_g", [1, D]).ap()
    w_in = din("w_in", [1, D, INC]).ap()
    a_re = din("ssm_a_re", [1, 2, NG, NS]).ap()
    a_im = din("ssm_a_im", [1, 2, NG, NS]).ap()
    log_dt = din("ssm_log_dt", [1, 2, NG]).ap()
    b_re = din("ssm_b_re", [1, 2, NG, NS, 16]).ap()
    b_im = din("ssm_b_im", [1, 2, NG, NS, 16]).ap()
    c_re = din("ssm_c_re", [1, 2, NG, 16, NS]).ap()
    c_im = din("ssm_c_im", [1, 2, NG, 16, NS]).ap()
    ssm_d = din("ssm_d", [1, SSMW]).ap()
    w_glu = din("w_glu", [1, SSMW, SSMW]).ap()
    b_glu = din("b_glu", [1, SSMW]).ap()
    w_brs = din("w_br_ssm", [1, SSMW, D]).ap()
    w_bra = din("w_br_attn", [1, 512, D]).ap()
    w_out = din("w_out", [1, D, D]).ap()
    norm2_g = din("norm2_g", [1, D]).ap()
    w_ff1 = din("w_ff1", [1, D, DFF]).ap()
    w_ff2 = din("w_ff2", [1, DFF, D]).ap()
    rel_bias = din("rel_bias", [32, NH]).ap()
    final_g = din("final_g", [D]).ap()
    onehot = din("onehot", [3, 33, 384]).ap()
    ys = nc.dram_tensor("ys", [L, D], F32, kind="ExternalOutput").ap()
    dbg_out = {}
    if dbg:
        for nm, shp in (("d_yb", [SSMW, L]), ("d_ys", [SSMW, L]), ("d_ya", [512, L]), ("d_bias", [NH, 384])):
            dbg_out[nm] = nc.dram_tensor(nm, shp, F32, kind="ExternalOutput").ap()

    def dscr(name, shape, dt):
        return nc.dram_tensor(name, list(shape), dt, kind="Internal")

    wb_in = dscr("wb_in", [D, INC], BF16).ap()
    wb_glu = dscr("wb_glu", [SSMW, SSMW], BF16).ap()
    wb_brs = dscr("wb_brs", [SSMW, D], BF16).ap()
    wb_bra = dscr("wb_bra", [512, D], BF16).ap()
    wb_out = dscr("wb_out", [D, D], BF16).ap()
    wb_ff1 = dscr("wb_ff1", [D, DFF], BF16).ap()
    wb_ff2 = dscr("wb_ff2", [DFF, D], BF16).ap()
    kt_scr = dscr("kt_scr", [NH, HD, LP], BF16).ap()
    v_scr_h = dscr("v_scr", [LP, NH * HD], BF16)
    v_scr = v_scr_h.ap()
    u_scr = dscr("u_scr", [SSMW, L], BF16).ap()
    yb_scr = dscr("yb_scr", [SSMW, L], F32).ap()
    ys_scr = dscr("ys_scr", [SSMW, L], BF16).ap()
    bias_scr_h = dscr("bias_scr", [NH, 384], F32)
    bias_scr = bias_scr_h.ap()
    bias_rep_h = dscr("bias_rep", [NH, 128, 384], F32)
    bias_rep = bias_rep_h.ap()

    es = ExitStack()
    with es:
        S = Sync(nc, es)
        PE, ACT, DVE, POOL = S.pe, S.act, S.dve, S.pool
        es.enter_context(nc.Block())
        es.enter_context(nc.allow_non_contiguous_dma(reason="small parameter re-layouts"))

        uniq = [0]

        def sbt(stack, name, shape, dt=F32):
            uniq[0] += 1
            return stack.enter_context(nc.sbuf_tensor("%s_%d" % (name, uniq[0]), list(shape), dt))

        def V(fn, reads=(), writes=()):
            S.op(DVE, fn, reads, writes)

        def A(fn, reads=(), writes=()):
            S.op(ACT, fn, reads, writes)

        def G(fn, reads=(), writes=()):
            S.op(POOL, fn, reads, writes)

        def T(fn, reads=(), writes=()):
            S.op(PE, fn, reads, writes)

        def cp(i, out, in_, reads, writes):
            if i % 2 == 0:
                A(lambda: nc.scalar.copy(out, in_), reads, writes)
            else:
                V(lambda: nc.vector.tensor_copy(out=out, in_=in_), reads, writes)

        PS2 = [es.enter_context(nc.psum_tensor("ps2_%d" % i, [128, 1024], F32)) for i in range(4)]
        PB = [Buf("psb%d" % i) for i in range(8)]

        def bank(i):
            return PS2[i // 2][:, (i % 2) * 512:(i % 2) * 512 + 512]

        rr = [0]

        def ps_rr():
            i = rr[0] % 4
            rr[0] += 1
            return i

        ident_f = sbt(es, "ident_f", [128, 128], F32)
        ident_b = sbt(es, "ident_b", [128, 128], BF16)
        tri_f = sbt(es, "tri_f", [128, 128], BF16)
        tri_b = sbt(es, "tri_b", [128, 128], BF16)
        ntri_f = sbt(es, "ntri_f", [128, 128], BF16)
        ntri_b = sbt(es, "ntri_b", [128, 128], BF16)
        perm_f = sbt(es, "perm_f", [128, 128], F32)
        ones_b = sbt(es, "ones_b", [128, 128], BF16)
        sgn = sbt(es, "sgn", [128, 1], F32)
        gmask = sbt(es, "gmask", [128, 8], F32)
        iot = sbt(es, "iot", [128, 128], F32)
        gm_t = sbt(es, "gm_t", [128, 8], F32)
        BC = Buf("const")

        G(lambda: nc.gpsimd.iota(iot[:], [[1, 128]], base=0, channel_multiplier=-1,
                                 allow_small_or_imprecise_dtypes=True), writes=[BC])
        G(lambda: nc.gpsimd.iota(gm_t[:], [[16, 8]], base=0, channel_multiplier=-1,
                                 allow_small_or_imprecise_dtypes=True), writes=[BC])
        V(lambda: nc.vector.tensor_scalar(ident_f[:], iot[:], 0.0, None, ALU.is_equal), [BC], [BC])
        V(lambda: nc.vector.tensor_scalar(tri_f[:], iot[:], 0.0, None, ALU.is_ge), [BC], [BC])
        V(lambda: nc.vector.tensor_scalar(tri_b[:], iot[:], 0.0, None, ALU.is_le), [BC], [BC])
        V(lambda: nc.vector.tensor_scalar(ntri_f[:], tri_f[:], -1.0, None, ALU.mult), [BC], [BC])
        V(lambda: nc.vector.tensor_scalar(ntri_b[:], tri_b[:], -1.0, None, ALU.mult), [BC], [BC])
        V(lambda: nc.vector.tensor_scalar(perm_f[:], iot[:], 64.0, None, ALU.is_equal), [BC], [BC])
        V(lambda: nc.vector.tensor_scalar(ident_b[:], iot[:], -64.0, None, ALU.is_equal), [BC], [BC])
        V(lambda: nc.vector.tensor_tensor(perm_f[:], perm_f[:], ident_b[:], ALU.add), [BC], [BC])
        V(lambda: nc.vector.tensor_copy(out=ident_b[:], in_=ident_f[:]), [BC], [BC])
        V(lambda: nc.vector.memset(ones_b[:], 1.0), [BC], [BC])
        V(lambda: nc.vector.tensor_scalar(sgn[:], iot[:, 0:1], -63.5, None, ALU.is_le), [BC], [BC])
        V(lambda: nc.vector.tensor_scalar(sgn[:], sgn[:], 2.0, -1.0, ALU.mult, ALU.add), [BC], [BC])
        V(lambda: nc.vector.tensor_scalar(gmask[:], gm_t[:], 0.5, None, ALU.is_le), [BC], [BC])
        V(lambda: nc.vector.tensor_scalar(gm_t[:], gm_t[:], -15.5, None, ALU.is_ge), [BC], [BC])
        V(lambda: nc.vector.tensor_tensor(gmask[:], gmask[:], gm_t[:], ALU.mult), [BC], [BC])

        g1col = sbt(es, "g1col", [128, KT], F32)
        g2col = sbt(es, "g2col", [128, KT], F32)
        dcol = sbt(es, "dcol", [128, 8], F32)
        bgcol = sbt(es, "bgcol", [128, 8], F32)
        S.dma(g1col[:], norm1_g[0].rearrange("(k p) -> p k", p=128), writes=[BC])
        S.dma(g2col[:], norm2_g[0].rearrange("(k p) -> p k", p=128), writes=[BC])
        S.dma(dcol[:], ssm_d[0].rearrange("(k p) -> p k", p=128), writes=[BC])
        S.dma(bgcol[:], b_glu[0].rearrange("(k p) -> p k", p=128), writes=[BC])

        with ExitStack() as ps:
            NST = 3
            zero_b = sbt(ps, "zero_b", [128, 1536], BF16)
            V(lambda: nc.vector.memset(zero_b[:], 0.0), [BC], [BC])
            stg_f = [sbt(ps, "stgf%d" % i, [128, 2048], F32) for i in range(NST)]
            stg_b = [sbt(ps, "stgb%d" % i, [128, 2048], BF16) for i in range(NST)]
            bf_ = [Buf() for _ in range(NST)]
            bb_ = [Buf() for _ in range(NST)]
            cnt = [0]

            def conv(src, dst, rows, cols, scol=None):
                for r0 in range(0, rows, 128):
                    for c0 in range(0, cols, 2048):
                        cw = min(2048, cols - c0)
                        i = cnt[0] % NST
                        e = cnt[0] % 3
                        cnt[0] += 1
                        S.dma(stg_f[i][:, 0:cw], src[r0:r0 + 128, c0:c0 + cw], writes=[bf_[i]])
                        o, a = stg_b[i][:, 0:cw], stg_f[i][:, 0:cw]
                        if scol is not None:
                            sc = scol[:, r0 // 128:r0 // 128 + 1]
                            if e == 0:
                                V(lambda: nc.vector.tensor_scalar(o, a, sc, None, ALU.mult), [bf_[i], BC], [bb_[i]])
                            elif e == 1:
                                A(lambda: nc.scalar.activation(o, a, AF.Copy, scale=sc), [bf_[i], BC], [bb_[i]])
                            else:
                                G(lambda: nc.gpsimd.tensor_scalar(o, a, sc, None, ALU.mult), [bf_[i], BC], [bb_[i]])
                        else:
                            if e == 0:
                                V(lambda: nc.vector.tensor_copy(out=o, in_=a), [bf_[i]], [bb_[i]])
                            elif e == 1:
                                A(lambda: nc.scalar.copy(o, a), [bf_[i]], [bb_[i]])
                            else:
                                G(lambda: nc.gpsimd.tensor_copy(out=o, in_=a), [bf_[i]], [bb_[i]])
                        S.dma(dst[r0:r0 + 128, c0:c0 + cw], o, reads=[bb_[i]])

            conv(w_in[0], wb_in, D, INC, g1col)
            conv(w_glu[0], wb_glu, SSMW, SSMW)
            conv(w_brs[0], wb_brs, SSMW, D)
            conv(w_bra[0], wb_bra, 512, D)
            conv(w_out[0], wb_out, D, D)
            conv(w_ff1[0], wb_ff1, D, DFF, g2col)
            conv(w_ff2[0], wb_ff2, DFF, D)
            if not CTX:
                for h in range(NH):
                    S.dma(kt_scr[h, :, 0:HALO], zero_b[:, 0:HALO], reads=[BC])
                    S.dma(kt_scr[h, :, HALO + L:LP], zero_b[:, 0:HALO], reads=[BC])
                for r0 in list(range(0, HALO, 128)) + list(range(HALO + L, LP, 128)):
                    S.dma(v_scr[r0:r0 + 128, :], zero_b[:, :], reads=[BC])

            tab = sbt(ps, "tab33", [33, NH], F32)
            oh = sbt(ps, "oh33", [33, 3, 384], F32)
            bsb = sbt(ps, "bias_sb", [4, 3, 384], F32)
            btmp = Buf()
            BBT = Buf("bt")
            V(lambda: nc.vector.memset(tab[32:33, :], NEG), writes=[btmp])
            S.dma(tab[0:32, :], rel_bias[:, :], writes=[btmp])
            S.dma(oh[:], onehot.rearrange("g b m -> b g m"), writes=[btmp])
            for gi in range(3):
                T(lambda: nc.tensor.matmul(bank(gi)[0:4, 0:384], tab[:, gi * 4:(gi + 1) * 4], oh[:, gi, :],
                                           start=True, stop=True), [btmp], [PB[gi]])
                V(lambda: nc.vector.tensor_copy(out=bsb[:, gi, :], in_=bank(gi)[0:4, 0:384]), [PB[gi]], [btmp])
                S.dma(bias_scr[gi * 4:(gi + 1) * 4, :], bsb[:, gi, :], reads=[btmp], writes=[BBT])
            for h in range(NH):
                S.dma(bias_rep[h], bass.AP(bias_scr_h, h * 384, [[0, 128], [1, 384]]), reads=[BBT], writes=[BBT])
            if dbg:
                S.dma(dbg_out["d_bias"], bias_scr, reads=[BBT], writes=[BBT])
            S.barrier()

        def make_wring(stack):
            NW = 3
            wring = [sbt(stack, "wring%d" % i, [128, 16, 512], BF16) for i in range(NW)]
            wbuf = [Buf("w%d" % i) for i in range(NW)]
            wcnt = [0]

            def wload(src_rows, nk, c0, cw=512):
                i = wcnt[0] % NW
                wcnt[0] += 1
                S.dma(wring[i][:, 0:nk, 0:cw], src_rows.rearrange("(k p) c -> p k c", p=128)[:, :, c0:c0 + cw],
                      writes=[wbuf[i]])
                return wring[i], wbuf[i]
            return wload

        def make_norm(stack):
            xh = [sbt(stack, "xhat%d" % i, [128, D], BF16) for i in range(2)]
            ssq = sbt(stack, "ssq", [128, 4], F32)
            rstd = sbt(stack, "rstd", [128, 4], F32)
            B_xh = [Buf("xhat0"), Buf("xhat1")]
            B_st = Buf("st")

            def stats(src_tile, B_src):
                for tt in range(4):
                    A(lambda: nc.scalar.activation(xh[tt % 2][:], src_tile[:, tt, :], AF.Square, accum_out=ssq[:, tt:tt + 1]),
                      [B_src], [B_xh[tt % 2], B_st])
                A(lambda: nc.scalar.activation(rstd[:], ssq[:], AF.Sqrt, scale=1.0 / D, bias=EPS), [B_st], [B_st])
                V(lambda: nc.vector.reciprocal(rstd[:], rstd[:]), [B_st], [B_st])

            def rmsnorm_to_T(src_tile, dstT, B_src, B_dst):
                stats(src_tile, B_src)
                for tt in range(4):
                    xhat, B_xhat = xh[tt % 2], B_xh[tt % 2]
                    V(lambda: nc.vector.tensor_scalar(xhat[:], src_tile[:, tt, :], rstd[:, tt:tt + 1], None, ALU.mult),
                      [B_src, B_st], [B_xhat])
                    psv = PS2[tt % 2][:].bitcast(BF16)
                    pbs = [PB[2 * (tt % 2)], PB[2 * (tt % 2) + 1]]
                    for kt in range(KT):
                        T(lambda: nc.tensor.transpose(psv[:, kt * 128:(kt + 1) * 128], xhat[:, kt * 128:(kt + 1) * 128],
                                                      ident_b[:]), [B_xhat, BC], pbs)
                    A(lambda: nc.scalar.copy(dstT[:, :, tt * 128:(tt + 1) * 128],
                                             psv[:, 0:2048].rearrange("p (k t) -> p k t", k=KT)), pbs, [B_dst])
            return rmsnorm_to_T, stats, rstd, B_st

        def proj_ws(wload, wsrc_rows, nk, col0, n_mt, rhsT, B_rhs, evac):
            for m0 in range(0, n_mt, 4):
                nm = min(4, n_mt - m0)
                wt, wb = wload(wsrc_rows, nk, col0 + m0 * 128, nm * 128)
                for mi in range(nm):
                    bi = ps_rr()
                    for k in range(nk):
                        T(lambda: nc.tensor.matmul(bank(bi), wt[:, k, mi * 128:(mi + 1) * 128], rhsT[:, k, :],
                                                   start=(k == 0), stop=(k == nk - 1)), [wb, B_rhs], [PB[bi]])
                    evac(m0 + mi, bank(bi), PB[bi])

        with ExitStack() as ps:
            wload = make_wring(ps)
            rmsnorm_to_T, _, _, _ = make_norm(ps)
            xb = sbt(ps, "xb", [128, 4, D], F32)
            xnT = sbt(ps, "xnT", [128, KT, NB], BF16)
            kst = [sbt(ps, "kst%d" % i, [128, NB], BF16) for i in range(4)]
            B_x, B_xnT = Buf("x"), Buf("xnT")
            B_kst = [Buf() for _ in range(4)]
            kc = [0]
            NCB = CTX // NB
            jobs = [("own", b) for b in range(NBLK)] + [("ctx", b) for b in range(NCB)]
            for kind, b in jobs:
                t0 = b * NB
                if kind == "own":
                    xsrc, udst, kvoff = xs, u_scr, HALO + t0
                else:
                    xsrc, udst = xc, uc_scr
                    kvoff = (HALO + L + t0) if b < 2 else ((t0 - (CTX - HALO)) if b >= NCB - 2 else None)
                S.dma(xb[:], xsrc[t0:t0 + NB, :].rearrange("(t p) d -> p t d", p=128), writes=[B_x])
                rmsnorm_to_T(xb, xnT, B_x, B_xnT)

                def ev_u(mt, psap, psb):
                    i = kc[0] % 4
                    kc[0] += 1
                    cp(mt, kst[i][:], psap, [psb], [B_kst[i]])
                    S.dma(udst[mt * 128:(mt + 1) * 128, t0:t0 + NB], kst[i][:], reads=[B_kst[i]])
                proj_ws(wload, wb_in, KT, 0, 8, xnT, B_xnT, ev_u)
                if kvoff is None:
                    continue

                def ev_k(mt, psap, psb):
                    i = kc[0] % 4
                    kc[0] += 1
                    cp(mt, kst[i][:], psap, [psb], [B_kst[i]])
                    S.dma(kt_scr[mt, :, kvoff:kvoff + NB], kst[i][:], reads=[B_kst[i]])
                proj_ws(wload, wb_in, KT, K_OFF, NH, xnT, B_xnT, ev_k)
                for gi in range(3):
                    wt, wb = wload(wb_in, KT, V_OFF + gi * 512, 512)
                    for tt in range(4):
                        bi = ps_rr()
                        for k in range(KT):
                            T(lambda: nc.tensor.matmul(bank(bi), xnT[:, k, tt * 128:(tt + 1) * 128], wt[:, k, :],
                                                       start=(k == 0), stop=(k == KT - 1)), [wb, B_xnT], [PB[bi]])
                        i = kc[0] % 4
                        kc[0] += 1
                        cp(tt, kst[i][:], bank(bi), [PB[bi]], [B_kst[i]])
                        S.dma(v_scr[kvoff + tt * 128:kvoff + (tt + 1) * 128, gi * 512:(gi + 1) * 512], kst[i][:],
                              reads=[B_kst[i]])
            S.barrier()

        with ExitStack() as ps:
            T1re = sbt(ps, "T1re", [128, NG, NS], BF16)
            T1im = sbt(ps, "T1im", [128, NG, NS], BF16)
            T2re = sbt(ps, "T2re", [128, NG, 128], BF16)
            T2im = sbt(ps, "T2im", [128, NG, 128], BF16)
            G1 = sbt(ps, "G1", [128, NG], F32)
            G2 = sbt(ps, "G2", [128, NG], F32)
            Bmat = sbt(ps, "Bmat", [128, 8, 8, 128], BF16)
            Cm1 = sbt(ps, "Cm1", [128, NG, 16], BF16)
            Cm2 = sbt(ps, "Cm2", [128, NG, 16], BF16)
            BTAB = Buf("ssmtab")

            def cmul(o_re, o_im, x_re, x_im, y_re, y_im, t1, t2, bufs):
                f = lambda fn: S.op(DVE, fn, bufs, bufs)
                E = nc.vector
                f(lambda: E.tensor_tensor(t1, x_re, y_re, ALU.mult))
                f(lambda: E.tensor_tensor(t2, x_im, y_im, ALU.mult))
                f(lambda: E.tensor_tensor(t2, t1, t2, ALU.subtract))
                f(lambda: E.tensor_tensor(t1, x_re, y_im, ALU.mult))
                f(lambda: E.tensor_tensor(o_im, x_im, y_re, ALU.mult))
                f(lambda: E.tensor_tensor(o_im, o_im, t1, ALU.add))
                f(lambda: E.tensor_copy(out=o_re, in_=t2))

            def abar_of(ar, ai, ldt, shape, stack, tag):
                bufs = [BTAB]
                mk = lambda nm: sbt(stack, tag + nm, shape, F32)
                dt_, lr, th, mag, cs, sn, t1, t2 = (mk(n)[:] for n in ("dt", "lr", "th", "mag", "cs", "sn", "t1", "t2"))
                o = {k: mk(k)[:] for k in ("abr", "abi", "air", "aii", "fr", "fi")}
                f = lambda fn: S.op(DVE, fn, bufs, bufs)
                fa = lambda fn: S.op(ACT, fn, bufs, bufs)
                fa(lambda: nc.scalar.activation(dt_, ldt, AF.Exp))
                f(lambda: nc.vector.tensor_tensor(lr, ar, dt_, ALU.mult))
                f(lambda: nc.vector.tensor_tensor(th, ai, dt_, ALU.mult))
                f(lambda: nc.vector.tensor_copy(out=t2, in_=th))
                for jj in range(1, 8):
                    f(lambda: nc.vector.tensor_scalar(t1, th, (2 * jj - 1) * math.pi, -2.0 * math.pi, ALU.is_gt, ALU.mult))
                    f(lambda: nc.vector.tensor_tensor(t2, t2, t1, ALU.add))
                fa(lambda: nc.scalar.activation(sn, t2, AF.Sin))
                f(lambda: nc.vector.tensor_scalar(t1, t2, -1.0, None, ALU.mult))
                f(lambda: nc.vector.tensor_tensor(t1, t1, t2, ALU.max))
                f(lambda: nc.vector.tensor_scalar(t1, t1, -1.0, math.pi / 2, ALU.mult, ALU.add))
                fa(lambda: nc.scalar.activation(cs, t1, AF.Sin))
                fa(lambda: nc.scalar.activation(mag, lr, AF.Exp))
                f(lambda: nc.vector.tensor_tensor(o["abr"], mag, cs, ALU.mult))
                f(lambda: nc.vector.tensor_tensor(o["abi"], mag, sn, ALU.mult))
                fa(lambda: nc.scalar.activation(mag, lr, AF.Exp, scale=-1.0))
                f(lambda: nc.vector.tensor_tensor(o["air"], mag, cs, ALU.mult))
                f(lambda: nc.vector.tensor_tensor(o["aii"], mag, sn, ALU.mult))
                f(lambda: nc.vector.tensor_scalar(o["aii"], o["aii"], -1.0, None, ALU.mult))
                f(lambda: nc.vector.tensor_tensor(t1, ar, ar, ALU.mult))
                f(lambda: nc.vector.tensor_tensor(t2, ai, ai, ALU.mult))
                f(lambda: nc.vector.tensor_tensor(t1, t1, t2, ALU.add))
                f(lambda: nc.vector.reciprocal(t1, t1))
                f(lambda: nc.vector.tensor_scalar(cs, o["abr"], -1.0, None, ALU.add))
                f(lambda: nc.vector.tensor_tensor(t2, cs, ar, ALU.mult))
                f(lambda: nc.vector.tensor_tensor(mag, o["abi"], ai, ALU.mult))
                f(lambda: nc.vector.tensor_tensor(t2, t2, mag, ALU.add))
                f(lambda: nc.vector.tensor_tensor(o["fr"], t2, t1, ALU.mult))
                f(lambda: nc.vector.tensor_tensor(t2, o["abi"], ar, ALU.mult))
                f(lambda: nc.vector.tensor_tensor(mag, cs, ai, ALU.mult))
                f(lambda: nc.vector.tensor_tensor(t2, t2, mag, ALU.subtract))
                f(lambda: nc.vector.tensor_tensor(o["fi"], t2, t1, ALU.mult))
                return o

            def gen_ssm_tables(dr):
                rev = (dr == 1)
                bufs = [BTAB]
                f = lambda fn: S.op(DVE, fn, bufs, bufs)
                with ExitStack() as p2:
                    arT = sbt(p2, "arT", [128, NG], F32)
                    aiT = sbt(p2, "aiT", [128, NG], F32)
                    ldT = sbt(p2, "ldT", [128, NG], F32)
                    for half in (0, 64):
                        S.dma(arT[half:half + 64, :], a_re[0, dr].rearrange("g n -> n g"), writes=bufs)
                        S.dma(aiT[half:half + 64, :], a_im[0, dr].rearrange("g n -> n g"), writes=bufs)
                    S.dma(ldT[:], log_dt[0, dr].partition_broadcast(128), writes=bufs)
                    st = abar_of(arT[:], aiT[:], ldT[:], [128, NG], p2, "s_")
                    GB = 16
                    Are = sbt(p2, "Are", [128, GB, 128], F32)
                    Aim = sbt(p2, "Aim", [128, GB, 128], F32)
                    Nre = sbt(p2, "Nre", [128, GB, 128], F32)
                    Nim = sbt(p2, "Nim", [128, GB, 128], F32)
                    tA = sbt(p2, "tA", [128, GB, 64], F32)
                    tB = sbt(p2, "tB", [128, GB, 64], F32)
                    cur = [sbt(p2, "cur%d" % i, [128, GB], F32) for i in range(4)]
                    tc1 = sbt(p2, "tc1", [128, GB], F32)
                    tc2 = sbt(p2, "tc2", [128, GB], F32)

                    def sl(lo, hi):
                        return slice(128 - hi, 128 - lo) if rev else slice(lo, hi)
                    for gb in range(NG // GB):
                        gs = slice(gb * GB, (gb + 1) * GB)
                        for (Pre, Pim, b_re_, b_im_, one) in ((Are, Aim, st["abr"], st["abi"], True),
                                                               (Nre, Nim, st["air"], st["aii"], False)):
                            if one:
                                f(lambda: nc.vector.memset(Pre[:, :, sl(0, 1)], 1.0))
                                f(lambda: nc.vector.memset(Pim[:, :, sl(0, 1)], 0.0))
                            else:
                                f(lambda: nc.vector.tensor_copy(out=Pre[:, :, sl(0, 1)], in_=st["fr"][:, gs].unsqueeze(2)))
                                f(lambda: nc.vector.tensor_copy(out=Pim[:, :, sl(0, 1)], in_=st["fi"][:, gs].unsqueeze(2)))
                            f(lambda: nc.vector.tensor_copy(out=cur[0][:], in_=b_re_[:, gs]))
                            f(lambda: nc.vector.tensor_copy(out=cur[1][:], in_=b_im_[:, gs]))
                            cr, ci, nr, ni = cur[0], cur[1], cur[2], cur[3]
                            w = 1
                            while w < 128:
                                bc = lambda t: t[:].unsqueeze(2).broadcast_to([128, GB, w])
                                cmul(Pre[:, :, sl(w, 2 * w)], Pim[:, :, sl(w, 2 * w)], Pre[:, :, sl(0, w)], Pim[:, :, sl(0, w)],
                                     bc(cr), bc(ci), tA[:, :, 0:w], tB[:, :, 0:w], bufs)
                                if 2 * w < 128:
                                    cmul(nr[:], ni[:], cr[:], ci[:], cr[:], ci[:], tc1[:], tc2[:], bufs)
                                    cr, ci, nr, ni = nr, ni, cr, ci
                                w *= 2
                        f(lambda: nc.vector.tensor_copy(out=T2re[:, gs, :], in_=Are[:]))
                        f(lambda: nc.vector.tensor_copy(out=T2im[:, gs, :], in_=Aim[:]))
                        e127 = 0 if rev else 127
                        cmul(G1[:, gs], G2[:, gs], Are[:, :, e127], Aim[:, :, e127], st["abr"][:, gs], st["abi"][:, gs],
                             tc1[:], tc2[:], bufs)
                        for src_t, dst_t in ((Nre, T1re), (Nim, T1im)):
                            for q4 in range(GB // 8):
                                for gg in range(8):
                                    g_l = q4 * 8 + gg
                                    T(lambda: nc.tensor.transpose(PS2[0][:, gg * 64:(gg + 1) * 64], src_t[0:64, g_l, :],
                                                                  ident_f[0:64, 0:64]), bufs + [BC], [PB[0]])
                                g0 = gb * GB + q4 * 8
                                A(lambda: nc.scalar.copy(dst_t[:, g0:g0 + 8, :].rearrange("p g n -> p (g n)"),
                                                         PS2[0][:, 0:512]), [PB[0]], bufs)
                    f(lambda: nc.vector.tensor_scalar(G2[:], G2[:], sgn[:, 0:1], None, ALU.mult))
                with ExitStack() as p2:
                    bl = sbt(p2, "bl", [64, 2, NG, 16], F32)
                    S.dma(bl[:, 0], b_re[0, dr].rearrange("g n c -> n g c"), writes=bufs)
                    S.dma(bl[:, 1], b_im[0, dr].rearrange("g n c -> n g c"), writes=bufs)
                    bcomp = sbt(p2, "bcomp", [128, 8, 128], F32)
                    for kt in range(8):
                        for ri in range(2):
                            T(lambda: nc.tensor.transpose(PS2[0][:, ri * 64:(ri + 1) * 64],
                                                          bl[:, ri, kt * 8:(kt + 1) * 8, :].rearrange("p g c -> p (g c)"),
                                                          ident_f[0:64, 0:64]), bufs + [BC], [PB[0]])
                        A(lambda: nc.scalar.copy(bcomp[:, kt, :], PS2[0][:, 0:128]), [PB[0]], bufs)
                    for g8 in range(8):
                        f(lambda: nc.vector.tensor_scalar(Bmat[:, :, g8, :], bcomp[:], gmask[:, g8:g8 + 1], None, ALU.mult))
                    cl = sbt(p2, "cl", [128, 8, 2, NS], F32)
                    for cm, (top, bot) in ((Cm1, (c_re, c_im)), (Cm2, (c_im, c_re))):
                        S.dma(cl[:, :, 0, :], top[0, dr].rearrange("(k g) c n -> (g c) k n", g=8), writes=bufs)
                        S.dma(cl[:, :, 1, :], bot[0, dr].rearrange("(k g) c n -> (g c) k n", g=8), writes=bufs)
                        for kt in range(8):
                            T(lambda: nc.tensor.transpose(PS2[kt // 4][:, (kt % 4) * 128:(kt % 4 + 1) * 128],
                                                          cl[:, kt].rearrange("p r n -> p (r n)"), ident_f[:]),
                              bufs + [BC], [PB[0], PB[2]])
                        for hf in range(2):
                            A(lambda: nc.scalar.copy(cm[:, hf * 32:(hf + 1) * 32, :].rearrange("p g c -> p (g c)"),
                                                     PS2[hf][:, 0:512]), [PB[0], PB[2]], bufs)
                    f(lambda: nc.vector.tensor_scalar(Cm1[64:128], Cm1[64:128], -1.0, None, ALU.mult))
                    f(lambda: nc.vector.tensor_scalar(Cm2[:], Cm2[:], -1.0, None, ALU.mult))
                S.barrier()

            uTb = [sbt(ps, "uT%d" % i, [128, 8, NB], BF16) for i in range(2)]
            B_uTb = [Buf() for _ in range(2)]
            yacc = sbt(ps, "yacc", [128, 8, NB], F32)
            B_yacc = Buf("yacc")
            M1 = [sbt(ps, "M1_%d" % i, [128, 8, 128], BF16) for i in range(2)]
            M2 = [sbt(ps, "M2_%d" % i, [128, 8, 128], BF16) for i in range(2)]
            H1 = [sbt(ps, "H1_%d" % i, [128, 8, 128], BF16) for i in range(2)]
            H2 = [sbt(ps, "H2_%d" % i, [128, 8, 128], BF16) for i in range(2)]
            B_M = [Buf() for _ in range(2)]
            B_H = [Buf() for _ in range(2)]
            ccar = [sbt(ps, "ccar%d" % i, [128, NG], F32) for i in range(2)]
            xcar = sbt(ps, "xcar", [128, NG], F32)
            tcar = sbt(ps, "tcar", [128, NG], F32)
            ytok = sbt(ps, "ytok", [128, SSMW], F32)
            B_car, B_ytok = Buf("car"), Buf("ytok")
            mcnt = [0]

            def ssm_chunk(dr, uT, B_uT, c, first, summary=False):
                tok = slice(c * 128, (c + 1) * 128)
                tri, ntri = (tri_f, ntri_f) if dr == 0 else (tri_b, ntri_b)
                last = 127 if dr == 0 else 0
                cprev = ccar[0]
                for kt in range(8):
                    gs = slice(kt * 8, (kt + 1) * 8)
                    mi = mcnt[0] % 2
                    mcnt[0] += 1
                    pa = 0 if mi == 0 else 3
                    pa_b = [PB[2 * pa], PB[2 * pa + 1]]
                    for hf in range(2):
                        T(lambda: nc.tensor.matmul(PS2[pa][:, hf * 512:(hf + 1) * 512], uT[:, kt, tok],
                                                   Bmat[:, kt, hf * 4:(hf + 1) * 4, :].rearrange("p g x -> p (g x)"),
                                                   start=True, stop=True), [B_uT, BTAB], [pa_b[hf]])
                    pv = PS2[pa][:].rearrange("p (g r n) -> p g r n", g=8, r=2)
                    m1v = M1[mi][:].rearrange("p g (r n) -> p g r n", r=2)
                    m2v = M2[mi][:].rearrange("p g (r n) -> p g r n", r=2)
                    for r in range(2):
                        V(lambda: nc.vector.tensor_tensor(m1v[:, :, r, :], pv[:, :, r, :], T1re[:, gs, :], ALU.mult),
                          pa_b + [BTAB], [B_M[mi]])
                        V(lambda: nc.vector.tensor_tensor(m2v[:, :, r, :], pv[:, :, 1 - r, :], T1im[:, gs, :], ALU.mult),
                          pa_b + [BTAB], [B_M[mi]])
                    wb_ = [PB[2], PB[3]]
                    if summary:
                        for g8 in range(8):
                            g = kt * 8 + g8
                            o = PS2[1][:, g:g + 1]
                            T(lambda: nc.tensor.matmul(o, M1[mi][:, g8, :], tri_f[:, 127:128], start=True, stop=False,
                                                       skip_group_check=True), [B_M[mi], BC], [PB[2]])
                            T(lambda: nc.tensor.matmul(o[0:64, :], M2[mi][:, g8, 0:64], ntri_f[:, 127:128], start=False,
                                                       stop=False, skip_group_check=True), [B_M[mi], BC], [PB[2]])
                            T(lambda: nc.tensor.matmul(o[64:128, :], M2[mi][:, g8, 64:128], tri_f[:, 127:128], start=False,
                                                       stop=True, skip_group_check=True), [B_M[mi], BC], [PB[2]])
                        continue
                    for g8 in range(8):
                        o = PS2[1][:, g8 * 128:(g8 + 1) * 128]
                        ob = wb_[g8 // 4]
                        T(lambda: nc.tensor.matmul(o, M1[mi][:, g8, :], tri[:], start=True, stop=False), [B_M[mi], BC], [ob])
                        T(lambda: nc.tensor.matmul(o[0:64, :], M2[mi][:, g8, 0:64], ntri[:], start=False, stop=False,
                                                   skip_group_check=True), [B_M[mi], BC], [ob])
                        T(lambda: nc.tensor.matmul(o[64:128, :], M2[mi][:, g8, 64:128], tri[:], start=False, stop=True,
                                                   skip_group_check=True), [B_M[mi], BC], [ob])
                    wv = PS2[1][:].rearrange("p (g t) -> p g t", g=8)
                    if first:
                        V(lambda: nc.vector.tensor_copy(out=xcar[:, gs], in_=wv[:, :, last]), wb_, [B_car])
                    else:
                        V(lambda: nc.vector.tensor_tensor(xcar[:, gs], wv[:, :, last], cprev[:, gs], ALU.add),
                          wb_ + [B_car], [B_car])
                    for g8 in range(8):
                        g = kt * 8 + g8
                        wsl = PS2[1][:, g8 * 128:(g8 + 1) * 128]
                        if first:
                            V(lambda: nc.vector.tensor_tensor(H1[mi][:, g8, :], wsl, T2re[:, g, :], ALU.mult),
                              wb_ + [BTAB], [B_H[mi]])
                            V(lambda: nc.vector.tensor_tensor(H2[mi][:, g8, :], wsl, T2im[:, g, :], ALU.mult),
                              wb_ + [BTAB], [B_H[mi]])
                        else:
                            V(lambda: nc.vector.scalar_tensor_tensor(H1[mi][:, g8, :], wsl, cprev[:, g:g + 1], T2re[:, g, :],
                                                                     ALU.add, ALU.mult), wb_ + [BTAB, B_car], [B_H[mi]])
                            V(lambda: nc.vector.scalar_tensor_tensor(H2[mi][:, g8, :], wsl, cprev[:, g:g + 1], T2im[:, g, :],
                                                                     ALU.add, ALU.mult), wb_ + [BTAB, B_car], [B_H[mi]])
                    yb_ = [PB[4], PB[5]]
                    for g8 in range(8):
                        g = kt * 8 + g8
                        o = PS2[2][:, g * 16:(g + 1) * 16]
                        T(lambda: nc.tensor.matmul(o, H1[mi][:, g8, :], Cm1[:, g, :], start=True, stop=False,
                                                   skip_group_check=True), [B_H[mi], BTAB], [yb_[g // 32]])
                        T(lambda: nc.tensor.matmul(o, H2[mi][:, g8, :], Cm2[:, g, :], start=False, stop=True,
                                                   skip_group_check=True), [B_H[mi], BTAB], [yb_[g // 32]])
                if summary:
                    if first:
                        V(lambda: nc.vector.tensor_copy(out=xcar[:], in_=PS2[1][:, 0:NG]), [PB[2]], [B_car])
                    else:
                        V(lambda: nc.vector.tensor_tensor(xcar[:], PS2[1][:, 0:NG], cprev[:], ALU.add), [PB[2], B_car], [B_car])
                T(lambda: nc.tensor.matmul(PS2[1][:, 512:512 + NG], perm_f[:], xcar[:], start=True, stop=True), [B_car, BC], [PB[3]])
                V(lambda: nc.vector.tensor_tensor(tcar[:], G2[:], PS2[1][:, 512:512 + NG], ALU.mult), [PB[3], BTAB, B_car], [B_car])
                V(lambda: nc.vector.tensor_tensor(ccar[1][:], G1[:], xcar[:], ALU.mult), [B_car, BTAB], [B_car])
                V(lambda: nc.vector.tensor_tensor(ccar[0][:], ccar[1][:], tcar[:], ALU.add), [B_car], [B_car])
                if summary:
                    return
                A(lambda: nc.scalar.copy(ytok[:], PS2[2][:]), [PB[4], PB[5]], [B_ytok])
                for k in range(8):
                    T(lambda: nc.tensor.transpose(PS2[1][:, k * 128:(k + 1) * 128], ytok[:, k * 128:(k + 1) * 128], ident_f[:]),
                      [B_ytok, BC], [PB[2], PB[3]])
                A(lambda: nc.scalar.copy(yacc[:, :, tok], PS2[1][:].rearrange("p (k t) -> p k t", k=8)),
                  [PB[2], PB[3]], [B_yacc])

            gen_ssm_tables(1)
            if CTX:
                flg = sbt(ps, "flg", [128, 2], F32)
                S.dma(flg[:], flags_in, writes=[BC])
                for ib, b in enumerate(range(CTX // NB - 1, -1, -1)):
                    ui = ib % 2
                    S.dma(uTb[ui][:], uc_scr[:, b * NB:(b + 1) * NB].rearrange("(k p) t -> p k t", p=128), writes=[B_uTb[ui]])
                    for c in range(3, -1, -1):
                        ssm_chunk(1, uTb[ui], B_uTb[ui], c, first=(ib == 0 and c == 3), summary=True)
                V(lambda: nc.vector.tensor_scalar(ccar[0][:], ccar[0][:], flg[:, 1:2], None, ALU.mult), [B_car, BC], [B_car])
            for ib, b in enumerate(range(NBLK - 1, -1, -1)):
                t0 = b * NB
                ui = ib % 2
                S.dma(uTb[ui][:], u_scr[:, t0:t0 + NB].rearrange("(k p) t -> p k t", p=128), writes=[B_uTb[ui]])
                for c in range(3, -1, -1):
                    ssm_chunk(1, uTb[ui], B_uTb[ui], c, first=(ib == 0 and c == 3 and not CTX))
                S.dma(yb_scr[:, t0:t0 + NB].rearrange("(k p) t -> p k t", p=128), yacc[:], reads=[B_yacc])
            S.barrier()

            gen_ssm_tables(0)
            wglu = sbt(ps, "wglu", [128, 8, SSMW], BF16)
            B_wglu = Buf()
            S.dma(wglu[:], wb_glu.rearrange("(k p) c -> p k c", p=128), writes=[B_wglu])
            ybt = [sbt(ps, "ybt%d" % i, [128, NB], F32) for i in range(2)]
            B_ybt = [Buf() for _ in range(2)]
            gt = [sbt(ps, "gt%d" % i, [128, NB], F32) for i in range(2)]
            B_gt = [Buf() for _ in range(2)]
            zT = sbt(ps, "zT", [128, 8, NB], BF16)
            ysT = sbt(ps, "ysT", [128, 8, NB], BF16)
            sgl = [sbt(ps, "sgl%d" % i, [128, NB], BF16) for i in range(2)]
            B_sgl = [Buf() for _ in range(2)]
            B_z, B_ys = Buf("z"), Buf("ys")
            yc = [0]
            if CTX:
                for b in range(CTX // NB):
                    ui = b % 2
                    S.dma(uTb[ui][:], uc_scr[:, b * NB:(b + 1) * NB].rearrange("(k p) t -> p k t", p=128), writes=[B_uTb[ui]])
                    for c in range(4):
                        ssm_chunk(0, uTb[ui], B_uTb[ui], c, first=(b == 0 and c == 0), summary=True)
                V(lambda: nc.vector.tensor_scalar(ccar[0][:], ccar[0][:], flg[:, 0:1], None, ALU.mult), [B_car, BC], [B_car])
            for b in range(NBLK):
                t0 = b * NB
                ui = b % 2
                S.dma(uTb[ui][:], u_scr[:, t0:t0 + NB].rearrange("(k p) t -> p k t", p=128), writes=[B_uTb[ui]])
                for c in range(4):
                    ssm_chunk(0, uTb[ui], B_uTb[ui], c, first=(b == 0 and c == 0 and not CTX))
                for k in range(8):
                    i = yc[0] % 2
                    yc[0] += 1
                    S.dma(ybt[i][:], yb_scr[k * 128:(k + 1) * 128, t0:t0 + NB], writes=[B_ybt[i]])
                    yk = yacc[:, k, :]
                    V(lambda: nc.vector.scalar_tensor_tensor(yk, uTb[ui][:, k, :], dcol[:, k:k + 1], yk, ALU.mult, ALU.add),
                      [B_uTb[ui], BC, B_yacc], [B_yacc])
                    G(lambda: nc.gpsimd.tensor_tensor(yk, yk, ybt[i][:], ALU.add), [B_yacc, B_ybt[i]], [B_yacc])
                    if dbg:
                        S.dma(dbg_out["d_yb"][k * 128:(k + 1) * 128, t0:t0 + NB], ybt[i][:], reads=[B_ybt[i]])
                    G(lambda: nc.gpsimd.tensor_tensor(gt[0][:], yk, yk, ALU.mult), [B_yacc], [B_gt[0]])
                    G(lambda: nc.gpsimd.tensor_scalar(gt[0][:], gt[0][:], 0.044715, 1.0, ALU.mult, ALU.add), [B_gt[0]], [B_gt[0]])
                    G(lambda: nc.gpsimd.tensor_tensor(gt[0][:], gt[0][:], yk, ALU.mult), [B_gt[0], B_yacc], [B_gt[0]])
                    A(lambda: nc.scalar.activation(gt[1][:], gt[0][:], AF.Sigmoid, scale=GELU_C), [B_gt[0]], [B_gt[1]])
                    G(lambda: nc.gpsimd.tensor_tensor(zT[:, k, :], yk, gt[1][:], ALU.mult), [B_gt[1], B_yacc], [B_z])
                for mt in range(8):
                    bi = ps_rr()
                    for k in range(8):
                        T(lambda: nc.tensor.matmul(bank(bi), wglu[:, k, mt * 128:(mt + 1) * 128], zT[:, k, :],
                                                   start=(k == 0), stop=(k == 7)), [B_wglu, B_z], [PB[bi]])
                    i = mt % 2
                    A(lambda: nc.scalar.activation(sgl[i][:], bank(bi), AF.Sigmoid, bias=bgcol[:, mt:mt + 1]),
                      [PB[bi], BC], [B_sgl[i]])
                    V(lambda: nc.vector.tensor_tensor(ysT[:, mt, :], zT[:, mt, :], sgl[i][:], ALU.mult),
                      [B_z, B_sgl[i]], [B_ys])
                S.dma(ys_scr[:, t0:t0 + NB].rearrange("(k p) t -> p k t", p=128), ysT[:], reads=[B_ys])
                if dbg:
                    for k in range(8):
                        G(lambda: nc.gpsimd.tensor_copy(out=gt[0][:], in_=ysT[:, k, :]), [B_ys], [B_gt[0]])
                        S.dma(dbg_out["d_ys"][k * 128:(k + 1) * 128, t0:t0 + NB], gt[0][:], reads=[B_gt[0]])
            S.barrier()

        with ExitStack() as ps:
            wload = make_wring(ps)
            rmsnorm_to_T, stats, rstd, B_st = make_norm(ps)
            gfin = sbt(ps, "gfin", [128, D], F32)
            S.dma(gfin[:], final_g.partition_broadcast(128), writes=[BC])
            NQs = (128, 128, 32)
            bt_hi = sbt(ps, "bt_hi", [128, 24 * 128], BF16)
            bt_lo = sbt(ps, "bt_lo", [128, 24 * 128], BF16)
            BT = {}
            with ExitStack() as p2:
                bst = sbt(p2, "bst", [128, 24 * 128], F32)
                V(lambda: nc.vector.memset(bst[:], 0.0), writes=[BBT])
                for gi in range(3):
                    nq = NQs[gi]
                    for j in range(4):
                        for k2 in range(2):
                            col = ((gi * 4 + j) * 2 + k2) * 128
                            src = bass.AP(bias_rep_h, (gi * 4 + j) * 128 * 384 + 255 - 128 * k2, [[383, 128], [1, nq]])
                            S.dma(bst[:, col:col + nq], src, reads=[BBT], writes=[BBT])
                            BT[(gi, j, k2)] = (bt_hi[:, col:col + nq], bt_lo[:, col:col + nq])
                V(lambda: nc.vector.tensor_copy(out=bt_hi[:], in_=bst[:]), [BBT], [BBT])
                V(lambda: nc.vector.tensor_tensor(bst[:], bst[:], bt_hi[:], ALU.subtract), [BBT], [BBT])
                V(lambda: nc.vector.tensor_copy(out=bt_lo[:], in_=bst[:]), [BBT], [BBT])
                S.barrier()
            xb = sbt(ps, "xb", [128, 4, D], F32)
            xnT = sbt(ps, "xnT", [128, KT, NB], BF16)
            ysT = sbt(ps, "ysT2", [128, 8, NB], BF16)
            qT = sbt(ps, "qT", [128, NH, NB], BF16)
            mixT = sbt(ps, "mixT", [128, KT, NB], BF16)
            yaT = sbt(ps, "yaT", [128, 4, NB], BF16)
            sg = [sbt(ps, "sg%d" % i, [128, NB], BF16) for i in range(2)]
            sgt = [sbt(ps, "sgt%d" % i, [128, NB], BF16) for i in range(2)]
            B_x, B_xnT, B_ys, B_q, B_mixT, B_ya = (Buf(n) for n in "x xnT ys q mixT ya".split())
            B_sg = [Buf() for _ in range(2)]
            B_sgt = [Buf() for _ in range(2)]
            sgc = [0]
            kwin = [sbt(ps, "kwin%d" % i, [128, NB + 2 * HALO], BF16) for i in range(2)]
            B_kwin = [Buf() for _ in range(2)]
            vwin = [sbt(ps, "vwin%d" % i, [128, 2, 256], BF16) for i in range(3)]
            B_vwin = [Buf() for _ in range(3)]
            kmc = [sbt(ps, "kmc%d" % i, [128, 2], F32) for i in range(3)]
            PT = [sbt(ps, "PT%d" % i, [128, 2, 128], BF16) for i in range(3)]
            B_PT = [Buf() for _ in range(3)]
            rden = sbt(ps, "rden", [128, NB], F32)
            B_rden = Buf()
            actq = sbt(ps, "actq", [128, 8, NB], BF16)
            relu_t = [sbt(ps, "relu%d" % i, [128, NB], BF16) for i in range(2)]
            B_relu = [Buf() for _ in range(2)]
            B_actq = Buf()
            kwc, vwc, ptc = [0], [0], [0]

            for b in range(NBLK):
                t0 = b * NB
                S.dma(xb[:], xs[t0:t0 + NB, :].rearrange("(t p) d -> p t d", p=128), writes=[B_x])
                S.dma(ysT[:], ys_scr[:, t0:t0 + NB].rearrange("(k p) t -> p k t", p=128), writes=[B_ys])
                rmsnorm_to_T(xb, xnT, B_x, B_xnT)
                for m0 in range(0, KT, 4):
                    wtg, wbg = wload(wb_in, KT, G_OFF + m0 * 128, 512)
                    wtb, wbb = wload(wb_brs, 8, m0 * 128, 512)
                    for mi in range(4):
                        mt = m0 + mi
                        bi = ps_rr()
                        for k in range(KT):
                            T(lambda: nc.tensor.matmul(bank(bi), wtg[:, k, mi * 128:(mi + 1) * 128], xnT[:, k, :],
                                                       start=(k == 0), stop=(k == KT - 1)), [wbg, B_xnT], [PB[bi]])
                        i = sgc[0] % 2
                        sgc[0] += 1
                        A(lambda: nc.scalar.activation(sg[i][:], bank(bi), AF.Sigmoid), [PB[bi]], [B_sg[i]])
                        bj = ps_rr()
                        for k in range(8):
                            T(lambda: nc.tensor.matmul(bank(bj), wtb[:, k, mi * 128:(mi + 1) * 128], ysT[:, k, :],
                                                       start=(k == 0), stop=(k == 7)), [wbb, B_ys], [PB[bj]])
                        V(lambda: nc.vector.tensor_tensor(mixT[:, mt, :], bank(bj), sg[i][:], ALU.mult),
                          [PB[bj], B_sg[i]], [B_mixT])

                def ev_q(mt, psap, psb):
                    A(lambda: nc.scalar.activation(qT[:, mt, :], psap, AF.Copy, scale=HD ** -0.5), [psb], [B_q])
                proj_ws(wload, wb_in, KT, Q_OFF, NH, xnT, B_xnT, ev_q)

                for rnd in range(2):
                    first_mm = {0: True, 1: True}
                    for gi in range(3):
                        d = DIL[gi]
                        reach = 64 * d
                        nq = NQs[gi]
                        nk2 = (128, nq)
                        nunits = 4 if gi == 0 else d
                        kw = {}
                        for jj in range(2):
                            h = gi * 4 + rnd * 2 + jj
                            i = kwc[0] % 2
                            kwc[0] += 1
                            S.dma(kwin[i][:, 0:NB + 2 * reach], kt_scr[h, :, HALO + t0 - reach:HALO + t0 + NB + reach],
                                  writes=[B_kwin[i]])
                            kw[jj] = i
                        for u in range(nunits):
                            if gi == 0:
                                qsl = slice(u * 128, (u + 1) * 128)
                                kcol0, kstep = u * 128, 1
                                tstart = t0 + u * 128 - 64
                            else:
                                qsl = slice(u, NB, d)
                                kcol0, kstep = u, d
                                tstart = t0 + u - reach
                            vi = vwc[0] % 3
                            vwc[0] += 1
                            c0 = gi * 512 + rnd * 256
                            for k2 in range(2):
                                n2 = nk2[k2]
                                r0 = HALO + tstart + k2 * 128 * kstep
                                src = bass.AP(v_scr_h, r0 * (NH * HD) + c0, [[kstep * NH * HD, n2], [1, 256]])
                                S.dma(vwin[vi][0:n2, k2, :], src, writes=[B_vwin[vi]])
                                srcm = bass.AP(kmask_h, r0, [[kstep, n2], [1, 1]])
                                S.dma(kmc[vi][0:n2, k2:k2 + 1], srcm, writes=[B_vwin[vi]])
                            for jj in range(2):
                                j = rnd * 2 + jj
                                h = gi * 4 + j
                                ki = kw[jj]
                                pi_ = ptc[0] % 3
                                ptc[0] += 1
                                sb_i = ps_rr()
                                for k2 in range(2):
                                    n2 = nk2[k2]
                                    kc0 = kcol0 + k2 * 128 * kstep
                                    ksl = slice(kc0, kc0 + (n2 - 1) * kstep + 1, kstep)
                                    so = bank(sb_i)[0:n2, k2 * 128:k2 * 128 + nq]
                                    bhi, blo = BT[(gi, j, k2)]
                                    T(lambda: nc.tensor.matmul(so, kwin[ki][:, ksl], qT[:, h, qsl], start=True, stop=False,
                                                               skip_group_check=True), [B_kwin[ki], B_q], [PB[sb_i]])
                                    T(lambda: nc.tensor.matmul(so, ident_b[0:n2, 0:n2], bhi[0:n2, :], start=False, stop=False,
                                                               skip_group_check=True), [BBT, BC], [PB[sb_i]])
                                    T(lambda: nc.tensor.matmul(so, ident_b[0:n2, 0:n2], blo[0:n2, :], start=False, stop=True,
                                                               skip_group_check=True), [BBT, BC], [PB[sb_i]])
                                    A(lambda: nc.scalar.activation(PT[pi_][0:n2, k2, 0:nq], so, AF.Exp,
                                                                   bias=kmc[vi][0:n2, k2:k2 + 1]),
                                      [PB[sb_i], B_vwin[vi]], [B_PT[pi_]])
                                ob, db = 4 + jj, 6 + jj
                                for k2 in range(2):
                                    n2 = nk2[k2]
                                    T(lambda: nc.tensor.matmul(bank(ob)[:, qsl], vwin[vi][0:n2, k2, jj * 128:(jj + 1) * 128],
                                                               PT[pi_][0:n2, k2, 0:nq], start=first_mm[jj], stop=False,
                                                               skip_group_check=True),
                                      [B_vwin[vi], B_PT[pi_]], [PB[ob]])
                                    T(lambda: nc.tensor.matmul(bank(db)[:, qsl], ones_b[0:n2, :], PT[pi_][0:n2, k2, 0:nq],
                                                               start=first_mm[jj], stop=False, skip_group_check=True),
                                      [BC, B_PT[pi_]], [PB[db]])
                                    first_mm[jj] = False
                    for jj in range(2):
                        j = rnd * 2 + jj
                        V(lambda: nc.vector.tensor_scalar(rden[:], bank(6 + jj), 1e-30, None, ALU.add), [PB[6 + jj]], [B_rden])
                        V(lambda: nc.vector.reciprocal(rden[:], rden[:]), [B_rden], [B_rden])
                        V(lambda: nc.vector.tensor_tensor(yaT[:, j, :], bank(4 + jj), rden[:], ALU.mult),
                          [PB[4 + jj], B_rden], [B_ya])
                if dbg:
                    for j in range(4):
                        V(lambda: nc.vector.tensor_copy(out=rden[:], in_=yaT[:, j, :]), [B_ya], [B_rden])
                        S.dma(dbg_out["d_ya"][j * 128:(j + 1) * 128, t0:t0 + NB], rden[:], reads=[B_rden])

                for m0 in range(0, KT, 4):
                    wtg, wbg = wload(wb_in, KT, G_OFF + D + m0 * 128, 512)
                    wtb, wbb = wload(wb_bra, 4, m0 * 128, 512)
                    for mi in range(4):
                        mt = m0 + mi
                        bi = ps_rr()
                        for k in range(KT):
                            T(lambda: nc.tensor.matmul(bank(bi), wtg[:, k, mi * 128:(mi + 1) * 128], xnT[:, k, :],
                                                       start=(k == 0), stop=(k == KT - 1)), [wbg, B_xnT], [PB[bi]])
                        i = sgc[0] % 2
                        sgc[0] += 1
                        A(lambda: nc.scalar.activation(sg[i][:], bank(bi), AF.Sigmoid), [PB[bi]], [B_sg[i]])
                        bj = ps_rr()
                        for k in range(4):
                            T(lambda: nc.tensor.matmul(bank(bj), wtb[:, k, mi * 128:(mi + 1) * 128], yaT[:, k, :],
                                                       start=(k == 0), stop=(k == 3)), [wbb, B_ya], [PB[bj]])
                        V(lambda: nc.vector.tensor_tensor(sgt[i][:], bank(bj), sg[i][:], ALU.mult),
                          [PB[bj], B_sg[i]], [B_sgt[i]])
                        G(lambda: nc.gpsimd.tensor_tensor(mixT[:, mt, :], sgt[i][:], mixT[:, mt, :], ALU.add),
                          [B_sgt[i], B_mixT], [B_mixT])

                for cg in range(4):
                    wt, wb = wload(wb_out, KT, cg * 512, 512)
                    for tt in range(4):
                        bi = ps_rr()
                        for k in range(KT):
                            T(lambda: nc.tensor.matmul(bank(bi), mixT[:, k, tt * 128:(tt + 1) * 128], wt[:, k, :],
                                                       start=(k == 0), stop=(k == KT - 1)), [wb, B_mixT], [PB[bi]])
                        xsl = xb[:, tt, cg * 512:(cg + 1) * 512]
                        V(lambda: nc.vector.tensor_tensor(xsl, bank(bi), xsl, ALU.add), [PB[bi], B_x], [B_x])
                rmsnorm_to_T(xb, xnT, B_x, B_xnT)
                for qf in range(8):
                    def ev_ff1(mt, psap, psb):
                        i = sgc[0] % 2
                        sgc[0] += 1
                        A(lambda: nc.scalar.activation(relu_t[i][:], psap, AF.Relu), [psb], [B_relu[i]])
                        V(lambda: nc.vector.scalar_tensor_tensor(actq[:, mt, :], psap, 0.0, relu_t[i][:], ALU.max, ALU.mult),
                          [psb, B_relu[i]], [B_actq])
                    proj_ws(wload, wb_ff1, KT, qf * 1024, 8, xnT, B_xnT, ev_ff1)
                    for cg in range(4):
                        wt, wb = wload(wb_ff2[qf * 1024:(qf + 1) * 1024, :], 8, cg * 512, 512)
                        for tt in range(4):
                            bi = ps_rr()
                            for k in range(8):
                                T(lambda: nc.tensor.matmul(bank(bi), actq[:, k, tt * 128:(tt + 1) * 128], wt[:, k, :],
                                                           start=(k == 0), stop=(k == 7)), [wb, B_actq], [PB[bi]])
                            xsl = xb[:, tt, cg * 512:(cg + 1) * 512]
                            V(lambda: nc.vector.tensor_tensor(xsl, bank(bi), xsl, ALU.add), [PB[bi], B_x], [B_x])
                stats(xb, B_x)
                for tt in range(4):
                    V(lambda: nc.vector.scalar_tensor_tensor(xb[:, tt, :], xb[:, tt, :], rstd[:, tt:tt + 1], gfin[:],
                                                             ALU.mult, ALU.mult), [B_x, B_st, BC], [B_x])
                S.dma(ys[t0:t0 + NB, :].rearrange("(t p) d -> p t d", p=128), xb[:], reads=[B_x])
            S.barrier()
    return nc


_PROG = {}


def _get_prog(L, dbg=False, CTX=0):
    key = (L, dbg, CTX)
    if key not in _PROG:
        _PROG[key] = build_program(L, dbg, CTX)
    return _PROG[key]


def _kmask(L, nvalid):
    m = np.full((L + 2 * HALO, 1), NEG, np.float32)
    m[HALO:HALO + nvalid] = 0.0
    return m


def kernel(**inputs):
    f32 = lambda a: np.ascontiguousarray(np.asarray(a, dtype=np.float32))
    xp = f32(inputs["x_prompt"])
    xsm = f32(inputs["x_sample"])
    B, SL, _ = xp.shape
    LS = xsm.shape[1]
    assert xsm.shape[0] == 1 and LS == 2 * SL and B + 2 <= 8
    L = SL
    shared = {}
    for k in ("norm1_g", "w_in", "ssm_a_re", "ssm_a_im", "ssm_log_dt", "ssm_b_re", "ssm_b_im", "ssm_c_re",
              "ssm_c_im", "ssm_d", "w_glu", "b_glu", "w_br_ssm", "w_br_attn", "w_out", "norm2_g", "w_ff1",
              "w_ff2", "rel_bias", "final_g"):
        shared[k] = f32(inputs[k])
    shared["onehot"] = _t5_onehot()
    zeros_ctx = np.zeros((L, D), np.float32)

    def km(left, right):
        m = np.full((L + 2 * HALO, 1), NEG, np.float32)
        m[HALO:HALO + L] = 0.0
        if left:
            m[:HALO] = 0.0
        if right:
            m[HALO + L:] = 0.0
        return m

    def fl(a, b):
        f = np.zeros((128, 2), np.float32)
        f[:, 0] = a
        f[:, 1] = b
        return f
    in_maps = []
    for c in range(8):
        m = dict(shared)
        if c < B:
            m["xs"], m["xc"], m["kmask"], m["flags"] = xp[c], zeros_ctx, km(False, False), fl(0.0, 0.0)
        elif c == B:
            m["xs"], m["xc"], m["kmask"], m["flags"] = xsm[0, :L], xsm[0, L:], km(False, True), fl(0.0, 1.0)
        elif c == B + 1:
            m["xs"], m["xc"], m["kmask"], m["flags"] = xsm[0, L:], xsm[0, :L], km(True, False), fl(1.0, 0.0)
        else:
            m["xs"], m["xc"], m["kmask"], m["flags"] = zeros_ctx, zeros_ctx, km(False, False), fl(0.0, 0.0)
        in_maps.append(m)
    nc = _get_prog(L, CTX=L)
    res = run_bass_kernel_spmd(nc, in_maps, core_ids=list(range(8)))
    y_prompt = np.stack([np.asarray(res.results[c]["ys"], dtype=np.float32) for c in range(B)], axis=0)
    y_sample = np.concatenate([np.asarray(res.results[B]["ys"], dtype=np.float32),
                               np.asarray(res.results[B + 1]["ys"], dtype=np.float32)], axis=0)[None]
    return (y_prompt, y_sample)
```

```python
import math
from contextlib import ExitStack

import numpy as np
import concourse.bass as bass
import concourse.mybir as mybir
from concourse.bass_utils import run_bass_kernel_spmd

F32 = mybir.dt.float32
BF16 = mybir.dt.bfloat16
AF = mybir.ActivationFunctionType
ALU = mybir.AluOpType

D = 2048
KT = 16
SSMW = 1024
NG = 64
NS = 64
NH = 12
HD = 128
INC = 9728
Q_OFF, K_OFF, V_OFF, G_OFF = 1024, 2560, 4096, 5632
DFF = 8192
NB = 512
HALO = 1024
NEG = -30000.0
EPS = 1e-6
DIL = (1, 4, 16)
GELU_C = 2.0 * math.sqrt(2.0 / math.pi)


class Buf:
    __slots__ = ("w", "r", "name")

    def __init__(self, name=""):
        self.w = {}
        self.r = {}
        self.name = name


class Eng:
    def __init__(self, eng, sem, is_pe=False):
        self.eng = eng
        self.sem = sem
        self.cnt = 0
        self.known = {}
        self.is_pe = is_pe


class Sync:
    NR = 16

    def __init__(self, nc, es):
        self.nc = nc
        self.sems = {}

        def mk(name):
            s = es.enter_context(nc.semaphore(name))
            self.sems[id(s)] = s
            return s

        self.pe = Eng(nc.tensor, mk("s_pe"), is_pe=True)
        self.act = Eng(nc.scalar, mk("s_act"))
        self.dve = Eng(nc.vector, mk("s_dve"))
        self.pool = Eng(nc.gpsimd, mk("s_pool"))
        self.sp = Eng(nc.sync, mk("s_sp"))
        self.ring = [mk("s_dma%d" % i) for i in range(self.NR)]
        self.ring_cnt = [0] * self.NR
        self.n_dma = 0

    def _waits(self, E, reads, writes):
        need = {}
        for b in reads:
            for k, v in b.w.items():
                if need.get(k, 0) < v:
                    need[k] = v
        for b in writes:
            for k, v in b.w.items():
                if need.get(k, 0) < v:
                    need[k] = v
            for k, v in b.r.items():
                if need.get(k, 0) < v:
                    need[k] = v
        for k, v in need.items():
            if E.is_pe and k == id(E.sem):
                continue
            if E.known.get(k, 0) >= v:
                continue
            E.eng.wait_ge(self.sems[k], v)
            E.known[k] = v

    def op(self, E, fn, reads=(), writes=()):
        self._waits(E, reads, writes)
        inst = fn()
        E.cnt += 1
        inst.then_inc(E.sem, 1)
        k = id(E.sem)
        for b in writes:
            b.w = {k: E.cnt}
            b.r = {}
        for b in reads:
            if b.r.get(k, 0) < E.cnt:
                b.r[k] = E.cnt

    def dma(self, out, in_, reads=(), writes=(), q=None, **kw):
        E = q if q is not None else self.sp
        i = self.n_dma % self.NR
        self.n_dma += 1
        sem = self.ring[i]
        k = id(sem)
        prev = self.ring_cnt[i]
        if prev > 0 and E.known.get(k, 0) < prev:
            E.eng.wait_ge(sem, prev)
            E.known[k] = prev
        self._waits(E, reads, writes)
        E.eng.dma_start(out=out, in_=in_, **kw).then_inc(sem, 16)
        self.ring_cnt[i] = prev + 16
        for b in writes:
            b.w = {k: prev + 16}
            b.r = {}
        for b in reads:
            if b.r.get(k, 0) < prev + 16:
                b.r[k] = prev + 16

    def barrier(self):
        engs = (self.pe, self.act, self.dve, self.pool, self.sp)
        for E in engs:
            for X in engs:
                if X is E or X.cnt == 0:
                    continue
                k = id(X.sem)
                if E.known.get(k, 0) < X.cnt:
                    E.eng.wait_ge(X.sem, X.cnt)
                    E.known[k] = X.cnt
            for i in range(self.NR):
                c = self.ring_cnt[i]
                k = id(self.ring[i])
                if c > 0 and E.known.get(k, 0) < c:
                    E.eng.wait_ge(self.ring[i], c)
                    E.known[k] = c


def _t5_onehot():
    oh = np.zeros((3, 33, 384), np.float32)
    for gi, d in enumerate(DIL):
        for m in range(384):
            rel = 191 - m
            if abs(rel) > 64:
                oh[gi, 32, m] = 1.0
                continue
            r = rel * d
            ret = 16 if r > 0 else 0
            n = abs(r)
            nf = np.float32(max(n, 1))
            large = 8 + int(np.float32(np.log(nf / np.float32(8.0)) / np.float32(math.log(128.0)) * np.float32(8.0)))
            large = min(large, 15)
            b = ret + (n if n < 8 else large)
            oh[gi, b, m] = 1.0
    return oh


def build_program(L, dbg=False, CTX=0):
    assert L % NB == 0 and CTX % NB == 0 and (CTX == 0 or CTX >= 2 * HALO)
    NBLK = L // NB
    LP = L + 2 * HALO
    nc = bass.Bass("TRN2", target_bir_lowering=False)

    def din(name, shape, dt=F32):
        return nc.dram_tensor(name, list(shape), dt, kind="ExternalInput")

    def dscr_early(name, shape, dt):
        return nc.dram_tensor(name, list(shape), dt, kind="Internal")

    xs = din("xs", [L, D]).ap()
    kmask_h = din("kmask", [LP, 1])
    if CTX:
        xc = din("xc", [CTX, D]).ap()
        flags_in = din("flags", [128, 2]).ap()
        uc_scr = dscr_early("uc_scr", [SSMW, CTX], BF16).ap()
    norm1_g = din("norm1_g", [1, D]).ap()
    w_in = din("w_in", [1, D, INC]).ap()
    a_re = din("ssm_a_re", [1, 2, NG, NS]).ap()
    a_im = din("ssm_a_im", [1, 2, NG, NS]).ap()
    log_dt = din("ssm_log_dt", [1, 2, NG]).ap()
    b_re = din("ssm_b_re", [1, 2, NG, NS, 16]).ap()
    b_im = din("ssm_b_im", [1, 2, NG, NS, 16]).ap()
    c_re = din("ssm_c_re", [1, 2, NG, 16, NS]).ap()
    c_im = din("ssm_c_im", [1, 2, NG, 16, NS]).ap()
    ssm_d = din("ssm_d", [1, SSMW]).ap()
    w_glu = din("w_glu", [1, SSMW, SSMW]).ap()
    b_glu = din("b_glu", [1, SSMW]).ap()
    w_brs = din("w_br_ssm", [1, SSMW, D]).ap()
    w_bra = din("w_br_attn", [1, 512, D]).ap()
    w_out = din("w_out", [1, D, D]).ap()
    norm2_g = din("norm2_g", [1, D]).ap()
    w_ff1 = din("w_ff1", [1, D, DFF]).ap()
    w_ff2 = din("w_ff2", [1, DFF, D]).ap()
    rel_bias = din("rel_bias", [32, NH]).ap()
    final_g = din("final_g", [D]).ap()
    onehot = din("onehot", [3, 33, 384]).ap()
    ys = nc.dram_tensor("ys", [L, D], F32, kind="ExternalOutput").ap()
    dbg_out = {}
    if dbg:
        for nm, shp in (("d_yb", [SSMW, L]), ("d_ys", [SSMW, L]), ("d_ya", [512, L]), ("d_bias", [NH, 384])):
            dbg_out[nm] = nc.dram_tensor(nm, shp, F32, kind="ExternalOutput").ap()

    def dscr(name, shape, dt):
        return nc.dram_tensor(name, list(shape), dt, kind="Internal")

    wb_in = dscr("wb_in", [D, INC], BF16).ap()
    wb_glu = dscr("wb_glu", [SSMW, SSMW], BF16).ap()
    wb_brs = dscr("wb_brs", [SSMW, D], BF16).ap()
    wb_bra = dscr("wb_bra", [512, D], BF16).ap()
    wb_out = dscr("wb_out", [D, D], BF16).ap()
    wb_ff1 = dscr("wb_ff1", [D, DFF], BF16).ap()
    wb_ff2 = dscr("wb_ff2", [DFF, D], BF16).ap()
    kt_scr = dscr("kt_scr", [NH, HD, LP], BF16).ap()
    v_scr_h = dscr("v_scr", [LP, NH * HD], BF16)
    v_scr = v_scr_h.ap()
    u_scr = dscr("u_scr", [SSMW, L], BF16).ap()
    yb_scr = dscr("yb_scr", [SSMW, L], F32).ap()
    ys_scr = dscr("ys_scr", [SSMW, L], BF16).ap()
    bias_scr_h = dscr("bias_scr", [NH, 384], F32)
    bias_scr = bias_scr_h.ap()
    bias_rep_h = dscr("bias_rep", [NH, 128, 384], F32)
    bias_rep = bias_rep_h.ap()

    es = ExitStack()
    with es:
        S = Sync(nc, es)
        PE, ACT, DVE, POOL = S.pe, S.act, S.dve, S.pool
        es.enter_context(nc.Block())
        es.enter_context(nc.allow_non_contiguous_dma(reason="small parameter re-layouts"))

        uniq = [0]

        def sbt(stack, name, shape, dt=F32):
            uniq[0] += 1
            return stack.enter_context(nc.sbuf_tensor("%s_%d" % (name, uniq[0]), list(shape), dt))

        def V(fn, reads=(), writes=()):
            S.op(DVE, fn, reads, writes)

        def A(fn, reads=(), writes=()):
            S.op(ACT, fn, reads, writes)

        def G(fn, reads=(), writes=()):
            S.op(POOL, fn, reads, writes)

        def T(fn, reads=(), writes=()):
            S.op(PE, fn, reads, writes)

        def cp(i, out, in_, reads, writes):
            if i % 2 == 0:
                A(lambda: nc.scalar.copy(out, in_), reads, writes)
            else:
                V(lambda: nc.vector.tensor_copy(out=out, in_=in_), reads, writes)

        PS2 = [es.enter_context(nc.psum_tensor("ps2_%d" % i, [128, 1024], F32)) for i in range(4)]
        PB = [Buf("psb%d" % i) for i in range(8)]

        def bank(i):
            return PS2[i // 2][:, (i % 2) * 512:(i % 2) * 512 + 512]

        rr = [0]

        def ps_rr():
            i = rr[0] % 4
            rr[0] += 1
            return i

        ident_f = sbt(es, "ident_f", [128, 128], F32)
        ident_b = sbt(es, "ident_b", [128, 128], BF16)
        tri_f = sbt(es, "tri_f", [128, 128], BF16)
        tri_b = sbt(es, "tri_b", [128, 128], BF16)
        ntri_f = sbt(es, "ntri_f", [128, 128], BF16)
        ntri_b = sbt(es, "ntri_b", [128, 128], BF16)
        perm_f = sbt(es, "perm_f", [128, 128], F32)
        ones_b = sbt(es, "ones_b", [128, 128], BF16)
        sgn = sbt(es, "sgn", [128, 1], F32)
        gmask = sbt(es, "gmask", [128, 8], F32)
        iot = sbt(es, "iot", [128, 128], F32)
        gm_t = sbt(es, "gm_t", [128, 8], F32)
        BC = Buf("const")

        G(lambda: nc.gpsimd.iota(iot[:], [[1, 128]], base=0, channel_multiplier=-1,
                                 allow_small_or_imprecise_dtypes=True), writes=[BC])
        G(lambda: nc.gpsimd.iota(gm_t[:], [[16, 8]], base=0, channel_multiplier=-1,
                                 allow_small_or_imprecise_dtypes=True), writes=[BC])
        V(lambda: nc.vector.tensor_scalar(ident_f[:], iot[:], 0.0, None, ALU.is_equal), [BC], [BC])
        V(lambda: nc.vector.tensor_scalar(tri_f[:], iot[:], 0.0, None, ALU.is_ge), [BC], [BC])
        V(lambda: nc.vector.tensor_scalar(tri_b[:], iot[:], 0.0, None, ALU.is_le), [BC], [BC])
        V(lambda: nc.vector.tensor_scalar(ntri_f[:], tri_f[:], -1.0, None, ALU.mult), [BC], [BC])
        V(lambda: nc.vector.tensor_scalar(ntri_b[:], tri_b[:], -1.0, None, ALU.mult), [BC], [BC])
        V(lambda: nc.vector.tensor_scalar(perm_f[:], iot[:], 64.0, None, ALU.is_equal), [BC], [BC])
        V(lambda: nc.vector.tensor_scalar(ident_b[:], iot[:], -64.0, None, ALU.is_equal), [BC], [BC])
        V(lambda: nc.vector.tensor_tensor(perm_f[:], perm_f[:], ident_b[:], ALU.add), [BC], [BC])
        V(lambda: nc.vector.tensor_copy(out=ident_b[:], in_=ident_f[:]), [BC], [BC])
        V(lambda: nc.vector.memset(ones_b[:], 1.0), [BC], [BC])
        V(lambda: nc.vector.tensor_scalar(sgn[:], iot[:, 0:1], -63.5, None, ALU.is_le), [BC], [BC])
        V(lambda: nc.vector.tensor_scalar(sgn[:], sgn[:], 2.0, -1.0, ALU.mult, ALU.add), [BC], [BC])
        V(lambda: nc.vector.tensor_scalar(gmask[:], gm_t[:], 0.5, None, ALU.is_le), [BC], [BC])
        V(lambda: nc.vector.tensor_scalar(gm_t[:], gm_t[:], -15.5, None, ALU.is_ge), [BC], [BC])
        V(lambda: nc.vector.tensor_tensor(gmask[:], gmask[:], gm_t[:], ALU.mult), [BC], [BC])

        g1col = sbt(es, "g1col", [128, KT], F32)
        g2col = sbt(es, "g2col", [128, KT], F32)
        dcol = sbt(es, "dcol", [128, 8], F32)
        bgcol = sbt(es, "bgcol", [128, 8], F32)
        S.dma(g1col[:], norm1_g[0].rearrange("(k p) -> p k", p=128), writes=[BC])
        S.dma(g2col[:], norm2_g[0].rearrange("(k p) -> p k", p=128), writes=[BC])
        S.dma(dcol[:], ssm_d[0].rearrange("(k p) -> p k", p=128), writes=[BC])
        S.dma(bgcol[:], b_glu[0].rearrange("(k p) -> p k", p=128), writes=[BC])

        with ExitStack() as ps:
            NST = 3
            zero_b = sbt(ps, "zero_b", [128, 1536], BF16)
            V(lambda: nc.vector.memset(zero_b[:], 0.0), [BC], [BC])
            stg_f = [sbt(ps, "stgf%d" % i, [128, 2048], F32) for i in range(NST)]
            stg_b = [sbt(ps, "stgb%d" % i, [128, 2048], BF16) for i in range(NST)]
            bf_ = [Buf() for _ in range(NST)]
            bb_ = [Buf() for _ in range(NST)]
            cnt = [0]

            def conv(src, dst, rows, cols, scol=None):
                for r0 in range(0, rows, 128):
                    for c0 in range(0, cols, 2048):
                        cw = min(2048, cols - c0)
                        i = cnt[0] % NST
                        e = cnt[0] % 3
                        cnt[0] += 1
                        S.dma(stg_f[i][:, 0:cw], src[r0:r0 + 128, c0:c0 + cw], writes=[bf_[i]])
                        o, a = stg_b[i][:, 0:cw], stg_f[i][:, 0:cw]
                        if scol is not None:
                            sc = scol[:, r0 // 128:r0 // 128 + 1]
                            if e == 0:
                                V(lambda: nc.vector.tensor_scalar(o, a, sc, None, ALU.mult), [bf_[i], BC], [bb_[i]])
                            elif e == 1:
                                A(lambda: nc.scalar.activation(o, a, AF.Copy, scale=sc), [bf_[i], BC], [bb_[i]])
                            else:
                                G(lambda: nc.gpsimd.tensor_scalar(o, a, sc, None, ALU.mult), [bf_[i], BC], [bb_[i]])
                        else:
                            if e == 0:
                                V(lambda: nc.vector.tensor_copy(out=o, in_=a), [bf_[i]], [bb_[i]])
                            elif e == 1:
                                A(lambda: nc.scalar.copy(o, a), [bf_[i]], [bb_[i]])
                            else:
                                G(lambda: nc.gpsimd.tensor_copy(out=o, in_=a), [bf_[i]], [bb_[i]])
                        S.dma(dst[r0:r0 + 128, c0:c0 + cw], o, reads=[bb_[i]])

            conv(w_in[0], wb_in, D, INC, g1col)
            conv(w_glu[0], wb_glu, SSMW, SSMW)
            conv(w_brs[0], wb_brs, SSMW, D)
            conv(w_bra[0], wb_bra, 512, D)
            conv(w_out[0], wb_out, D, D)
            conv(w_ff1[0], wb_ff1, D, DFF, g2col)
            conv(w_ff2[0], wb_ff2, DFF, D)
            if not CTX:
                for h in range(NH):
                    S.dma(kt_scr[h, :, 0:HALO], zero_b[:, 0:HALO], reads=[BC])
                    S.dma(kt_scr[h, :, HALO + L:LP], zero_b[:, 0:HALO], reads=[BC])
                for r0 in list(range(0, HALO, 128)) + list(range(HALO + L, LP, 128)):
                    S.dma(v_scr[r0:r0 + 128, :], zero_b[:, :], reads=[BC])

            tab = sbt(ps, "tab33", [33, NH], F32)
            oh = sbt(ps, "oh33", [33, 3, 384], F32)
            bsb = sbt(ps, "bias_sb", [4, 3, 384], F32)
            btmp = Buf()
            BBT = Buf("bt")
            V(lambda: nc.vector.memset(tab[32:33, :], NEG), writes=[btmp])
            S.dma(tab[0:32, :], rel_bias[:, :], writes=[btmp])
            S.dma(oh[:], onehot.rearrange("g b m -> b g m"), writes=[btmp])
            for gi in range(3):
                T(lambda: nc.tensor.matmul(bank(gi)[0:4, 0:384], tab[:, gi * 4:(gi + 1) * 4], oh[:, gi, :],
                                           start=True, stop=True), [btmp], [PB[gi]])
                V(lambda: nc.vector.tensor_copy(out=bsb[:, gi, :], in_=bank(gi)[0:4, 0:384]), [PB[gi]], [btmp])
                S.dma(bias_scr[gi * 4:(gi + 1) * 4, :], bsb[:, gi, :], reads=[btmp], writes=[BBT])
            for h in range(NH):
                S.dma(bias_rep[h], bass.AP(bias_scr_h, h * 384, [[0, 128], [1, 384]]), reads=[BBT], writes=[BBT])
            if dbg:
                S.dma(dbg_out["d_bias"], bias_scr, reads=[BBT], writes=[BBT])
            S.barrier()

        def make_wring(stack):
            NW = 3
            wring = [sbt(stack, "wring%d" % i, [128, 16, 512], BF16) for i in range(NW)]
            wbuf = [Buf("w%d" % i) for i in range(NW)]
            wcnt = [0]

            def wload(src_rows, nk, c0, cw=512):
                i = wcnt[0] % NW
                wcnt[0] += 1
                S.dma(wring[i][:, 0:nk, 0:cw], src_rows.rearrange("(k p) c -> p k c", p=128)[:, :, c0:c0 + cw],
                      writes=[wbuf[i]])
                return wring[i], wbuf[i]
            return wload

        def make_norm(stack):
            xh = [sbt(stack, "xhat%d" % i, [128, D], BF16) for i in range(2)]
            ssq = sbt(stack, "ssq", [128, 4], F32)
            rstd = sbt(stack, "rstd", [128, 4], F32)
            B_xh = [Buf("xhat0"), Buf("xhat1")]
            B_st = Buf("st")

            def stats(src_tile, B_src):
                for tt in range(4):
                    A(lambda: nc.scalar.activation(xh[tt % 2][:], src_tile[:, tt, :], AF.Square, accum_out=ssq[:, tt:tt + 1]),
                      [B_src], [B_xh[tt % 2], B_st])
                A(lambda: nc.scalar.activation(rstd[:], ssq[:], AF.Sqrt, scale=1.0 / D, bias=EPS), [B_st], [B_st])
                V(lambda: nc.vector.reciprocal(rstd[:], rstd[:]), [B_st], [B_st])

            def rmsnorm_to_T(src_tile, dstT, B_src, B_dst):
                stats(src_tile, B_src)
                for tt in range(4):
                    xhat, B_xhat = xh[tt % 2], B_xh[tt % 2]
                    V(lambda: nc.vector.tensor_scalar(xhat[:], src_tile[:, tt, :], rstd[:, tt:tt + 1], None, ALU.mult),
                      [B_src, B_st], [B_xhat])
                    psv = PS2[tt % 2][:].bitcast(BF16)
                    pbs = [PB[2 * (tt % 2)], PB[2 * (tt % 2) + 1]]
                    for kt in range(KT):
                        T(lambda: nc.tensor.transpose(psv[:, kt * 128:(kt + 1) * 128], xhat[:, kt * 128:(kt + 1) * 128],
                                                      ident_b[:]), [B_xhat, BC], pbs)
                    A(lambda: nc.scalar.copy(dstT[:, :, tt * 128:(tt + 1) * 128],
                                             psv[:, 0:2048].rearrange("p (k t) -> p k t", k=KT)), pbs, [B_dst])
            return rmsnorm_to_T, stats, rstd, B_st

        def proj_ws(wload, wsrc_rows, nk, col0, n_mt, rhsT, B_rhs, evac):
            for m0 in range(0, n_mt, 4):
                nm = min(4, n_mt - m0)
                wt, wb = wload(wsrc_rows, nk, col0 + m0 * 128, nm * 128)
                for mi in range(nm):
                    bi = ps_rr()
                    for k in range(nk):
                        T(lambda: nc.tensor.matmul(bank(bi), wt[:, k, mi * 128:(mi + 1) * 128], rhsT[:, k, :],
                                                   start=(k == 0), stop=(k == nk - 1)), [wb, B_rhs], [PB[bi]])
                    evac(m0 + mi, bank(bi), PB[bi])

        with ExitStack() as ps:
            wload = make_wring(ps)
            rmsnorm_to_T, _, _, _ = make_norm(ps)
            xb = sbt(ps, "xb", [128, 4, D], F32)
            xnT = sbt(ps, "xnT", [128, KT, NB], BF16)
            kst = [sbt(ps, "kst%d" % i, [128, NB], BF16) for i in range(4)]
            B_x, B_xnT = Buf("x"), Buf("xnT")
            B_kst = [Buf() for _ in range(4)]
            kc = [0]
            NCB = CTX // NB
            jobs = [("own", b) for b in range(NBLK)] + [("ctx", b) for b in range(NCB)]
            for kind, b in jobs:
                t0 = b * NB
                if kind == "own":
                    xsrc, udst, kvoff = xs, u_scr, HALO + t0
                else:
                    xsrc, udst = xc, uc_scr
                    kvoff = (HALO + L + t0) if b < 2 else ((t0 - (CTX - HALO)) if b >= NCB - 2 else None)
                S.dma(xb[:], xsrc[t0:t0 + NB, :].rearrange("(t p) d -> p t d", p=128), writes=[B_x])
                rmsnorm_to_T(xb, xnT, B_x, B_xnT)

                def ev_u(mt, psap, psb):
                    i = kc[0] % 4
                    kc[0] += 1
                    cp(mt, kst[i][:], psap, [psb], [B_kst[i]])
                    S.dma(udst[mt * 128:(mt + 1) * 128, t0:t0 + NB], kst[i][:], reads=[B_kst[i]])
                proj_ws(wload, wb_in, KT, 0, 8, xnT, B_xnT, ev_u)
                if kvoff is None:
                    continue

                def ev_k(mt, psap, psb):
                    i = kc[0] % 4
                    kc[0] += 1
                    cp(mt, kst[i][:], psap, [psb], [B_kst[i]])
                    S.dma(kt_scr[mt, :, kvoff:kvoff + NB], kst[i][:], reads=[B_kst[i]])
                proj_ws(wload, wb_in, KT, K_OFF, NH, xnT, B_xnT, ev_k)
                for gi in range(3):
                    wt, wb = wload(wb_in, KT, V_OFF + gi * 512, 512)
                    for tt in range(4):
                        bi = ps_rr()
                        for k in range(KT):
                            T(lambda: nc.tensor.matmul(bank(bi), xnT[:, k, tt * 128:(tt + 1) * 128], wt[:, k, :],
                                                       start=(k == 0), stop=(k == KT - 1)), [wb, B_xnT], [PB[bi]])
                        i = kc[0] % 4
                        kc[0] += 1
                        cp(tt, kst[i][:], bank(bi), [PB[bi]], [B_kst[i]])
                        S.dma(v_scr[kvoff + tt * 128:kvoff + (tt + 1) * 128, gi * 512:(gi + 1) * 512], kst[i][:],
                              reads=[B_kst[i]])
            S.barrier()

        with ExitStack() as ps:
            T1re = sbt(ps, "T1re", [128, NG, NS], BF16)
            T1im = sbt(ps, "T1im", [128, NG, NS], BF16)
            T2re = sbt(ps, "T2re", [128, NG, 128], BF16)
            T2im = sbt(ps, "T2im", [128, NG, 128], BF16)
            G1 = sbt(ps, "G1", [128, NG], F32)
            G2 = sbt(ps, "G2", [128, NG], F32)
            Bmat = sbt(ps, "Bmat", [128, 8, 8, 128], BF16)
            Cm1 = sbt(ps, "Cm1", [128, NG, 16], BF16)
            Cm2 = sbt(ps, "Cm2", [128, NG, 16], BF16)
            BTAB = Buf("ssmtab")

            def cmul(o_re, o_im, x_re, x_im, y_re, y_im, t1, t2, bufs):
                f = lambda fn: S.op(DVE, fn, bufs, bufs)
                E = nc.vector
                f(lambda: E.tensor_tensor(t1, x_re, y_re, ALU.mult))
                f(lambda: E.tensor_tensor(t2, x_im, y_im, ALU.mult))
                f(lambda: E.tensor_tensor(t2, t1, t2, ALU.subtract))
                f(lambda: E.tensor_tensor(t1, x_re, y_im, ALU.mult))
                f(lambda: E.tensor_tensor(o_im, x_im, y_re, ALU.mult))
                f(lambda: E.tensor_tensor(o_im, o_im, t1, ALU.add))
                f(lambda: E.tensor_copy(out=o_re, in_=t2))

            def abar_of(ar, ai, ldt, shape, stack, tag):
                bufs = [BTAB]
                mk = lambda nm: sbt(stack, tag + nm, shape, F32)
                dt_, lr, th, mag, cs, sn, t1, t2 = (mk(n)[:] for n in ("dt", "lr", "th", "mag", "cs", "sn", "t1", "t2"))
                o = {k: mk(k)[:] for k in ("abr", "abi", "air", "aii", "fr", "fi")}
                f = lambda fn: S.op(DVE, fn, bufs, bufs)
                fa = lambda fn: S.op(ACT, fn, bufs, bufs)
                fa(lambda: nc.scalar.activation(dt_, ldt, AF.Exp))
                f(lambda: nc.vector.tensor_tensor(lr, ar, dt_, ALU.mult))
                f(lambda: nc.vector.tensor_tensor(th, ai, dt_, ALU.mult))
                f(lambda: nc.vector.tensor_copy(out=t2, in_=th))
                for jj in range(1, 8):
                    f(lambda: nc.vector.tensor_scalar(t1, th, (2 * jj - 1) * math.pi, -2.0 * math.pi, ALU.is_gt, ALU.mult))
                    f(lambda: nc.vector.tensor_tensor(t2, t2, t1, ALU.add))
                fa(lambda: nc.scalar.activation(sn, t2, AF.Sin))
                f(lambda: nc.vector.tensor_scalar(t1, t2, -1.0, None, ALU.mult))
                f(lambda: nc.vector.tensor_tensor(t1, t1, t2, ALU.max))
                f(lambda: nc.vector.tensor_scalar(t1, t1, -1.0, math.pi / 2, ALU.mult, ALU.add))
                fa(lambda: nc.scalar.activation(cs, t1, AF.Sin))
                fa(lambda: nc.scalar.activation(mag, lr, AF.Exp))
                f(lambda: nc.vector.tensor_tensor(o["abr"], mag, cs, ALU.mult))
                f(lambda: nc.vector.tensor_tensor(o["abi"], mag, sn, ALU.mult))
                fa(lambda: nc.scalar.activation(mag, lr, AF.Exp, scale=-1.0))
                f(lambda: nc.vector.tensor_tensor(o["air"], mag, cs, ALU.mult))
                f(lambda: nc.vector.tensor_tensor(o["aii"], mag, sn, ALU.mult))
                f(lambda: nc.vector.tensor_scalar(o["aii"], o["aii"], -1.0, None, ALU.mult))
                f(lambda: nc.vector.tensor_tensor(t1, ar, ar, ALU.mult))
                f(lambda: nc.vector.tensor_tensor(t2, ai, ai, ALU.mult))
                f(lambda: nc.vector.tensor_tensor(t1, t1, t2, ALU.add))
                f(lambda: nc.vector.reciprocal(t1, t1))
                f(lambda: nc.vector.tensor_scalar(cs, o["abr"], -1.0, None, ALU.add))
                f(lambda: nc.vector.tensor_tensor(t2, cs, ar, ALU.mult))
                f(lambda: nc.vector.tensor_tensor(mag, o["abi"], ai, ALU.mult))
                f(lambda: nc.vector.tensor_tensor(t2, t2, mag, ALU.add))
                f(lambda: nc.vector.tensor_tensor(o["fr"], t2, t1, ALU.mult))
                f(lambda: nc.vector.tensor_tensor(t2, o["abi"], ar, ALU.mult))
                f(lambda: nc.vector.tensor_tensor(mag, cs, ai, ALU.mult))
                f(lambda: nc.vector.tensor_tensor(t2, t2, mag, ALU.subtract))
                f(lambda: nc.vector.tensor_tensor(o["fi"], t2, t1, ALU.mult))
                return o

            def gen_ssm_tables(dr):
                rev = (dr == 1)
                bufs = [BTAB]
                f = lambda fn: S.op(DVE, fn, bufs, bufs)
                with ExitStack() as p2:
                    arT = sbt(p2, "arT", [128, NG], F32)
                    aiT = sbt(p2, "aiT", [128, NG], F32)
                    ldT = sbt(p2, "ldT", [128, NG], F32)
                    for half in (0, 64):
                        S.dma(arT[half:half + 64, :], a_re[0, dr].rearrange("g n -> n g"), writes=bufs)
                        S.dma(aiT[half:half + 64, :], a_im[0, dr].rearrange("g n -> n g"), writes=bufs)
                    S.dma(ldT[:], log_dt[0, dr].partition_broadcast(128), writes=bufs)
                    st = abar_of(arT[:], aiT[:], ldT[:], [128, NG], p2, "s_")
                    GB = 16
                    Are = sbt(p2, "Are", [128, GB, 128], F32)
                    Aim = sbt(p2, "Aim", [128, GB, 128], F32)
                    Nre = sbt(p2, "Nre", [128, GB, 128], F32)
                    Nim = sbt(p2, "Nim", [128, GB, 128], F32)
                    tA = sbt(p2, "tA", [128, GB, 64], F32)
                    tB = sbt(p2, "tB", [128, GB, 64], F32)
                    cur = [sbt(p2, "cur%d" % i, [128, GB], F32) for i in range(4)]
                    tc1 = sbt(p2, "tc1", [128, GB], F32)
                    tc2 = sbt(p2, "tc2", [128, GB], F32)

                    def sl(lo, hi):
                        return slice(128 - hi, 128 - lo) if rev else slice(lo, hi)
                    for gb in range(NG // GB):
                        gs = slice(gb * GB, (gb + 1) * GB)
                        for (Pre, Pim, b_re_, b_im_, one) in ((Are, Aim, st["abr"], st["abi"], True),
                                                               (Nre, Nim, st["air"], st["aii"], False)):
                            if one:
                                f(lambda: nc.vector.memset(Pre[:, :, sl(0, 1)], 1.0))
                                f(lambda: nc.vector.memset(Pim[:, :, sl(0, 1)], 0.0))
                            else:
                                f(lambda: nc.vector.tensor_copy(out=Pre[:, :, sl(0, 1)], in_=st["fr"][:, gs].unsqueeze(2)))
                                f(lambda: nc.vector.tensor_copy(out=Pim[:, :, sl(0, 1)], in_=st["fi"][:, gs].unsqueeze(2)))
                            f(lambda: nc.vector.tensor_copy(out=cur[0][:], in_=b_re_[:, gs]))
                            f(lambda: nc.vector.tensor_copy(out=cur[1][:], in_=b_im_[:, gs]))
                            cr, ci, nr, ni = cur[0], cur[1], cur[2], cur[3]
                            w = 1
                            while w < 128:
                                bc = lambda t: t[:].unsqueeze(2).broadcast_to([128, GB, w])
                                cmul(Pre[:, :, sl(w, 2 * w)], Pim[:, :, sl(w, 2 * w)], Pre[:, :, sl(0, w)], Pim[:, :, sl(0, w)],
                                     bc(cr), bc(ci), tA[:, :, 0:w], tB[:, :, 0:w], bufs)
                                if 2 * w < 128:
                                    cmul(nr[:], ni[:], cr[:], ci[:], cr[:], ci[:], tc1[:], tc2[:], bufs)
                                    cr, ci, nr, ni = nr, ni, cr, ci
                                w *= 2
                        f(lambda: nc.vector.tensor_copy(out=T2re[:, gs, :], in_=Are[:]))
                        f(lambda: nc.vector.tensor_copy(out=T2im[:, gs, :], in_=Aim[:]))
                        e127 = 0 if rev else 127
                        cmul(G1[:, gs], G2[:, gs], Are[:, :, e127], Aim[:, :, e127], st["abr"][:, gs], st["abi"][:, gs],
                             tc1[:], tc2[:], bufs)
                        for src_t, dst_t in ((Nre, T1re), (Nim, T1im)):
                            for q4 in range(GB // 8):
                                for gg in range(8):
                                    g_l = q4 * 8 + gg
                                    T(lambda: nc.tensor.transpose(PS2[0][:, gg * 64:(gg + 1) * 64], src_t[0:64, g_l, :],
                                                                  ident_f[0:64, 0:64]), bufs + [BC], [PB[0]])
                                g0 = gb * GB + q4 * 8
                                A(lambda: nc.scalar.copy(dst_t[:, g0:g0 + 8, :].rearrange("p g n -> p (g n)"),
                                                         PS2[0][:, 0:512]), [PB[0]], bufs)
                    f(lambda: nc.vector.tensor_scalar(G2[:], G2[:], sgn[:, 0:1], None, ALU.mult))
                with ExitStack() as p2:
                    bl = sbt(p2, "bl", [64, 2, NG, 16], F32)
                    S.dma(bl[:, 0], b_re[0, dr].rearrange("g n c -> n g c"), writes=bufs)
                    S.dma(bl[:, 1], b_im[0, dr].rearrange("g n c -> n g c"), writes=bufs)
                    bcomp = sbt(p2, "bcomp", [128, 8, 128], F32)
                    for kt in range(8):
                        for ri in range(2):
                            T(lambda: nc.tensor.transpose(PS2[0][:, ri * 64:(ri + 1) * 64],
                                                          bl[:, ri, kt * 8:(kt + 1) * 8, :].rearrange("p g c -> p (g c)"),
                                                          ident_f[0:64, 0:64]), bufs + [BC], [PB[0]])
                        A(lambda: nc.scalar.copy(bcomp[:, kt, :], PS2[0][:, 0:128]), [PB[0]], bufs)
                    for g8 in range(8):
                        f(lambda: nc.vector.tensor_scalar(Bmat[:, :, g8, :], bcomp[:], gmask[:, g8:g8 + 1], None, ALU.mult))
                    cl = sbt(p2, "cl", [128, 8, 2, NS], F32)
                    for cm, (top, bot) in ((Cm1, (c_re, c_im)), (Cm2, (c_im, c_re))):
                        S.dma(cl[:, :, 0, :], top[0, dr].rearrange("(k g) c n -> (g c) k n", g=8), writes=bufs)
                        S.dma(cl[:, :, 1, :], bot[0, dr].rearrange("(k g) c n -> (g c) k n", g=8), writes=bufs)
                        for kt in range(8):
                            T(lambda: nc.tensor.transpose(PS2[kt // 4][:, (kt % 4) * 128:(kt % 4 + 1) * 128],
                                                          cl[:, kt].rearrange("p r n -> p (r n)"), ident_f[:]),
                              bufs + [BC], [PB[0], PB[2]])
                        for hf in range(2):
                            A(lambda: nc.scalar.copy(cm[:, hf * 32:(hf + 1) * 32, :].rearrange("p g c -> p (g c)"),
                                                     PS2[hf][:, 0:512]), [PB[0], PB[2]], bufs)
                    f(lambda: nc.vector.tensor_scalar(Cm1[64:128], Cm1[64:128], -1.0, None, ALU.mult))
                    f(lambda: nc.vector.tensor_scalar(Cm2[:], Cm2[:], -1.0, None, ALU.mult))
                S.barrier()

            uTb = [sbt(ps, "uT%d" % i, [128, 8, NB], BF16) for i in range(3)]
            B_uTb = [Buf() for _ in range(3)]
            yaccs = [sbt(ps, "yacc%d" % i, [128, 8, NB], F32) for i in range(2)]
            B_yaccs = [Buf("yacc0"), Buf("yacc1")]
            NMB = 3
            M1 = [sbt(ps, "M1_%d" % i, [128, 4, 128], BF16) for i in range(NMB)]
            M2 = [sbt(ps, "M2_%d" % i, [128, 4, 128], BF16) for i in range(NMB)]
            H1 = [sbt(ps, "H1_%d" % i, [128, 4, 128], BF16) for i in range(NMB)]
            H2 = [sbt(ps, "H2_%d" % i, [128, 4, 128], BF16) for i in range(NMB)]
            B_M = [Buf() for _ in range(NMB)]
            B_H = [Buf() for _ in range(NMB)]
            ccar = [sbt(ps, "ccar%d" % i, [128, NG], F32) for i in range(2)]
            xcar = sbt(ps, "xcar", [128, NG], F32)
            tcar = sbt(ps, "tcar", [128, NG], F32)
            ytok = sbt(ps, "ytok", [128, SSMW], F32)
            B_car, B_ytok = Buf("car"), Buf("ytok")
            flg = sbt(ps, "flg", [128, 2], F32)
            if CTX:
                S.dma(flg[:], flags_in, writes=[BC])

            def ssm_run(dr, chunks):
                tri, ntri = (tri_f, ntri_f) if dr == 0 else (tri_b, ntri_b)
                last = 127 if dr == 0 else 0
                units = [(ci, j) for ci in range(len(chunks)) for j in range(16)]
                NU = len(units)
                cprev = ccar[0]

                def Bu(t):
                    ci, j = units[t]
                    ck = chunks[ci]
                    if j == 0 and ck.get("pre"):
                        ck["pre"]()
                    kt, hf = j // 2, j % 2
                    tok = slice(ck["c"] * 128, (ck["c"] + 1) * 128)
                    T(lambda: nc.tensor.matmul(bank(t % 2), ck["uT"][:, kt, tok],
                                               Bmat[:, kt, hf * 4:(hf + 1) * 4, :].rearrange("p g x -> p (g x)"),
                                               start=True, stop=True), [ck["B_uT"], BTAB], [PB[t % 2]])

                def st1(t):
                    ci, j = units[t]
                    gs = slice(j * 4, (j + 1) * 4)
                    mi = t % NMB
                    pv = bank(t % 2).rearrange("p (g r n) -> p g r n", g=4, r=2)
                    V(lambda: nc.vector.tensor_tensor(M1[mi][:].rearrange("p g (r n) -> p g r n", r=2), pv,
                                                      T1re[:, gs, :].unsqueeze(2).broadcast_to([128, 4, 2, NS]), ALU.mult),
                      [PB[t % 2], BTAB], [B_M[mi]])
                    V(lambda: nc.vector.tensor_tensor(M2[mi][:].rearrange("p g (r n) -> p g r n", r=2), pv,
                                                      T1im[:, gs, :].unsqueeze(2).broadcast_to([128, 4, 2, NS]), ALU.mult),
                      [PB[t % 2], BTAB], [B_M[mi]])

                def csum(t):
                    ci, j = units[t]
                    ck = chunks[ci]
                    mi = t % NMB
                    if ck["summary"]:
                        for g4 in range(4):
                            g = j * 4 + g4
                            o = PS2[3][:, 512 + g:512 + g + 1]
                            T(lambda: nc.tensor.matmul(o, M1[mi][:, g4, :], tri_f[:, 127:128], start=True, stop=False,
                                                       skip_group_check=True), [B_M[mi], BC], [PB[7]])
                            T(lambda: nc.tensor.matmul(o[0:64, :], M2[mi][:, g4, 64:128], ntri_f[:, 127:128], start=False,
                                                       stop=False, skip_group_check=True), [B_M[mi], BC], [PB[7]])
                            T(lambda: nc.tensor.matmul(o[64:128, :], M2[mi][:, g4, 0:64], tri_f[:, 127:128], start=False,
                                                       stop=True, skip_group_check=True), [B_M[mi], BC], [PB[7]])
                        return
                    wbk = 2 + t % 2
                    for g4 in range(4):
                        o = bank(wbk)[:, g4 * 128:(g4 + 1) * 128]
                        T(lambda: nc.tensor.matmul(o, M1[mi][:, g4, :], tri[:], start=True, stop=False,
                                                   skip_group_check=True), [B_M[mi], BC], [PB[wbk]])
                        T(lambda: nc.tensor.matmul(o[0:64, :], M2[mi][:, g4, 64:128], ntri[:], start=False, stop=False,
                                                   skip_group_check=True), [B_M[mi], BC], [PB[wbk]])
                        T(lambda: nc.tensor.matmul(o[64:128, :], M2[mi][:, g4, 0:64], tri[:], start=False, stop=True,
                                                   skip_group_check=True), [B_M[mi], BC], [PB[wbk]])

                def carry_update(ck):
                    T(lambda: nc.tensor.matmul(PS2[3][:, 640:640 + NG], perm_f[:], xcar[:], start=True, stop=True,
                                               skip_group_check=True), [B_car, BC], [PB[7]])
                    V(lambda: nc.vector.tensor_tensor(tcar[:], G2[:], PS2[3][:, 640:640 + NG], ALU.mult),
                      [PB[7], BTAB, B_car], [B_car])
                    V(lambda: nc.vector.tensor_tensor(ccar[1][:], G1[:], xcar[:], ALU.mult), [B_car, BTAB], [B_car])
                    if ck.get("scale") is not None:
                        V(lambda: nc.vector.tensor_tensor(ccar[1][:], ccar[1][:], tcar[:], ALU.add), [B_car], [B_car])
                        V(lambda: nc.vector.tensor_scalar(ccar[0][:], ccar[1][:], ck["scale"], None, ALU.mult),
                          [B_car, BC], [B_car])
                    else:
                        V(lambda: nc.vector.tensor_tensor(ccar[0][:], ccar[1][:], tcar[:], ALU.add), [B_car], [B_car])

                def st2(t):
                    ci, j = units[t]
                    ck = chunks[ci]
                    mi = t % NMB
                    if ck["summary"]:
                        if j == 15:
                            if ck["first"]:
                                V(lambda: nc.vector.tensor_copy(out=xcar[:], in_=PS2[3][:, 512:512 + NG]), [PB[7]], [B_car])
                            else:
                                V(lambda: nc.vector.tensor_tensor(xcar[:], PS2[3][:, 512:512 + NG], cprev[:], ALU.add),
                                  [PB[7], B_car], [B_car])
                            carry_update(ck)
                        return
                    wbk = 2 + t % 2
                    gs = slice(j * 4, (j + 1) * 4)
                    wv = bank(wbk).rearrange("p (g t) -> p g t", g=4)
                    if ck["first"]:
                        V(lambda: nc.vector.tensor_copy(out=xcar[:, gs], in_=wv[:, :, last]), [PB[wbk]], [B_car])
                    else:
                        V(lambda: nc.vector.tensor_tensor(xcar[:, gs], wv[:, :, last], cprev[:, gs], ALU.add),
                          [PB[wbk], B_car], [B_car])
                    for g4 in range(4):
                        g = j * 4 + g4
                        wsl = bank(wbk)[:, g4 * 128:(g4 + 1) * 128]
                        if ck["first"]:
                            V(lambda: nc.vector.tensor_tensor(H1[mi][:, g4, :], wsl, T2re[:, g, :], ALU.mult),
                              [PB[wbk], BTAB], [B_H[mi]])
                            V(lambda: nc.vector.tensor_tensor(H2[mi][:, g4, :], wsl, T2im[:, g, :], ALU.mult),
                              [PB[wbk], BTAB], [B_H[mi]])
                        else:
                            V(lambda: nc.vector.scalar_tensor_tensor(H1[mi][:, g4, :], wsl, cprev[:, g:g + 1], T2re[:, g, :],
                                                                     ALU.add, ALU.mult), [PB[wbk], BTAB, B_car], [B_H[mi]])
                            V(lambda: nc.vector.scalar_tensor_tensor(H2[mi][:, g4, :], wsl, cprev[:, g:g + 1], T2im[:, g, :],
                                                                     ALU.add, ALU.mult), [PB[wbk], BTAB, B_car], [B_H[mi]])
                    if j == 15:
                        carry_update(ck)

                def cproj(t):
                    ci, j = units[t]
                    ck = chunks[ci]
                    if ck["summary"]:
                        return
                    mi = t % NMB
                    for g4 in range(4):
                        g = j * 4 + g4
                        o = PS2[2][:, g * 16:(g + 1) * 16]
                        T(lambda: nc.tensor.matmul(o, H1[mi][:, g4, :], Cm1[:, g, :], start=True, stop=False,
                                                   skip_group_check=True), [B_H[mi], BTAB], [PB[4 + g // 32]])
                        T(lambda: nc.tensor.matmul(o, H2[mi][:, g4, :], Cm2[:, g, :], start=False, stop=True,
                                                   skip_group_check=True), [B_H[mi], BTAB], [PB[4 + g // 32]])
                    if j == 15:
                        tok = slice(ck["c"] * 128, (ck["c"] + 1) * 128)
                        A(lambda: nc.scalar.copy(ytok[:], PS2[2][:]), [PB[4], PB[5]], [B_ytok])
                        for hf in range(2):
                            for k in range(4):
                                kk = hf * 4 + k
                                T(lambda: nc.tensor.transpose(bank(6)[:, k * 128:(k + 1) * 128],
                                                              ytok[:, kk * 128:(kk + 1) * 128], ident_f[:]),
                                  [B_ytok, BC], [PB[6]])
                            A(lambda: nc.scalar.copy(ck["yacc"][:, hf * 4:(hf + 1) * 4, tok],
                                                     bank(6).rearrange("p (k t) -> p k t", k=4)), [PB[6]], [ck["B_yacc"]])
                        if ck.get("post"):
                            ck["post"]()

                for t in range(-1, NU + 1):
                    if t + 1 < NU:
                        Bu(t + 1)
                        st1(t + 1)
                    if 0 <= t < NU:
                        csum(t)
                    if 0 <= t - 1:
                        cproj(t - 1)
                    if 0 <= t < NU:
                        st2(t)

            def mk_chunks(dr, own_posts):
                chunks = []
                ub = [0]
                nctx = CTX // NB
                order = range(nctx - 1, -1, -1) if dr == 1 else range(nctx)
                corder = range(3, -1, -1) if dr == 1 else range(4)
                for ib, b in enumerate(order):
                    ui = ub[0] % 3
                    ub[0] += 1

                    def pre(ui=ui, b=b):
                        S.dma(uTb[ui][:], uc_scr[:, b * NB:(b + 1) * NB].rearrange("(k p) t -> p k t", p=128),
                              writes=[B_uTb[ui]])
                    for ic, c in enumerate(corder):
                        lastc = (ib == nctx - 1 and ic == 3)
                        chunks.append(dict(uT=uTb[ui], B_uT=B_uTb[ui], c=c, summary=True, first=(ib == 0 and ic == 0),
                                           pre=pre if ic == 0 else None,
                                           scale=(flg[:, 1:2] if dr == 1 else flg[:, 0:1]) if lastc else None))
                border = range(NBLK - 1, -1, -1) if dr == 1 else range(NBLK)
                for ib, b in enumerate(border):
                    ui = ub[0] % 3
                    ub[0] += 1
                    yi = ib % 2

                    def pre(ui=ui, b=b):
                        S.dma(uTb[ui][:], u_scr[:, b * NB:(b + 1) * NB].rearrange("(k p) t -> p k t", p=128),
                              writes=[B_uTb[ui]])
                    for ic, c in enumerate(corder):
                        post = None
                        if ic == 3:
                            post = (lambda b=b, ui=ui, yi=yi: own_posts(b, ui, yi))
                        chunks.append(dict(uT=uTb[ui], B_uT=B_uTb[ui], c=c, summary=False,
                                           first=(ib == 0 and ic == 0 and not CTX), pre=pre if ic == 0 else None,
                                           post=post, yacc=yaccs[yi], B_yacc=B_yaccs[yi]))
                return chunks

            gen_ssm_tables(1)

            def post_bwd(b, ui, yi):
                S.dma(yb_scr[:, b * NB:(b + 1) * NB].rearrange("(k p) t -> p k t", p=128), yaccs[yi][:],
                      reads=[B_yaccs[yi]])
            ssm_run(1, mk_chunks(1, post_bwd))
            S.barrier()

            gen_ssm_tables(0)
            wglu = sbt(ps, "wglu", [128, 8, SSMW], BF16)
            B_wglu = Buf()
            S.dma(wglu[:], wb_glu.rearrange("(k p) c -> p k c", p=128), writes=[B_wglu])
            ybt = [sbt(ps, "ybt%d" % i, [128, NB], F32) for i in range(2)]
            B_ybt = [Buf() for _ in range(2)]
            gt = [sbt(ps, "gt%d" % i, [128, NB], F32) for i in range(2)]
            B_gt = [Buf() for _ in range(2)]
            zT = sbt(ps, "zT", [128, 8, NB], BF16)
            ysT = sbt(ps, "ysT", [128, 8, NB], BF16)
            sgl = [sbt(ps, "sgl%d" % i, [128, NB], BF16) for i in range(2)]
            B_sgl = [Buf() for _ in range(2)]
            B_z, B_ys = Buf("z"), Buf("ys")
            yc = [0]

            def post_fwd(b, ui, yi):
                t0 = b * NB
                yacc, B_yacc = yaccs[yi], B_yaccs[yi]
                for k in range(8):
                    i = yc[0] % 2
                    yc[0] += 1
                    S.dma(ybt[i][:], yb_scr[k * 128:(k + 1) * 128, t0:t0 + NB], writes=[B_ybt[i]])
                    yk = yacc[:, k, :]
                    G(lambda: nc.gpsimd.scalar_tensor_tensor(yk, uTb[ui][:, k, :], dcol[:, k:k + 1], yk, ALU.mult, ALU.add)
                      if False else nc.gpsimd.tensor_tensor(yk, yk, ybt[i][:], ALU.add), [B_yacc, B_ybt[i]], [B_yacc])
                    G(lambda: nc.gpsimd.tensor_scalar(gt[0][:], uTb[ui][:, k, :], dcol[:, k:k + 1], None, ALU.mult),
                      [B_uTb[ui], BC], [B_gt[0]])
                    G(lambda: nc.gpsimd.tensor_tensor(yk, yk, gt[0][:], ALU.add), [B_yacc, B_gt[0]], [B_yacc])
                    if dbg:
                        S.dma(dbg_out["d_yb"][k * 128:(k + 1) * 128, t0:t0 + NB], ybt[i][:], reads=[B_ybt[i]])
                    G(lambda: nc.gpsimd.tensor_tensor(gt[0][:], yk, yk, ALU.mult), [B_yacc], [B_gt[0]])
                    G(lambda: nc.gpsimd.tensor_scalar(gt[0][:], gt[0][:], 0.044715, 1.0, ALU.mult, ALU.add), [B_gt[0]], [B_gt[0]])
                    G(lambda: nc.gpsimd.tensor_tensor(gt[0][:], gt[0][:], yk, ALU.mult), [B_gt[0], B_yacc], [B_gt[0]])
                    A(lambda: nc.scalar.activation(gt[1][:], gt[0][:], AF.Sigmoid, scale=GELU_C), [B_gt[0]], [B_gt[1]])
                    G(lambda: nc.gpsimd.tensor_tensor(zT[:, k, :], yk, gt[1][:], ALU.mult), [B_gt[1], B_yacc], [B_z])
                for mt in range(8):
                    for k in range(8):
                        T(lambda: nc.tensor.matmul(bank(7), wglu[:, k, mt * 128:(mt + 1) * 128], zT[:, k, :],
                                                   start=(k == 0), stop=(k == 7), skip_group_check=True),
                          [B_wglu, B_z], [PB[7]])
                    i = mt % 2
                    A(lambda: nc.scalar.activation(sgl[i][:], bank(7), AF.Sigmoid, bias=bgcol[:, mt:mt + 1]),
                      [PB[7], BC], [B_sgl[i]])
                    G(lambda: nc.gpsimd.tensor_tensor(ysT[:, mt, :], zT[:, mt, :], sgl[i][:], ALU.mult),
                      [B_z, B_sgl[i]], [B_ys])
                S.dma(ys_scr[:, t0:t0 + NB].rearrange("(k p) t -> p k t", p=128), ysT[:], reads=[B_ys])
                if dbg:
                    for k in range(8):
                        G(lambda: nc.gpsimd.tensor_copy(out=gt[0][:], in_=ysT[:, k, :]), [B_ys], [B_gt[0]])
                        S.dma(dbg_out["d_ys"][k * 128:(k + 1) * 128, t0:t0 + NB], gt[0][:], reads=[B_gt[0]])
            ssm_run(0, mk_chunks(0, post_fwd))
            S.barrier()

        with ExitStack() as ps:
            wload = make_wring(ps)
            rmsnorm_to_T, stats, rstd, B_st = make_norm(ps)
            gfin = sbt(ps, "gfin", [128, D], F32)
            S.dma(gfin[:], final_g.partition_broadcast(128), writes=[BC])
            NQs = (128, 128, 32)
            bt_hi = sbt(ps, "bt_hi", [128, 24 * 128], BF16)
            bt_lo = sbt(ps, "bt_lo", [128, 24 * 128], BF16)
            BT = {}
            with ExitStack() as p2:
                bst = sbt(p2, "bst", [128, 24 * 128], F32)
                V(lambda: nc.vector.memset(bst[:], 0.0), writes=[BBT])
                for gi in range(3):
                    nq = NQs[gi]
                    for j in range(4):
                        for k2 in range(2):
                            col = ((gi * 4 + j) * 2 + k2) * 128
                            src = bass.AP(bias_rep_h, (gi * 4 + j) * 128 * 384 + 255 - 128 * k2, [[383, 128], [1, nq]])
                            S.dma(bst[:, col:col + nq], src, reads=[BBT], writes=[BBT])
                            BT[(gi, j, k2)] = (bt_hi[:, col:col + nq], bt_lo[:, col:col + nq])
                V(lambda: nc.vector.tensor_copy(out=bt_hi[:], in_=bst[:]), [BBT], [BBT])
                V(lambda: nc.vector.tensor_tensor(bst[:], bst[:], bt_hi[:], ALU.subtract), [BBT], [BBT])
                V(lambda: nc.vector.tensor_copy(out=bt_lo[:], in_=bst[:]), [BBT], [BBT])
                S.barrier()
            xb = sbt(ps, "xb", [128, 4, D], F32)
            xnT = sbt(ps, "xnT", [128, KT, NB], BF16)
            ysT = sbt(ps, "ysT2", [128, 8, NB], BF16)
            qT = sbt(ps, "qT", [128, NH, NB], BF16)
            mixT = sbt(ps, "mixT", [128, KT, NB], BF16)
            yaT = sbt(ps, "yaT", [128, 4, NB], BF16)
            sg = [sbt(ps, "sg%d" % i, [128, NB], BF16) for i in range(2)]
            sgt = [sbt(ps, "sgt%d" % i, [128, NB], BF16) for i in range(2)]
            B_x, B_xnT, B_ys, B_q, B_mixT, B_ya = (Buf(n) for n in "x xnT ys q mixT ya".split())
            B_sg = [Buf() for _ in range(2)]
            B_sgt = [Buf() for _ in range(2)]
            sgc = [0]
            kwin = [sbt(ps, "kwin%d" % i, [128, NB + 2 * HALO], BF16) for i in range(2)]
            B_kwin = [Buf() for _ in range(2)]
            vwin = [sbt(ps, "vwin%d" % i, [128, 2, 256], BF16) for i in range(3)]
            B_vwin = [Buf() for _ in range(3)]
            kmc = [sbt(ps, "kmc%d" % i, [128, 2], F32) for i in range(3)]
            PT = [sbt(ps, "PT%d" % i, [128, 2, 128], BF16) for i in range(3)]
            B_PT = [Buf() for _ in range(3)]
            rden = sbt(ps, "rden", [128, NB], F32)
            B_rden = Buf()
            actq = sbt(ps, "actq", [128, 8, NB], BF16)
            relu_t = [sbt(ps, "relu%d" % i, [128, NB], BF16) for i in range(2)]
            B_relu = [Buf() for _ in range(2)]
            B_actq = Buf()
            kwc, vwc, ptc = [0], [0], [0]

            for b in range(NBLK):
                t0 = b * NB
                S.dma(xb[:], xs[t0:t0 + NB, :].rearrange("(t p) d -> p t d", p=128), writes=[B_x])
                S.dma(ysT[:], ys_scr[:, t0:t0 + NB].rearrange("(k p) t -> p k t", p=128), writes=[B_ys])
                rmsnorm_to_T(xb, xnT, B_x, B_xnT)
                for m0 in range(0, KT, 4):
                    wtg, wbg = wload(wb_in, KT, G_OFF + m0 * 128, 512)
                    wtb, wbb = wload(wb_brs, 8, m0 * 128, 512)
                    for mi in range(4):
                        mt = m0 + mi
                        bi = ps_rr()
                        for k in range(KT):
                            T(lambda: nc.tensor.matmul(bank(bi), wtg[:, k, mi * 128:(mi + 1) * 128], xnT[:, k, :],
                                                       start=(k == 0), stop=(k == KT - 1)), [wbg, B_xnT], [PB[bi]])
                        i = sgc[0] % 2
                        sgc[0] += 1
                        A(lambda: nc.scalar.activation(sg[i][:], bank(bi), AF.Sigmoid), [PB[bi]], [B_sg[i]])
                        bj = ps_rr()
                        for k in range(8):
                            T(lambda: nc.tensor.matmul(bank(bj), wtb[:, k, mi * 128:(mi + 1) * 128], ysT[:, k, :],
                                                       start=(k == 0), stop=(k == 7)), [wbb, B_ys], [PB[bj]])
                        V(lambda: nc.vector.tensor_tensor(mixT[:, mt, :], bank(bj), sg[i][:], ALU.mult),
                          [PB[bj], B_sg[i]], [B_mixT])

                def ev_q(mt, psap, psb):
                    A(lambda: nc.scalar.activation(qT[:, mt, :], psap, AF.Copy, scale=HD ** -0.5), [psb], [B_q])
                proj_ws(wload, wb_in, KT, Q_OFF, NH, xnT, B_xnT, ev_q)

                for rnd in range(2):
                    first_mm = {0: True, 1: True}
                    for gi in range(3):
                        d = DIL[gi]
                        reach = 64 * d
                        nq = NQs[gi]
                        nk2 = (128, nq)
                        nunits = 4 if gi == 0 else d
                        kw = {}
                        for jj in range(2):
                            h = gi * 4 + rnd * 2 + jj
                            i = kwc[0] % 2
                            kwc[0] += 1
                            S.dma(kwin[i][:, 0:NB + 2 * reach], kt_scr[h, :, HALO + t0 - reach:HALO + t0 + NB + reach],
                                  writes=[B_kwin[i]])
                            kw[jj] = i
                        for u in range(nunits):
                            if gi == 0:
                                qsl = slice(u * 128, (u + 1) * 128)
                                kcol0, kstep = u * 128, 1
                                tstart = t0 + u * 128 - 64
                            else:
                                qsl = slice(u, NB, d)
                                kcol0, kstep = u, d
                                tstart = t0 + u - reach
                            vi = vwc[0] % 3
                            vwc[0] += 1
                            c0 = gi * 512 + rnd * 256
                            for k2 in range(2):
                                n2 = nk2[k2]
                                r0 = HALO + tstart + k2 * 128 * kstep
                                src = bass.AP(v_scr_h, r0 * (NH * HD) + c0, [[kstep * NH * HD, n2], [1, 256]])
                                S.dma(vwin[vi][0:n2, k2, :], src, writes=[B_vwin[vi]])
                                srcm = bass.AP(kmask_h, r0, [[kstep, n2], [1, 1]])
                                S.dma(kmc[vi][0:n2, k2:k2 + 1], srcm, writes=[B_vwin[vi]])
                            for jj in range(2):
                                j = rnd * 2 + jj
                                h = gi * 4 + j
                                ki = kw[jj]
                                pi_ = ptc[0] % 3
                                ptc[0] += 1
                                sb_i = ps_rr()
                                for k2 in range(2):
                                    n2 = nk2[k2]
                                    kc0 = kcol0 + k2 * 128 * kstep
                                    ksl = slice(kc0, kc0 + (n2 - 1) * kstep + 1, kstep)
                                    so = bank(sb_i)[0:n2, k2 * 128:k2 * 128 + nq]
                                    bhi, blo = BT[(gi, j, k2)]
                                    T(lambda: nc.tensor.matmul(so, kwin[ki][:, ksl], qT[:, h, qsl], start=True, stop=False,
                                                               skip_group_check=True), [B_kwin[ki], B_q], [PB[sb_i]])
                                    T(lambda: nc.tensor.matmul(so, ident_b[0:n2, 0:n2], bhi[0:n2, :], start=False, stop=False,
                                                               skip_group_check=True), [BBT, BC], [PB[sb_i]])
                                    T(lambda: nc.tensor.matmul(so, ident_b[0:n2, 0:n2], blo[0:n2, :], start=False, stop=True,
                                                               skip_group_check=True), [BBT, BC], [PB[sb_i]])
                                    A(lambda: nc.scalar.activation(PT[pi_][0:n2, k2, 0:nq], so, AF.Exp,
                                                                   bias=kmc[vi][0:n2, k2:k2 + 1]),
                                      [PB[sb_i], B_vwin[vi]], [B_PT[pi_]])
                                ob, db = 4 + jj, 6 + jj
                                for k2 in range(2):
                                    n2 = nk2[k2]
                                    T(lambda: nc.tensor.matmul(bank(ob)[:, qsl], vwin[vi][0:n2, k2, jj * 128:(jj + 1) * 128],
                                                               PT[pi_][0:n2, k2, 0:nq], start=first_mm[jj], stop=False,
                                                               skip_group_check=True),
                                      [B_vwin[vi], B_PT[pi_]], [PB[ob]])
                                    T(lambda: nc.tensor.matmul(bank(db)[:, qsl], ones_b[0:n2, :], PT[pi_][0:n2, k2, 0:nq],
                                                               start=first_mm[jj], stop=False, skip_group_check=True),
                                      [BC, B_PT[pi_]], [PB[db]])
                                    first_mm[jj] = False
                    for jj in range(2):
                        j = rnd * 2 + jj
                        V(lambda: nc.vector.tensor_scalar(rden[:], bank(6 + jj), 1e-30, None, ALU.add), [PB[6 + jj]], [B_rden])
                        V(lambda: nc.vector.reciprocal(rden[:], rden[:]), [B_rden], [B_rden])
                        V(lambda: nc.vector.tensor_tensor(yaT[:, j, :], bank(4 + jj), rden[:], ALU.mult),
                          [PB[4 + jj], B_rden], [B_ya])
                if dbg:
                    for j in range(4):
                        V(lambda: nc.vector.tensor_copy(out=rden[:], in_=yaT[:, j, :]), [B_ya], [B_rden])
                        S.dma(dbg_out["d_ya"][j * 128:(j + 1) * 128, t0:t0 + NB], rden[:], reads=[B_rden])

                for m0 in range(0, KT, 4):
                    wtg, wbg = wload(wb_in, KT, G_OFF + D + m0 * 128, 512)
                    wtb, wbb = wload(wb_bra, 4, m0 * 128, 512)
                    for mi in range(4):
                        mt = m0 + mi
                        bi = ps_rr()
                        for k in range(KT):
                            T(lambda: nc.tensor.matmul(bank(bi), wtg[:, k, mi * 128:(mi + 1) * 128], xnT[:, k, :],
                                                       start=(k == 0), stop=(k == KT - 1)), [wbg, B_xnT], [PB[bi]])
                        i = sgc[0] % 2
                        sgc[0] += 1
                        A(lambda: nc.scalar.activation(sg[i][:], bank(bi), AF.Sigmoid), [PB[bi]], [B_sg[i]])
                        bj = ps_rr()
                        for k in range(4):
                            T(lambda: nc.tensor.matmul(bank(bj), wtb[:, k, mi * 128:(mi + 1) * 128], yaT[:, k, :],
                                                       start=(k == 0), stop=(k == 3)), [wbb, B_ya], [PB[bj]])
                        V(lambda: nc.vector.tensor_tensor(sgt[i][:], bank(bj), sg[i][:], ALU.mult),
                          [PB[bj], B_sg[i]], [B_sgt[i]])
                        G(lambda: nc.gpsimd.tensor_tensor(mixT[:, mt, :], sgt[i][:], mixT[:, mt, :], ALU.add),
                          [B_sgt[i], B_mixT], [B_mixT])

                for cg in range(4):
                    wt, wb = wload(wb_out, KT, cg * 512, 512)
                    for tt in range(4):
                        bi = ps_rr()
                        for k in range(KT):
                            T(lambda: nc.tensor.matmul(bank(bi), mixT[:, k, tt * 128:(tt + 1) * 128], wt[:, k, :],
                                                       start=(k == 0), stop=(k == KT - 1)), [wb, B_mixT], [PB[bi]])
                        xsl = xb[:, tt, cg * 512:(cg + 1) * 512]
                        V(lambda: nc.vector.tensor_tensor(xsl, bank(bi), xsl, ALU.add), [PB[bi], B_x], [B_x])
                rmsnorm_to_T(xb, xnT, B_x, B_xnT)
                for qf in range(8):
                    def ev_ff1(mt, psap, psb):
                        i = sgc[0] % 2
                        sgc[0] += 1
                        A(lambda: nc.scalar.activation(relu_t[i][:], psap, AF.Relu), [psb], [B_relu[i]])
                        V(lambda: nc.vector.scalar_tensor_tensor(actq[:, mt, :], psap, 0.0, relu_t[i][:], ALU.max, ALU.mult),
                          [psb, B_relu[i]], [B_actq])
                    proj_ws(wload, wb_ff1, KT, qf * 1024, 8, xnT, B_xnT, ev_ff1)
                    for cg in range(4):
                        wt, wb = wload(wb_ff2[qf * 1024:(qf + 1) * 1024, :], 8, cg * 512, 512)
                        for tt in range(4):
                            bi = ps_rr()
                            for k in range(8):
                                T(lambda: nc.tensor.matmul(bank(bi), actq[:, k, tt * 128:(tt + 1) * 128], wt[:, k, :],
                                                           start=(k == 0), stop=(k == 7)), [wb, B_actq], [PB[bi]])
                            xsl = xb[:, tt, cg * 512:(cg + 1) * 512]
                            V(lambda: nc.vector.tensor_tensor(xsl, bank(bi), xsl, ALU.add), [PB[bi], B_x], [B_x])
                stats(xb, B_x)
                for tt in range(4):
                    V(lambda: nc.vector.scalar_tensor_tensor(xb[:, tt, :], xb[:, tt, :], rstd[:, tt:tt + 1], gfin[:],
                                                             ALU.mult, ALU.mult), [B_x, B_st, BC], [B_x])
                S.dma(ys[t0:t0 + NB, :].rearrange("(t p) d -> p t d", p=128), xb[:], reads=[B_x])
            S.barrier()
    return nc


_PROG = {}


def _get_prog(L, dbg=False, CTX=0):
    key = (L, dbg, CTX)
    if key not in _PROG:
        _PROG[key] = build_program(L, dbg, CTX)
    return _PROG[key]


def _kmask(L, nvalid):
    m = np.full((L + 2 * HALO, 1), NEG, np.float32)
    m[HALO:HALO + nvalid] = 0.0
    return m


def kernel(**inputs):
    f32 = lambda a: np.ascontiguousarray(np.asarray(a, dtype=np.float32))
    xp = f32(inputs["x_prompt"])
    xsm = f32(inputs["x_sample"])
    B, SL, _ = xp.shape
    LS = xsm.shape[1]
    assert xsm.shape[0] == 1 and LS == 2 * SL and B + 2 <= 8
    L = SL
    shared = {}
    for k in ("norm1_g", "w_in", "ssm_a_re", "ssm_a_im", "ssm_log_dt", "ssm_b_re", "ssm_b_im", "ssm_c_re",
              "ssm_c_im", "ssm_d", "w_glu", "b_glu", "w_br_ssm", "w_br_attn", "w_out", "norm2_g", "w_ff1",
              "w_ff2", "rel_bias", "final_g"):
        shared[k] = f32(inputs[k])
    shared["onehot"] = _t5_onehot()
    zeros_ctx = np.zeros((L, D), np.float32)

    def km(left, right):
        m = np.full((L + 2 * HALO, 1), NEG, np.float32)
        m[HALO:HALO + L] = 0.0
        if left:
            m[:HALO] = 0.0
        if right:
            m[HALO + L:] = 0.0
        return m

    def fl(a, b):
        f = np.zeros((128, 2), np.float32)
        f[:, 0] = a
        f[:, 1] = b
        return f
    in_maps = []
    for c in range(8):
        m = dict(shared)
        if c < B:
            m["xs"], m["xc"], m["kmask"], m["flags"] = xp[c], zeros_ctx, km(False, False), fl(0.0, 0.0)
        elif c == B:
            m["xs"], m["xc"], m["kmask"], m["flags"] = xsm[0, :L], xsm[0, L:], km(False, True), fl(0.0, 1.0)
        elif c == B + 1:
            m["xs"], m["xc"], m["kmask"], m["flags"] = xsm[0, L:], xsm[0, :L], km(True, False), fl(1.0, 0.0)
        else:
            m["xs"], m["xc"], m["kmask"], m["flags"] = zeros_ctx, zeros_ctx, km(False, False), fl(0.0, 0.0)
        in_maps.append(m)
    nc = _get_prog(L, CTX=L)
    res = run_bass_kernel_spmd(nc, in_maps, core_ids=list(range(8)))
    y_prompt = np.stack([np.asarray(res.results[c]["ys"], dtype=np.float32) for c in range(B)], axis=0)
    y_sample = np.concatenate([np.asarray(res.results[B]["ys"], dtype=np.float32),
                               np.asarray(res.results[B + 1]["ys"], dtype=np.float32)], axis=0)[None]
    return (y_prompt, y_sample)
```

```python
import math
from contextlib import ExitStack

import numpy as np
import concourse.bass as bass
import concourse.mybir as mybir
from concourse.bass_utils import run_bass_kernel_spmd

F32 = mybir.dt.float32
BF16 = mybir.dt.bfloat16
AF = mybir.ActivationFunctionType
ALU = mybir.AluOpType

D = 2048
KT = 16
SSMW = 1024
NG = 64
NS = 64
NH = 12
HD = 128
INC = 9728
Q_OFF, K_OFF, V_OFF, G_OFF = 1024, 2560, 4096, 5632
DFF = 8192
NB = 512
HALO = 1024
NEG = -30000.0
EPS = 1e-6
DIL = (1, 4, 16)
GELU_C = 2.0 * math.sqrt(2.0 / math.pi)


class Buf:
    __slots__ = ("w", "r", "name")

    def __init__(self, name=""):
        self.w = {}
        self.r = {}
        self.name = name


class Eng:
    def __init__(self, eng, sem, is_pe=False):
        self.eng = eng
        self.sem = sem
        self.cnt = 0
        self.known = {}
        self.is_pe = is_pe


class Sync:
    NR = 16

    def __init__(self, nc, es):
        self.nc = nc
        self.sems = {}

        def mk(name):
            s = es.enter_context(nc.semaphore(name))
            self.sems[id(s)] = s
            return s

        self.pe = Eng(nc.tensor, mk("s_pe"), is_pe=True)
        self.act = Eng(nc.scalar, mk("s_act"))
        self.dve = Eng(nc.vector, mk("s_dve"))
        self.pool = Eng(nc.gpsimd, mk("s_pool"))
        self.sp = Eng(nc.sync, mk("s_sp"))
        self.ring = [mk("s_dma%d" % i) for i in range(self.NR)]
        self.ring_cnt = [0] * self.NR
        self.n_dma = 0

    def _waits(self, E, reads, writes):
        need = {}
        for b in reads:
            for k, v in b.w.items():
                if need.get(k, 0) < v:
                    need[k] = v
        for b in writes:
            for k, v in b.w.items():
                if need.get(k, 0) < v:
                    need[k] = v
            for k, v in b.r.items():
                if need.get(k, 0) < v:
                    need[k] = v
        for k, v in need.items():
            if E.is_pe and k == id(E.sem):
                continue
            if E.known.get(k, 0) >= v:
                continue
            E.eng.wait_ge(self.sems[k], v)
            E.known[k] = v

    def op(self, E, fn, reads=(), writes=()):
        self._waits(E, reads, writes)
        inst = fn()
        E.cnt += 1
        inst.then_inc(E.sem, 1)
        k = id(E.sem)
        for b in writes:
            b.w = {k: E.cnt}
            b.r = {}
        for b in reads:
            if b.r.get(k, 0) < E.cnt:
                b.r[k] = E.cnt

    def dma(self, out, in_, reads=(), writes=(), q=None, **kw):
        E = q if q is not None else self.sp
        i = self.n_dma % self.NR
        self.n_dma += 1
        sem = self.ring[i]
        k = id(sem)
        prev = self.ring_cnt[i]
        if prev > 0 and E.known.get(k, 0) < prev:
            E.eng.wait_ge(sem, prev)
            E.known[k] = prev
        self._waits(E, reads, writes)
        E.eng.dma_start(out=out, in_=in_, **kw).then_inc(sem, 16)
        self.ring_cnt[i] = prev + 16
        for b in writes:
            b.w = {k: prev + 16}
            b.r = {}
        for b in reads:
            if b.r.get(k, 0) < prev + 16:
                b.r[k] = prev + 16

    def barrier(self):
        engs = (self.pe, self.act, self.dve, self.pool, self.sp)
        for E in engs:
            for X in engs:
                if X is E or X.cnt == 0:
                    continue
                k = id(X.sem)
                if E.known.get(k, 0) < X.cnt:
                    E.eng.wait_ge(X.sem, X.cnt)
                    E.known[k] = X.cnt
            for i in range(self.NR):
                c = self.ring_cnt[i]
                k = id(self.ring[i])
                if c > 0 and E.known.get(k, 0) < c:
                    E.eng.wait_ge(self.ring[i], c)
                    E.known[k] = c


def _t5_onehot():
    oh = np.zeros((3, 33, 384), np.float32)
    for gi, d in enumerate(DIL):
        for m in range(384):
            rel = 191 - m
            if abs(rel) > 64:
                oh[gi, 32, m] = 1.0
                continue
            r = rel * d
            ret = 16 if r > 0 else 0
            n = abs(r)
            nf = np.float32(max(n, 1))
            large = 8 + int(np.float32(np.log(nf / np.float32(8.0)) / np.float32(math.log(128.0)) * np.float32(8.0)))
            large = min(large, 15)
            b = ret + (n if n < 8 else large)
            oh[gi, b, m] = 1.0
    return oh


def build_program(L, dbg=False, CTX=0):
    assert L % NB == 0 and CTX % NB == 0 and (CTX == 0 or CTX >= 2 * HALO)
    NBLK = L // NB
    LP = L + 2 * HALO
    nc = bass.Bass("TRN2", target_bir_lowering=False)

    def din(name, shape, dt=F32):
        return nc.dram_tensor(name, list(shape), dt, kind="ExternalInput")

    def dscr_early(name, shape, dt):
        return nc.dram_tensor(name, list(shape), dt, kind="Internal")

    xs = din("xs", [L, D]).ap()
    kmask_h = din("kmask", [LP, 1])
    if CTX:
        xc = din("xc", [CTX, D]).ap()
        flags_in = din("flags", [128, 2]).ap()
        uc_scr = dscr_early("uc_scr", [SSMW, CTX], BF16).ap()
    norm1_g = din("norm1_g", [1, D]).ap()
    w_in = din("w_in", [1, D, INC]).ap()
    a_re = din("ssm_a_re", [1, 2, NG, NS]).ap()
    a_im = din("ssm_a_im", [1, 2, NG, NS]).ap()
    log_dt = din("ssm_log_dt", [1, 2, NG]).ap()
    b_re = din("ssm_b_re", [1, 2, NG, NS, 16]).ap()
    b_im = din("ssm_b_im", [1, 2, NG, NS, 16]).ap()
    c_re = din("ssm_c_re", [1, 2, NG, 16, NS]).ap()
    c_im = din("ssm_c_im", [1, 2, NG, 16, NS]).ap()
    ssm_d = din("ssm_d", [1, SSMW]).ap()
    w_glu = din("w_glu", [1, SSMW, SSMW]).ap()
    b_glu = din("b_glu", [1, SSMW]).ap()
    w_brs = din("w_br_ssm", [1, SSMW, D]).ap()
    w_bra = din("w_br_attn", [1, 512, D]).ap()
    w_out = din("w_out", [1, D, D]).ap()
    norm2_g = din("norm2_g", [1, D]).ap()
    w_ff1 = din("w_ff1", [1, D, DFF]).ap()
    w_ff2 = din("w_ff2", [1, DFF, D]).ap()
    rel_bias = din("rel_bias", [32, NH]).ap()
    final_g = din("final_g", [D]).ap()
    onehot = din("onehot", [3, 33, 384]).ap()
    ys = nc.dram_tensor("ys", [L, D], F32, kind="ExternalOutput").ap()
    dbg_out = {}
    if dbg:
        for nm, shp in (("d_yb", [SSMW, L]), ("d_ys", [SSMW, L]), ("d_ya", [512, L]), ("d_bias", [NH, 384])):
            dbg_out[nm] = nc.dram_tensor(nm, shp, F32, kind="ExternalOutput").ap()

    def dscr(name, shape, dt):
        return nc.dram_tensor(name, list(shape), dt, kind="Internal")

    wb_in = dscr("wb_in", [D, INC], BF16).ap()
    wb_glu = dscr("wb_glu", [SSMW, SSMW], BF16).ap()
    wb_brs = dscr("wb_brs", [SSMW, D], BF16).ap()
    wb_bra = dscr("wb_bra", [512, D], BF16).ap()
    wb_out = dscr("wb_out", [D, D], BF16).ap()
    wb_ff1 = dscr("wb_ff1", [D, DFF], BF16).ap()
    wb_ff2 = dscr("wb_ff2", [DFF, D], BF16).ap()
    kt_scr = dscr("kt_scr", [NH, HD, LP], BF16).ap()
    v_scr_h = dscr("v_scr", [LP, NH * HD], BF16)
    v_scr = v_scr_h.ap()
    u_scr = dscr("u_scr", [SSMW, L], BF16).ap()
    yb_scr = dscr("yb_scr", [SSMW, L], F32).ap()
    ys_scr = dscr("ys_scr", [SSMW, L], BF16).ap()
    bias_scr_h = dscr("bias_scr", [NH, 384], F32)
    bias_scr = bias_scr_h.ap()
    bias_rep_h = dscr("bias_rep", [NH, 128, 384], F32)
    bias_rep = bias_rep_h.ap()

    es = ExitStack()
    with es:
        S = Sync(nc, es)
        PE, ACT, DVE, POOL = S.pe, S.act, S.dve, S.pool
        es.enter_context(nc.Block())
        es.enter_context(nc.allow_non_contiguous_dma(reason="small parameter re-layouts"))

        uniq = [0]

        def sbt(stack, name, shape, dt=F32):
            uniq[0] += 1
            return stack.enter_context(nc.sbuf_tensor("%s_%d" % (name, uniq[0]), list(shape), dt))

        def V(fn, reads=(), writes=()):
            S.op(DVE, fn, reads, writes)

        def A(fn, reads=(), writes=()):
            S.op(ACT, fn, reads, writes)

        def G(fn, reads=(), writes=()):
            S.op(POOL, fn, reads, writes)

        def T(fn, reads=(), writes=()):
            S.op(PE, fn, reads, writes)

        def cp(i, out, in_, reads, writes):
            if i % 2 == 0:
                A(lambda: nc.scalar.copy(out, in_), reads, writes)
            else:
                V(lambda: nc.vector.tensor_copy(out=out, in_=in_), reads, writes)

        PS2 = [es.enter_context(nc.psum_tensor("ps2_%d" % i, [128, 1024], F32)) for i in range(4)]
        PB = [Buf("psb%d" % i) for i in range(8)]

        def bank(i):
            return PS2[i // 2][:, (i % 2) * 512:(i % 2) * 512 + 512]

        rr = [0]

        def ps_rr():
            i = rr[0] % 4
            rr[0] += 1
            return i

        ident_f = sbt(es, "ident_f", [128, 128], F32)
        ident_b = sbt(es, "ident_b", [128, 128], BF16)
        tri_f = sbt(es, "tri_f", [128, 128], BF16)
        tri_b = sbt(es, "tri_b", [128, 128], BF16)
        ntri_f = sbt(es, "ntri_f", [128, 128], BF16)
        ntri_b = sbt(es, "ntri_b", [128, 128], BF16)
        perm_f = sbt(es, "perm_f", [128, 128], F32)
        ones_b = sbt(es, "ones_b", [128, 128], BF16)
        sgn = sbt(es, "sgn", [128, 1], F32)
        gmask = sbt(es, "gmask", [128, 8], F32)
        iot = sbt(es, "iot", [128, 128], F32)
        gm_t = sbt(es, "gm_t", [128, 8], F32)
        BC = Buf("const")

        G(lambda: nc.gpsimd.iota(iot[:], [[1, 128]], base=0, channel_multiplier=-1,
                                 allow_small_or_imprecise_dtypes=True), writes=[BC])
        G(lambda: nc.gpsimd.iota(gm_t[:], [[16, 8]], base=0, channel_multiplier=-1,
                                 allow_small_or_imprecise_dtypes=True), writes=[BC])
        V(lambda: nc.vector.tensor_scalar(ident_f[:], iot[:], 0.0, None, ALU.is_equal), [BC], [BC])
        V(lambda: nc.vector.tensor_scalar(tri_f[:], iot[:], 0.0, None, ALU.is_ge), [BC], [BC])
        V(lambda: nc.vector.tensor_scalar(tri_b[:], iot[:], 0.0, None, ALU.is_le), [BC], [BC])
        V(lambda: nc.vector.tensor_scalar(ntri_f[:], tri_f[:], -1.0, None, ALU.mult), [BC], [BC])
        V(lambda: nc.vector.tensor_scalar(ntri_b[:], tri_b[:], -1.0, None, ALU.mult), [BC], [BC])
        V(lambda: nc.vector.tensor_scalar(perm_f[:], iot[:], 64.0, None, ALU.is_equal), [BC], [BC])
        V(lambda: nc.vector.tensor_scalar(ident_b[:], iot[:], -64.0, None, ALU.is_equal), [BC], [BC])
        V(lambda: nc.vector.tensor_tensor(perm_f[:], perm_f[:], ident_b[:], ALU.add), [BC], [BC])
        V(lambda: nc.vector.tensor_copy(out=ident_b[:], in_=ident_f[:]), [BC], [BC])
        V(lambda: nc.vector.memset(ones_b[:], 1.0), [BC], [BC])
        V(lambda: nc.vector.tensor_scalar(sgn[:], iot[:, 0:1], -63.5, None, ALU.is_le), [BC], [BC])
        V(lambda: nc.vector.tensor_scalar(sgn[:], sgn[:], 2.0, -1.0, ALU.mult, ALU.add), [BC], [BC])
        V(lambda: nc.vector.tensor_scalar(gmask[:], gm_t[:], 0.5, None, ALU.is_le), [BC], [BC])
        V(lambda: nc.vector.tensor_scalar(gm_t[:], gm_t[:], -15.5, None, ALU.is_ge), [BC], [BC])
        V(lambda: nc.vector.tensor_tensor(gmask[:], gmask[:], gm_t[:], ALU.mult), [BC], [BC])

        g1col = sbt(es, "g1col", [128, KT], F32)
        g2col = sbt(es, "g2col", [128, KT], F32)
        dcol = sbt(es, "dcol", [128, 8], F32)
        bgcol = sbt(es, "bgcol", [128, 8], F32)
        S.dma(g1col[:], norm1_g[0].rearrange("(k p) -> p k", p=128), writes=[BC])
        S.dma(g2col[:], norm2_g[0].rearrange("(k p) -> p k", p=128), writes=[BC])
        S.dma(dcol[:], ssm_d[0].rearrange("(k p) -> p k", p=128), writes=[BC])
        S.dma(bgcol[:], b_glu[0].rearrange("(k p) -> p k", p=128), writes=[BC])

        with ExitStack() as ps:
            NST = 3
            zero_b = sbt(ps, "zero_b", [128, 1536], BF16)
            V(lambda: nc.vector.memset(zero_b[:], 0.0), [BC], [BC])
            stg_f = [sbt(ps, "stgf%d" % i, [128, 2048], F32) for i in range(NST)]
            stg_b = [sbt(ps, "stgb%d" % i, [128, 2048], BF16) for i in range(NST)]
            bf_ = [Buf() for _ in range(NST)]
            bb_ = [Buf() for _ in range(NST)]
            cnt = [0]

            def conv(src, dst, rows, cols, scol=None):
                for r0 in range(0, rows, 128):
                    for c0 in range(0, cols, 2048):
                        cw = min(2048, cols - c0)
                        i = cnt[0] % NST
                        e = cnt[0] % 2
                        cnt[0] += 1
                        S.dma(stg_f[i][:, 0:cw], src[r0:r0 + 128, c0:c0 + cw], writes=[bf_[i]])
                        o, a = stg_b[i][:, 0:cw], stg_f[i][:, 0:cw]
                        if scol is not None:
                            sc = scol[:, r0 // 128:r0 // 128 + 1]
                            if e == 0:
                                V(lambda: nc.vector.tensor_scalar(o, a, sc, None, ALU.mult), [bf_[i], BC], [bb_[i]])
                            elif e == 1:
                                A(lambda: nc.scalar.activation(o, a, AF.Copy, scale=sc), [bf_[i], BC], [bb_[i]])
                            else:
                                G(lambda: nc.gpsimd.tensor_scalar(o, a, sc, None, ALU.mult), [bf_[i], BC], [bb_[i]])
                        else:
                            if e == 0:
                                V(lambda: nc.vector.tensor_copy(out=o, in_=a), [bf_[i]], [bb_[i]])
                            elif e == 1:
                                A(lambda: nc.scalar.copy(o, a), [bf_[i]], [bb_[i]])
                            else:
                                G(lambda: nc.gpsimd.tensor_copy(out=o, in_=a), [bf_[i]], [bb_[i]])
                        S.dma(dst[r0:r0 + 128, c0:c0 + cw], o, reads=[bb_[i]])

            conv(w_in[0], wb_in, D, INC, g1col)
            conv(w_glu[0], wb_glu, SSMW, SSMW)
            conv(w_brs[0], wb_brs, SSMW, D)
            conv(w_bra[0], wb_bra, 512, D)
            conv(w_out[0], wb_out, D, D)
            conv(w_ff1[0], wb_ff1, D, DFF, g2col)
            conv(w_ff2[0], wb_ff2, DFF, D)
            if not CTX:
                for h in range(NH):
                    S.dma(kt_scr[h, :, 0:HALO], zero_b[:, 0:HALO], reads=[BC])
                    S.dma(kt_scr[h, :, HALO + L:LP], zero_b[:, 0:HALO], reads=[BC])
                for r0 in list(range(0, HALO, 128)) + list(range(HALO + L, LP, 128)):
                    S.dma(v_scr[r0:r0 + 128, :], zero_b[:, :], reads=[BC])

            tab = sbt(ps, "tab33", [33, NH], F32)
            oh = sbt(ps, "oh33", [33, 3, 384], F32)
            bsb = sbt(ps, "bias_sb", [4, 3, 384], F32)
            btmp = Buf()
            BBT = Buf("bt")
            V(lambda: nc.vector.memset(tab[32:33, :], NEG), writes=[btmp])
            S.dma(tab[0:32, :], rel_bias[:, :], writes=[btmp])
            S.dma(oh[:], onehot.rearrange("g b m -> b g m"), writes=[btmp])
            for gi in range(3):
                T(lambda: nc.tensor.matmul(bank(gi)[0:4, 0:384], tab[:, gi * 4:(gi + 1) * 4], oh[:, gi, :],
                                           start=True, stop=True), [btmp], [PB[gi]])
                V(lambda: nc.vector.tensor_copy(out=bsb[:, gi, :], in_=bank(gi)[0:4, 0:384]), [PB[gi]], [btmp])
                S.dma(bias_scr[gi * 4:(gi + 1) * 4, :], bsb[:, gi, :], reads=[btmp], writes=[BBT])
            for h in range(NH):
                S.dma(bias_rep[h], bass.AP(bias_scr_h, h * 384, [[0, 128], [1, 384]]), reads=[BBT], writes=[BBT])
            if dbg:
                S.dma(dbg_out["d_bias"], bias_scr, reads=[BBT], writes=[BBT])
            S.barrier()

        def make_wring(stack):
            NW = 3
            wring = [sbt(stack, "wring%d" % i, [128, 16, 512], BF16) for i in range(NW)]
            wbuf = [Buf("w%d" % i) for i in range(NW)]
            wcnt = [0]

            def wload(src_rows, nk, c0, cw=512):
                i = wcnt[0] % NW
                wcnt[0] += 1
                S.dma(wring[i][:, 0:nk, 0:cw], src_rows.rearrange("(k p) c -> p k c", p=128)[:, :, c0:c0 + cw],
                      writes=[wbuf[i]])
                return wring[i], wbuf[i]
            return wload

        def make_norm(stack):
            xh = [sbt(stack, "xhat%d" % i, [128, D], BF16) for i in range(2)]
            ssq = sbt(stack, "ssq", [128, 4], F32)
            rstd = sbt(stack, "rstd", [128, 4], F32)
            B_xh = [Buf("xhat0"), Buf("xhat1")]
            B_st = Buf("st")

            def stats(src_tile, B_src):
                for tt in range(4):
                    A(lambda: nc.scalar.activation(xh[tt % 2][:], src_tile[:, tt, :], AF.Square, accum_out=ssq[:, tt:tt + 1]),
                      [B_src], [B_xh[tt % 2], B_st])
                A(lambda: nc.scalar.activation(rstd[:], ssq[:], AF.Sqrt, scale=1.0 / D, bias=EPS), [B_st], [B_st])
                V(lambda: nc.vector.reciprocal(rstd[:], rstd[:]), [B_st], [B_st])

            def rmsnorm_to_T(src_tile, dstT, B_src, B_dst):
                stats(src_tile, B_src)
                for tt in range(4):
                    xhat, B_xhat = xh[tt % 2], B_xh[tt % 2]
                    V(lambda: nc.vector.tensor_scalar(xhat[:], src_tile[:, tt, :], rstd[:, tt:tt + 1], None, ALU.mult),
                      [B_src, B_st], [B_xhat])
                    psv = PS2[tt % 2][:].bitcast(BF16)
                    pbs = [PB[2 * (tt % 2)], PB[2 * (tt % 2) + 1]]
                    for kt in range(KT):
                        T(lambda: nc.tensor.transpose(psv[:, kt * 128:(kt + 1) * 128], xhat[:, kt * 128:(kt + 1) * 128],
                                                      ident_b[:]), [B_xhat, BC], pbs)
                    A(lambda: nc.scalar.copy(dstT[:, :, tt * 128:(tt + 1) * 128],
                                             psv[:, 0:2048].rearrange("p (k t) -> p k t", k=KT)), pbs, [B_dst])
            return rmsnorm_to_T, stats, rstd, B_st

        def proj_ws(wload, wsrc_rows, nk, col0, n_mt, rhsT, B_rhs, evac):
            for m0 in range(0, n_mt, 4):
                nm = min(4, n_mt - m0)
                wt, wb = wload(wsrc_rows, nk, col0 + m0 * 128, nm * 128)
                for mi in range(nm):
                    bi = ps_rr()
                    for k in range(nk):
                        T(lambda: nc.tensor.matmul(bank(bi), wt[:, k, mi * 128:(mi + 1) * 128], rhsT[:, k, :],
                                                   start=(k == 0), stop=(k == nk - 1)), [wb, B_rhs], [PB[bi]])
                    evac(m0 + mi, bank(bi), PB[bi])

        with ExitStack() as ps:
            wload = make_wring(ps)
            rmsnorm_to_T, _, _, _ = make_norm(ps)
            xb = sbt(ps, "xb", [128, 4, D], F32)
            xnT = sbt(ps, "xnT", [128, KT, NB], BF16)
            kst = [sbt(ps, "kst%d" % i, [128, NB], BF16) for i in range(4)]
            B_x, B_xnT = Buf("x"), Buf("xnT")
            B_kst = [Buf() for _ in range(4)]
            kc = [0]
            NCB = CTX // NB
            jobs = [("own", b) for b in range(NBLK)] + [("ctx", b) for b in range(NCB)]
            for kind, b in jobs:
                t0 = b * NB
                if kind == "own":
                    xsrc, udst, kvoff = xs, u_scr, HALO + t0
                else:
                    xsrc, udst = xc, uc_scr
                    kvoff = (HALO + L + t0) if b < 2 else ((t0 - (CTX - HALO)) if b >= NCB - 2 else None)
                S.dma(xb[:], xsrc[t0:t0 + NB, :].rearrange("(t p) d -> p t d", p=128), writes=[B_x])
                rmsnorm_to_T(xb, xnT, B_x, B_xnT)

                def ev_u(mt, psap, psb):
                    i = kc[0] % 4
                    kc[0] += 1
                    cp(mt, kst[i][:], psap, [psb], [B_kst[i]])
                    S.dma(udst[mt * 128:(mt + 1) * 128, t0:t0 + NB], kst[i][:], reads=[B_kst[i]])
                proj_ws(wload, wb_in, KT, 0, 8, xnT, B_xnT, ev_u)
                if kvoff is None:
                    continue

                def ev_k(mt, psap, psb):
                    i = kc[0] % 4
                    kc[0] += 1
                    cp(mt, kst[i][:], psap, [psb], [B_kst[i]])
                    S.dma(kt_scr[mt, :, kvoff:kvoff + NB], kst[i][:], reads=[B_kst[i]])
                proj_ws(wload, wb_in, KT, K_OFF, NH, xnT, B_xnT, ev_k)
                for gi in range(3):
                    wt, wb = wload(wb_in, KT, V_OFF + gi * 512, 512)
                    for tt in range(4):
                        bi = ps_rr()
                        for k in range(KT):
                            T(lambda: nc.tensor.matmul(bank(bi), xnT[:, k, tt * 128:(tt + 1) * 128], wt[:, k, :],
                                                       start=(k == 0), stop=(k == KT - 1)), [wb, B_xnT], [PB[bi]])
                        i = kc[0] % 4
                        kc[0] += 1
                        cp(tt, kst[i][:], bank(bi), [PB[bi]], [B_kst[i]])
                        S.dma(v_scr[kvoff + tt * 128:kvoff + (tt + 1) * 128, gi * 512:(gi + 1) * 512], kst[i][:],
                              reads=[B_kst[i]])
            S.barrier()

        with ExitStack() as ps:
            T1re = sbt(ps, "T1re", [128, NG, NS], BF16)
            T1im = sbt(ps, "T1im", [128, NG, NS], BF16)
            T2re = sbt(ps, "T2re", [128, NG, 128], BF16)
            T2im = sbt(ps, "T2im", [128, NG, 128], BF16)
            G1 = sbt(ps, "G1", [128, NG], F32)
            G2 = sbt(ps, "G2", [128, NG], F32)
            Bmat = sbt(ps, "Bmat", [128, 8, 8, 128], BF16)
            Cm1 = sbt(ps, "Cm1", [128, NG, 16], BF16)
            Cm2 = sbt(ps, "Cm2", [128, NG, 16], BF16)
            BTAB = Buf("ssmtab")

            def cmul(o_re, o_im, x_re, x_im, y_re, y_im, t1, t2, bufs):
                f = lambda fn: S.op(DVE, fn, bufs, bufs)
                E = nc.vector
                f(lambda: E.tensor_tensor(t1, x_re, y_re, ALU.mult))
                f(lambda: E.tensor_tensor(t2, x_im, y_im, ALU.mult))
                f(lambda: E.tensor_tensor(t2, t1, t2, ALU.subtract))
                f(lambda: E.tensor_tensor(t1, x_re, y_im, ALU.mult))
                f(lambda: E.tensor_tensor(o_im, x_im, y_re, ALU.mult))
                f(lambda: E.tensor_tensor(o_im, o_im, t1, ALU.add))
                f(lambda: E.tensor_copy(out=o_re, in_=t2))

            def abar_of(ar, ai, ldt, shape, stack, tag):
                bufs = [BTAB]
                mk = lambda nm: sbt(stack, tag + nm, shape, F32)
                dt_, lr, th, mag, cs, sn, t1, t2 = (mk(n)[:] for n in ("dt", "lr", "th", "mag", "cs", "sn", "t1", "t2"))
                o = {k: mk(k)[:] for k in ("abr", "abi", "air", "aii", "fr", "fi")}
                f = lambda fn: S.op(DVE, fn, bufs, bufs)
                fa = lambda fn: S.op(ACT, fn, bufs, bufs)
                fa(lambda: nc.scalar.activation(dt_, ldt, AF.Exp))
                f(lambda: nc.vector.tensor_tensor(lr, ar, dt_, ALU.mult))
                f(lambda: nc.vector.tensor_tensor(th, ai, dt_, ALU.mult))
                f(lambda: nc.vector.tensor_copy(out=t2, in_=th))
                for jj in range(1, 8):
                    f(lambda: nc.vector.tensor_scalar(t1, th, (2 * jj - 1) * math.pi, -2.0 * math.pi, ALU.is_gt, ALU.mult))
                    f(lambda: nc.vector.tensor_tensor(t2, t2, t1, ALU.add))
                fa(lambda: nc.scalar.activation(sn, t2, AF.Sin))
                f(lambda: nc.vector.tensor_scalar(t1, t2, -1.0, None, ALU.mult))
                f(lambda: nc.vector.tensor_tensor(t1, t1, t2, ALU.max))
                f(lambda: nc.vector.tensor_scalar(t1, t1, -1.0, math.pi / 2, ALU.mult, ALU.add))
                fa(lambda: nc.scalar.activation(cs, t1, AF.Sin))
                fa(lambda: nc.scalar.activation(mag, lr, AF.Exp))
                f(lambda: nc.vector.tensor_tensor(o["abr"], mag, cs, ALU.mult))
                f(lambda: nc.vector.tensor_tensor(o["abi"], mag, sn, ALU.mult))
                fa(lambda: nc.scalar.activation(mag, lr, AF.Exp, scale=-1.0))
                f(lambda: nc.vector.tensor_tensor(o["air"], mag, cs, ALU.mult))
                f(lambda: nc.vector.tensor_tensor(o["aii"], mag, sn, ALU.mult))
                f(lambda: nc.vector.tensor_scalar(o["aii"], o["aii"], -1.0, None, ALU.mult))
                f(lambda: nc.vector.tensor_tensor(t1, ar, ar, ALU.mult))
                f(lambda: nc.vector.tensor_tensor(t2, ai, ai, ALU.mult))
                f(lambda: nc.vector.tensor_tensor(t1, t1, t2, ALU.add))
                f(lambda: nc.vector.reciprocal(t1, t1))
                f(lambda: nc.vector.tensor_scalar(cs, o["abr"], -1.0, None, ALU.add))
                f(lambda: nc.vector.tensor_tensor(t2, cs, ar, ALU.mult))
                f(lambda: nc.vector.tensor_tensor(mag, o["abi"], ai, ALU.mult))
                f(lambda: nc.vector.tensor_tensor(t2, t2, mag, ALU.add))
                f(lambda: nc.vector.tensor_tensor(o["fr"], t2, t1, ALU.mult))
                f(lambda: nc.vector.tensor_tensor(t2, o["abi"], ar, ALU.mult))
                f(lambda: nc.vector.tensor_tensor(mag, cs, ai, ALU.mult))
                f(lambda: nc.vector.tensor_tensor(t2, t2, mag, ALU.subtract))
                f(lambda: nc.vector.tensor_tensor(o["fi"], t2, t1, ALU.mult))
                return o

            def gen_ssm_tables(dr):
                rev = (dr == 1)
                bufs = [BTAB]
                f = lambda fn: S.op(DVE, fn, bufs, bufs)
                with ExitStack() as p2:
                    arT = sbt(p2, "arT", [128, NG], F32)
                    aiT = sbt(p2, "aiT", [128, NG], F32)
                    ldT = sbt(p2, "ldT", [128, NG], F32)
                    for half in (0, 64):
                        S.dma(arT[half:half + 64, :], a_re[0, dr].rearrange("g n -> n g"), writes=bufs)
                        S.dma(aiT[half:half + 64, :], a_im[0, dr].rearrange("g n -> n g"), writes=bufs)
                    S.dma(ldT[:], log_dt[0, dr].partition_broadcast(128), writes=bufs)
                    st = abar_of(arT[:], aiT[:], ldT[:], [128, NG], p2, "s_")
                    GB = 16
                    Are = sbt(p2, "Are", [128, GB, 128], F32)
                    Aim = sbt(p2, "Aim", [128, GB, 128], F32)
                    Nre = sbt(p2, "Nre", [128, GB, 128], F32)
                    Nim = sbt(p2, "Nim", [128, GB, 128], F32)
                    tA = sbt(p2, "tA", [128, GB, 64], F32)
                    tB = sbt(p2, "tB", [128, GB, 64], F32)
                    cur = [sbt(p2, "cur%d" % i, [128, GB], F32) for i in range(4)]
                    tc1 = sbt(p2, "tc1", [128, GB], F32)
                    tc2 = sbt(p2, "tc2", [128, GB], F32)

                    def sl(lo, hi):
                        return slice(128 - hi, 128 - lo) if rev else slice(lo, hi)
                    for gb in range(NG // GB):
                        gs = slice(gb * GB, (gb + 1) * GB)
                        for (Pre, Pim, b_re_, b_im_, one) in ((Are, Aim, st["abr"], st["abi"], True),
                                                               (Nre, Nim, st["air"], st["aii"], False)):
                            if one:
                                f(lambda: nc.vector.memset(Pre[:, :, sl(0, 1)], 1.0))
                                f(lambda: nc.vector.memset(Pim[:, :, sl(0, 1)], 0.0))
                            else:
                                f(lambda: nc.vector.tensor_copy(out=Pre[:, :, sl(0, 1)], in_=st["fr"][:, gs].unsqueeze(2)))
                                f(lambda: nc.vector.tensor_copy(out=Pim[:, :, sl(0, 1)], in_=st["fi"][:, gs].unsqueeze(2)))
                            f(lambda: nc.vector.tensor_copy(out=cur[0][:], in_=b_re_[:, gs]))
                            f(lambda: nc.vector.tensor_copy(out=cur[1][:], in_=b_im_[:, gs]))
                            cr, ci, nr, ni = cur[0], cur[1], cur[2], cur[3]
                            w = 1
                            while w < 128:
                                bc = lambda t: t[:].unsqueeze(2).broadcast_to([128, GB, w])
                                cmul(Pre[:, :, sl(w, 2 * w)], Pim[:, :, sl(w, 2 * w)], Pre[:, :, sl(0, w)], Pim[:, :, sl(0, w)],
                                     bc(cr), bc(ci), tA[:, :, 0:w], tB[:, :, 0:w], bufs)
                                if 2 * w < 128:
                                    cmul(nr[:], ni[:], cr[:], ci[:], cr[:], ci[:], tc1[:], tc2[:], bufs)
                                    cr, ci, nr, ni = nr, ni, cr, ci
                                w *= 2
                        f(lambda: nc.vector.tensor_copy(out=T2re[:, gs, :], in_=Are[:]))
                        f(lambda: nc.vector.tensor_copy(out=T2im[:, gs, :], in_=Aim[:]))
                        e127 = 0 if rev else 127
                        cmul(G1[:, gs], G2[:, gs], Are[:, :, e127], Aim[:, :, e127], st["abr"][:, gs], st["abi"][:, gs],
                             tc1[:], tc2[:], bufs)
                        for src_t, dst_t in ((Nre, T1re), (Nim, T1im)):
                            for q4 in range(GB // 8):
                                for gg in range(8):
                                    g_l = q4 * 8 + gg
                                    T(lambda: nc.tensor.transpose(PS2[0][:, gg * 64:(gg + 1) * 64], src_t[0:64, g_l, :],
                                                                  ident_f[0:64, 0:64]), bufs + [BC], [PB[0]])
                                g0 = gb * GB + q4 * 8
                                A(lambda: nc.scalar.copy(dst_t[:, g0:g0 + 8, :].rearrange("p g n -> p (g n)"),
                                                         PS2[0][:, 0:512]), [PB[0]], bufs)
                    f(lambda: nc.vector.tensor_scalar(G2[:], G2[:], sgn[:, 0:1], None, ALU.mult))
                with ExitStack() as p2:
                    bl = sbt(p2, "bl", [64, 2, NG, 16], F32)
                    S.dma(bl[:, 0], b_re[0, dr].rearrange("g n c -> n g c"), writes=bufs)
                    S.dma(bl[:, 1], b_im[0, dr].rearrange("g n c -> n g c"), writes=bufs)
                    bcomp = sbt(p2, "bcomp", [128, 8, 128], F32)
                    for kt in range(8):
                        for ri in range(2):
                            T(lambda: nc.tensor.transpose(PS2[0][:, ri * 64:(ri + 1) * 64],
                                                          bl[:, ri, kt * 8:(kt + 1) * 8, :].rearrange("p g c -> p (g c)"),
                                                          ident_f[0:64, 0:64]), bufs + [BC], [PB[0]])
                        A(lambda: nc.scalar.copy(bcomp[:, kt, :], PS2[0][:, 0:128]), [PB[0]], bufs)
                    for g8 in range(8):
                        f(lambda: nc.vector.tensor_scalar(Bmat[:, :, g8, :], bcomp[:], gmask[:, g8:g8 + 1], None, ALU.mult))
                    cl = sbt(p2, "cl", [128, 8, 2, NS], F32)
                    for cm, (top, bot) in ((Cm1, (c_re, c_im)), (Cm2, (c_im, c_re))):
                        S.dma(cl[:, :, 0, :], top[0, dr].rearrange("(k g) c n -> (g c) k n", g=8), writes=bufs)
                        S.dma(cl[:, :, 1, :], bot[0, dr].rearrange("(k g) c n -> (g c) k n", g=8), writes=bufs)
                        for kt in range(8):
                            T(lambda: nc.tensor.transpose(PS2[kt // 4][:, (kt % 4) * 128:(kt % 4 + 1) * 128],
                                                          cl[:, kt].rearrange("p r n -> p (r n)"), ident_f[:]),
                              bufs + [BC], [PB[0], PB[2]])
                        for hf in range(2):
                            A(lambda: nc.scalar.copy(cm[:, hf * 32:(hf + 1) * 32, :].rearrange("p g c -> p (g c)"),
                                                     PS2[hf][:, 0:512]), [PB[0], PB[2]], bufs)
                    f(lambda: nc.vector.tensor_scalar(Cm1[64:128], Cm1[64:128], -1.0, None, ALU.mult))
                    f(lambda: nc.vector.tensor_scalar(Cm2[:], Cm2[:], -1.0, None, ALU.mult))
                S.barrier()

            uTb = [sbt(ps, "uT%d" % i, [128, 8, NB], BF16) for i in range(3)]
            B_uTb = [Buf() for _ in range(3)]
            yaccs = [sbt(ps, "yacc%d" % i, [128, 8, NB], F32) for i in range(2)]
            B_yaccs = [Buf("yacc0"), Buf("yacc1")]
            NMB = 3
            M1 = [sbt(ps, "M1_%d" % i, [128, 4, 128], BF16) for i in range(NMB)]
            M2 = [sbt(ps, "M2_%d" % i, [128, 4, 128], BF16) for i in range(NMB)]
            H1 = [sbt(ps, "H1_%d" % i, [128, 4, 128], BF16) for i in range(NMB)]
            H2 = [sbt(ps, "H2_%d" % i, [128, 4, 128], BF16) for i in range(NMB)]
            Wp = [sbt(ps, "Wp_%d" % i, [128, 4, 128], BF16) for i in range(NMB)]
            B_Wp = [Buf() for _ in range(NMB)]
            B_M = [Buf() for _ in range(NMB)]
            B_H = [Buf() for _ in range(NMB)]
            B_H2 = [Buf() for _ in range(NMB)]
            ccar = [sbt(ps, "ccar%d" % i, [128, NG], F32) for i in range(2)]
            xcar = sbt(ps, "xcar", [128, NG], F32)
            tcar = sbt(ps, "tcar", [128, NG], F32)
            ytok = sbt(ps, "ytok", [128, SSMW], F32)
            B_car, B_ytok = Buf("car"), Buf("ytok")
            flg = sbt(ps, "flg", [128, 2], F32)
            if CTX:
                S.dma(flg[:], flags_in, writes=[BC])

            def ssm_run(dr, chunks):
                tri, ntri = (tri_f, ntri_f) if dr == 0 else (tri_b, ntri_b)
                last = 127 if dr == 0 else 0
                units = [(ci, j) for ci in range(len(chunks)) for j in range(16)]
                NU = len(units)
                cprev = ccar[0]

                def Bu(t):
                    ci, j = units[t]
                    ck = chunks[ci]
                    if j == 0 and ck.get("pre"):
                        ck["pre"]()
                    kt, hf = j // 2, j % 2
                    tok = slice(ck["c"] * 128, (ck["c"] + 1) * 128)
                    T(lambda: nc.tensor.matmul(bank(t % 2), ck["uT"][:, kt, tok],
                                               Bmat[:, kt, hf * 4:(hf + 1) * 4, :].rearrange("p g x -> p (g x)"),
                                               start=True, stop=True), [ck["B_uT"], BTAB], [PB[t % 2]])

                def st1(t):
                    ci, j = units[t]
                    gs = slice(j * 4, (j + 1) * 4)
                    mi = t % NMB
                    pv = bank(t % 2).rearrange("p (g r n) -> p g r n", g=4, r=2)
                    V(lambda: nc.vector.tensor_tensor(M1[mi][:].rearrange("p g (r n) -> p g r n", r=2), pv,
                                                      T1re[:, gs, :].unsqueeze(2).broadcast_to([128, 4, 2, NS]), ALU.mult),
                      [PB[t % 2], BTAB], [B_M[mi]])
                    V(lambda: nc.vector.tensor_tensor(M2[mi][:].rearrange("p g (r n) -> p g r n", r=2), pv,
                                                      T1im[:, gs, :].unsqueeze(2).broadcast_to([128, 4, 2, NS]), ALU.mult),
                      [PB[t % 2], BTAB], [B_M[mi]])

                def csum(t):
                    ci, j = units[t]
                    ck = chunks[ci]
                    mi = t % NMB
                    if ck["summary"]:
                        for g4 in range(4):
                            g = j * 4 + g4
                            o = PS2[3][:, 512 + g:512 + g + 1]
                            T(lambda: nc.tensor.matmul(o, M1[mi][:, g4, :], tri_f[:, 127:128], start=True, stop=False,
                                                       skip_group_check=True), [B_M[mi], BC], [PB[7]])
                            T(lambda: nc.tensor.matmul(o[0:64, :], M2[mi][:, g4, 64:128], ntri_f[:, 127:128], start=False,
                                                       stop=False, skip_group_check=True), [B_M[mi], BC], [PB[7]])
                            T(lambda: nc.tensor.matmul(o[64:128, :], M2[mi][:, g4, 0:64], tri_f[:, 127:128], start=False,
                                                       stop=True, skip_group_check=True), [B_M[mi], BC], [PB[7]])
                        return
                    wbk = 2 + t % 2
                    for g4 in range(4):
                        o = bank(wbk)[:, g4 * 128:(g4 + 1) * 128]
                        T(lambda: nc.tensor.matmul(o, M1[mi][:, g4, :], tri[:], start=True, stop=False,
                                                   skip_group_check=True), [B_M[mi], BC], [PB[wbk]])
                        T(lambda: nc.tensor.matmul(o[0:64, :], M2[mi][:, g4, 64:128], ntri[:], start=False, stop=False,
                                                   skip_group_check=True), [B_M[mi], BC], [PB[wbk]])
                        T(lambda: nc.tensor.matmul(o[64:128, :], M2[mi][:, g4, 0:64], tri[:], start=False, stop=True,
                                                   skip_group_check=True), [B_M[mi], BC], [PB[wbk]])

                def carry_update(ck):
                    T(lambda: nc.tensor.matmul(PS2[3][:, 640:640 + NG], perm_f[:], xcar[:], start=True, stop=True,
                                               skip_group_check=True), [B_car, BC], [PB[7]])
                    V(lambda: nc.vector.tensor_tensor(tcar[:], G2[:], PS2[3][:, 640:640 + NG], ALU.mult),
                      [PB[7], BTAB, B_car], [B_car])
                    V(lambda: nc.vector.tensor_tensor(ccar[1][:], G1[:], xcar[:], ALU.mult), [B_car, BTAB], [B_car])
                    if ck.get("scale") is not None:
                        V(lambda: nc.vector.tensor_tensor(ccar[1][:], ccar[1][:], tcar[:], ALU.add), [B_car], [B_car])
                        V(lambda: nc.vector.tensor_scalar(ccar[0][:], ccar[1][:], ck["scale"], None, ALU.mult),
                          [B_car, BC], [B_car])
                    else:
                        V(lambda: nc.vector.tensor_tensor(ccar[0][:], ccar[1][:], tcar[:], ALU.add), [B_car], [B_car])

                def st2(t):
                    ci, j = units[t]
                    ck = chunks[ci]
                    mi = t % NMB
                    if ck["summary"]:
                        if j == 15:
                            if ck["first"]:
                                V(lambda: nc.vector.tensor_copy(out=xcar[:], in_=PS2[3][:, 512:512 + NG]), [PB[7]], [B_car])
                            else:
                                V(lambda: nc.vector.tensor_tensor(xcar[:], PS2[3][:, 512:512 + NG], cprev[:], ALU.add),
                                  [PB[7], B_car], [B_car])
                            carry_update(ck)
                        return
                    wbk = 2 + t % 2
                    gs = slice(j * 4, (j + 1) * 4)
                    wv = bank(wbk).rearrange("p (g t) -> p g t", g=4)
                    if ck["first"]:
                        V(lambda: nc.vector.tensor_copy(out=xcar[:, gs], in_=wv[:, :, last]), [PB[wbk]], [B_car])
                    else:
                        V(lambda: nc.vector.tensor_tensor(xcar[:, gs], wv[:, :, last], cprev[:, gs], ALU.add),
                          [PB[wbk], B_car], [B_car])
                    for g4 in range(4):
                        g = j * 4 + g4
                        wsl = bank(wbk)[:, g4 * 128:(g4 + 1) * 128]
                        if ck["first"]:
                            A(lambda: nc.scalar.copy(Wp[mi][:, g4, :], wsl), [PB[wbk]], [B_Wp[mi]])
                        else:
                            A(lambda: nc.scalar.activation(Wp[mi][:, g4, :], wsl, AF.Identity, bias=cprev[:, g:g + 1]),
                              [PB[wbk], B_car], [B_Wp[mi]])
                    V(lambda: nc.vector.tensor_tensor(H1[mi][:], Wp[mi][:], T2re[:, gs, :], ALU.mult),
                      [B_Wp[mi], BTAB], [B_H[mi]])
                    G(lambda: nc.gpsimd.tensor_tensor(H2[mi][:], Wp[mi][:], T2im[:, gs, :], ALU.mult),
                      [B_Wp[mi], BTAB], [B_H2[mi]])
                    if j == 15:
                        carry_update(ck)

                def cproj(t):
                    ci, j = units[t]
                    ck = chunks[ci]
                    if ck["summary"]:
                        return
                    mi = t % NMB
                    for g4 in range(4):
                        g = j * 4 + g4
                        o = PS2[2][:, g * 16:(g + 1) * 16]
                        T(lambda: nc.tensor.matmul(o, H1[mi][:, g4, :], Cm1[:, g, :], start=True, stop=False,
                                                   skip_group_check=True), [B_H[mi], BTAB], [PB[4 + g // 32]])
                        T(lambda: nc.tensor.matmul(o, H2[mi][:, g4, :], Cm2[:, g, :], start=False, stop=True,
                                                   skip_group_check=True), [B_H2[mi], BTAB], [PB[4 + g // 32]])
                    if j == 15:
                        tok = slice(ck["c"] * 128, (ck["c"] + 1) * 128)
                        A(lambda: nc.scalar.copy(ytok[:], PS2[2][:]), [PB[4], PB[5]], [B_ytok])
                        for hf in range(2):
                            for k in range(4):
                                kk = hf * 4 + k
                                T(lambda: nc.tensor.transpose(bank(6)[:, k * 128:(k + 1) * 128],
                                                              ytok[:, kk * 128:(kk + 1) * 128], ident_f[:]),
                                  [B_ytok, BC], [PB[6]])
                            A(lambda: nc.scalar.copy(ck["yacc"][:, hf * 4:(hf + 1) * 4, tok],
                                                     bank(6).rearrange("p (k t) -> p k t", k=4)), [PB[6]], [ck["B_yacc"]])
                        if ck.get("post"):
                            ck["post"]()

                for t in range(-1, NU + 1):
                    if t + 1 < NU:
                        Bu(t + 1)
                        st1(t + 1)
                    if 0 <= t < NU:
                        csum(t)
                    if 0 <= t - 1:
                        cproj(t - 1)
                    if 0 <= t < NU:
                        st2(t)

            def mk_chunks(dr, own_posts):
                chunks = []
                ub = [0]
                nctx = CTX // NB
                order = range(nctx - 1, -1, -1) if dr == 1 else range(nctx)
                corder = range(3, -1, -1) if dr == 1 else range(4)
                for ib, b in enumerate(order):
                    ui = ub[0] % 3
                    ub[0] += 1

                    def pre(ui=ui, b=b):
                        S.dma(uTb[ui][:], uc_scr[:, b * NB:(b + 1) * NB].rearrange("(k p) t -> p k t", p=128),
                              writes=[B_uTb[ui]])
                    for ic, c in enumerate(corder):
                        lastc = (ib == nctx - 1 and ic == 3)
                        chunks.append(dict(uT=uTb[ui], B_uT=B_uTb[ui], c=c, summary=True, first=(ib == 0 and ic == 0),
                                           pre=pre if ic == 0 else None,
                                           scale=(flg[:, 1:2] if dr == 1 else flg[:, 0:1]) if lastc else None))
                border = range(NBLK - 1, -1, -1) if dr == 1 else range(NBLK)
                for ib, b in enumerate(border):
                    ui = ub[0] % 3
                    ub[0] += 1
                    yi = ib % 2

                    def pre(ui=ui, b=b):
                        S.dma(uTb[ui][:], u_scr[:, b * NB:(b + 1) * NB].rearrange("(k p) t -> p k t", p=128),
                              writes=[B_uTb[ui]])
                    for ic, c in enumerate(corder):
                        post = None
                        if ic == 3:
                            post = (lambda b=b, ui=ui, yi=yi: own_posts(b, ui, yi))
                        chunks.append(dict(uT=uTb[ui], B_uT=B_uTb[ui], c=c, summary=False,
                                           first=(ib == 0 and ic == 0 and not CTX), pre=pre if ic == 0 else None,
                                           post=post, yacc=yaccs[yi], B_yacc=B_yaccs[yi]))
                return chunks

            gen_ssm_tables(1)

            def post_bwd(b, ui, yi):
                S.dma(yb_scr[:, b * NB:(b + 1) * NB].rearrange("(k p) t -> p k t", p=128), yaccs[yi][:],
                      reads=[B_yaccs[yi]])
            ssm_run(1, mk_chunks(1, post_bwd))
            S.barrier()

            gen_ssm_tables(0)
            wglu = sbt(ps, "wglu", [128, 8, SSMW], BF16)
            B_wglu = Buf()
            S.dma(wglu[:], wb_glu.rearrange("(k p) c -> p k c", p=128), writes=[B_wglu])
            ybt = [sbt(ps, "ybt%d" % i, [128, NB], F32) for i in range(2)]
            B_ybt = [Buf() for _ in range(2)]
            gt = [sbt(ps, "gt%d" % i, [128, NB], F32) for i in range(2)]
            B_gt = [Buf() for _ in range(2)]
            zT = sbt(ps, "zT", [128, 8, NB], BF16)
            ysT = sbt(ps, "ysT", [128, 8, NB], BF16)
            sgl = [sbt(ps, "sgl%d" % i, [128, NB], BF16) for i in range(2)]
            B_sgl = [Buf() for _ in range(2)]
            B_z, B_ys = Buf("z"), Buf("ys")
            yc = [0]

            def post_fwd(b, ui, yi):
                t0 = b * NB
                yacc, B_yacc = yaccs[yi], B_yaccs[yi]
                for k in range(8):
                    i = yc[0] % 2
                    yc[0] += 1
                    S.dma(ybt[i][:], yb_scr[k * 128:(k + 1) * 128, t0:t0 + NB], writes=[B_ybt[i]])
                    yk = yacc[:, k, :]
                    G(lambda: nc.gpsimd.tensor_tensor(yk, yk, ybt[i][:], ALU.add), [B_yacc, B_ybt[i]], [B_yacc])
                    G(lambda: nc.gpsimd.tensor_scalar(gt[0][:], uTb[ui][:, k, :], dcol[:, k:k + 1], None, ALU.mult),
                      [B_uTb[ui], BC], [B_gt[0]])
                    G(lambda: nc.gpsimd.tensor_tensor(yk, yk, gt[0][:], ALU.add), [B_yacc, B_gt[0]], [B_yacc])
                    if dbg:
                        S.dma(dbg_out["d_yb"][k * 128:(k + 1) * 128, t0:t0 + NB], ybt[i][:], reads=[B_ybt[i]])
                    G(lambda: nc.gpsimd.tensor_tensor(gt[0][:], yk, yk, ALU.mult), [B_yacc], [B_gt[0]])
                    G(lambda: nc.gpsimd.tensor_scalar(gt[0][:], gt[0][:], 0.044715, 1.0, ALU.mult, ALU.add), [B_gt[0]], [B_gt[0]])
                    G(lambda: nc.gpsimd.tensor_tensor(gt[0][:], gt[0][:], yk, ALU.mult), [B_gt[0], B_yacc], [B_gt[0]])
                    A(lambda: nc.scalar.activation(gt[1][:], gt[0][:], AF.Sigmoid, scale=GELU_C), [B_gt[0]], [B_gt[1]])
                    G(lambda: nc.gpsimd.tensor_tensor(zT[:, k, :], yk, gt[1][:], ALU.mult), [B_gt[1], B_yacc], [B_z])
                for mt in range(8):
                    for k in range(8):
                        T(lambda: nc.tensor.matmul(bank(7), wglu[:, k, mt * 128:(mt + 1) * 128], zT[:, k, :],
                                                   start=(k == 0), stop=(k == 7), skip_group_check=True),
                          [B_wglu, B_z], [PB[7]])
                    i = mt % 2
                    A(lambda: nc.scalar.activation(sgl[i][:], bank(7), AF.Sigmoid, bias=bgcol[:, mt:mt + 1]),
                      [PB[7], BC], [B_sgl[i]])
                    G(lambda: nc.gpsimd.tensor_tensor(ysT[:, mt, :], zT[:, mt, :], sgl[i][:], ALU.mult),
                      [B_z, B_sgl[i]], [B_ys])
                S.dma(ys_scr[:, t0:t0 + NB].rearrange("(k p) t -> p k t", p=128), ysT[:], reads=[B_ys])
                if dbg:
                    for k in range(8):
                        G(lambda: nc.gpsimd.tensor_copy(out=gt[0][:], in_=ysT[:, k, :]), [B_ys], [B_gt[0]])
                        S.dma(dbg_out["d_ys"][k * 128:(k + 1) * 128, t0:t0 + NB], gt[0][:], reads=[B_gt[0]])
            ssm_run(0, mk_chunks(0, post_fwd))
            S.barrier()

        with ExitStack() as ps:
            wload = make_wring(ps)
            rmsnorm_to_T, stats, rstd, B_st = make_norm(ps)
            gfin = sbt(ps, "gfin", [128, D], F32)
            S.dma(gfin[:], final_g.partition_broadcast(128), writes=[BC])
            NQs = (128, 128, 32)
            bt_hi = sbt(ps, "bt_hi", [128, 24 * 128], BF16)
            bt_lo = sbt(ps, "bt_lo", [128, 24 * 128], BF16)
            BT = {}
            with ExitStack() as p2:
                bst = sbt(p2, "bst", [128, 24 * 128], F32)
                V(lambda: nc.vector.memset(bst[:], 0.0), writes=[BBT])
                for gi in range(3):
                    nq = NQs[gi]
                    for j in range(4):
                        for k2 in range(2):
                            col = ((gi * 4 + j) * 2 + k2) * 128
                            src = bass.AP(bias_rep_h, (gi * 4 + j) * 128 * 384 + 255 - 128 * k2, [[383, 128], [1, nq]])
                            S.dma(bst[:, col:col + nq], src, reads=[BBT], writes=[BBT])
                            BT[(gi, j, k2)] = (bt_hi[:, col:col + nq], bt_lo[:, col:col + nq])
                V(lambda: nc.vector.tensor_copy(out=bt_hi[:], in_=bst[:]), [BBT], [BBT])
                V(lambda: nc.vector.tensor_tensor(bst[:], bst[:], bt_hi[:], ALU.subtract), [BBT], [BBT])
                V(lambda: nc.vector.tensor_copy(out=bt_lo[:], in_=bst[:]), [BBT], [BBT])
                S.barrier()
            xb = sbt(ps, "xb", [128, 4, D], F32)
            xnT = sbt(ps, "xnT", [128, KT, NB], BF16)
            ysT = sbt(ps, "ysT2", [128, 8, NB], BF16)
            qT = sbt(ps, "qT", [128, NH, NB], BF16)
            mixT = sbt(ps, "mixT", [128, KT, NB], BF16)
            yaT = sbt(ps, "yaT", [128, 4, NB], BF16)
            sg = [sbt(ps, "sg%d" % i, [128, NB], BF16) for i in range(2)]
            sgt = [sbt(ps, "sgt%d" % i, [128, NB], BF16) for i in range(2)]
            B_x, B_xnT, B_ys, B_q, B_mixT, B_ya = (Buf(n) for n in "x xnT ys q mixT ya".split())
            B_sg = [Buf() for _ in range(2)]
            B_sgt = [Buf() for _ in range(2)]
            sgc = [0]
            kwin, B_kwin = [], []
            for gi_ in range(3):
                for i_ in range(2):
                    kwin.append(sbt(ps, "kwin%d_%d" % (gi_, i_), [128, NB + 128 * DIL[gi_]], BF16))
                    B_kwin.append(Buf())
            NVW, NPT = 5, 4
            vwin = [sbt(ps, "vwin%d" % i, [128, 2, 256], BF16) for i in range(NVW)]
            B_vwin = [Buf() for _ in range(NVW)]
            kmc = [sbt(ps, "kmc%d" % i, [128, 2], F32) for i in range(NVW)]
            PT = [sbt(ps, "PT%d" % i, [128, 2, 128], BF16) for i in range(NPT)]
            B_PT = [Buf() for _ in range(NPT)]
            rden = sbt(ps, "rden", [128, NB], F32)
            B_rden = Buf()
            print("pass2b sbuf remaining before actq:", nc.sbuf_bytes_remaining)
            actq = sbt(ps, "actq", [128, 8, NB], BF16)
            relu_t = [sbt(ps, "relu%d" % i, [128, NB], BF16) for i in range(2)]
            B_relu = [Buf() for _ in range(2)]
            B_actq = Buf()
            kwc, vwc, ptc = [0], [0], [0]

            for b in range(NBLK):
                t0 = b * NB
                S.dma(xb[:], xs[t0:t0 + NB, :].rearrange("(t p) d -> p t d", p=128), writes=[B_x])
                S.dma(ysT[:], ys_scr[:, t0:t0 + NB].rearrange("(k p) t -> p k t", p=128), writes=[B_ys])
                rmsnorm_to_T(xb, xnT, B_x, B_xnT)
                for m0 in range(0, KT, 4):
                    wtg, wbg = wload(wb_in, KT, G_OFF + m0 * 128, 512)
                    wtb, wbb = wload(wb_brs, 8, m0 * 128, 512)
                    for mi in range(4):
                        mt = m0 + mi
                        bi = ps_rr()
                        for k in range(KT):
                            T(lambda: nc.tensor.matmul(bank(bi), wtg[:, k, mi * 128:(mi + 1) * 128], xnT[:, k, :],
                                                       start=(k == 0), stop=(k == KT - 1)), [wbg, B_xnT], [PB[bi]])
                        i = sgc[0] % 2
                        sgc[0] += 1
                        A(lambda: nc.scalar.activation(sg[i][:], bank(bi), AF.Sigmoid), [PB[bi]], [B_sg[i]])
                        bj = ps_rr()
                        for k in range(8):
                            T(lambda: nc.tensor.matmul(bank(bj), wtb[:, k, mi * 128:(mi + 1) * 128], ysT[:, k, :],
                                                       start=(k == 0), stop=(k == 7)), [wbb, B_ys], [PB[bj]])
                        V(lambda: nc.vector.tensor_tensor(mixT[:, mt, :], bank(bj), sg[i][:], ALU.mult),
                          [PB[bj], B_sg[i]], [B_mixT])

                def ev_q(mt, psap, psb):
                    A(lambda: nc.scalar.activation(qT[:, mt, :], psap, AF.Copy, scale=HD ** -0.5), [psb], [B_q])
                proj_ws(wload, wb_in, KT, Q_OFF, NH, xnT, B_xnT, ev_q)

                for rnd in range(2):
                    first_mm = {0: True, 1: True}
                    items = []
                    unit_list = []
                    for gi in range(3):
                        for u in range(4 if gi == 0 else DIL[gi]):
                            unit_list.append((gi, u))
                            for jj in range(2):
                                items.append((gi, u, jj, len(unit_list) - 1))
                    uinfo = {}

                    def unit_geom(gi, u):
                        d = DIL[gi]
                        reach = 64 * d
                        if gi == 0:
                            return slice(u * 128, (u + 1) * 128), u * 128, 1, t0 + u * 128 - 64
                        return slice(u, NB, d), u, d, t0 + u - reach

                    def load_unit(ui_):
                        gi, u = unit_list[ui_]
                        nq = NQs[gi]
                        qsl, kcol0, kstep, tstart = unit_geom(gi, u)
                        if u == 0:
                            reach = 64 * DIL[gi]
                            for jj in range(2):
                                h = gi * 4 + rnd * 2 + jj
                                i = gi * 2 + jj
                                S.dma(kwin[i][:, 0:NB + 2 * reach], kt_scr[h, :, HALO + t0 - reach:HALO + t0 + NB + reach],
                                      writes=[B_kwin[i]])
                                kwmap[(gi, jj)] = i
                        vi = vwc[0] % NVW
                        vwc[0] += 1
                        c0 = gi * 512 + rnd * 256
                        r0 = HALO + tstart
                        RS = NH * HD
                        if nq == 128:
                            src = bass.AP(v_scr_h, r0 * RS + c0, [[kstep * RS, 128], [128 * kstep * RS, 2], [1, 256]])
                            S.dma(vwin[vi][:, :, :], src, writes=[B_vwin[vi]])
                            srcm = bass.AP(kmask_h, r0, [[kstep, 128], [128 * kstep, 2]])
                            S.dma(kmc[vi][:, :], srcm, writes=[B_vwin[vi]])
                        else:
                            for k2, n2 in ((0, 128), (1, nq)):
                                rr0 = r0 + k2 * 128 * kstep
                                src = bass.AP(v_scr_h, rr0 * RS + c0, [[kstep * RS, n2], [1, 256]])
                                S.dma(vwin[vi][0:n2, k2, :], src, writes=[B_vwin[vi]])
                                srcm = bass.AP(kmask_h, rr0, [[kstep, n2], [1, 1]])
                                S.dma(kmc[vi][0:n2, k2:k2 + 1], srcm, writes=[B_vwin[vi]])
                        uinfo[ui_] = vi

                    def scores(it):
                        gi, u, jj, ui_ = it
                        nq = NQs[gi]
                        nk2 = (128, nq)
                        qsl, kcol0, kstep, tstart = unit_geom(gi, u)
                        vi = uinfo[ui_]
                        j = rnd * 2 + jj
                        h = gi * 4 + j
                        ki = kwmap[(gi, jj)]
                        pi_ = ptc[0] % NPT
                        ptc[0] += 1
                        sb_i = ps_rr()
                        for k2 in range(2):
                            n2 = nk2[k2]
                            kc0 = kcol0 + k2 * 128 * kstep
                            ksl = slice(kc0, kc0 + (n2 - 1) * kstep + 1, kstep)
                            so = bank(sb_i)[0:n2, k2 * 128:k2 * 128 + nq]
                            bhi, blo = BT[(gi, j, k2)]
                            T(lambda: nc.tensor.matmul(so, kwin[ki][:, ksl], qT[:, h, qsl], start=True, stop=False,
                                                       skip_group_check=True), [B_kwin[ki], B_q], [PB[sb_i]])
                            T(lambda: nc.tensor.matmul(so, ident_b[0:n2, 0:n2], bhi[0:n2, :], start=False, stop=False,
                                                       skip_group_check=True), [BBT, BC], [PB[sb_i]])
                            T(lambda: nc.tensor.matmul(so, ident_b[0:n2, 0:n2], blo[0:n2, :], start=False, stop=True,
                                                       skip_group_check=True), [BBT, BC], [PB[sb_i]])
                            A(lambda: nc.scalar.activation(PT[pi_][0:n2, k2, 0:nq], so, AF.Exp,
                                                           bias=kmc[vi][0:n2, k2:k2 + 1]),
                              [PB[sb_i], B_vwin[vi]], [B_PT[pi_]])
                        return pi_

                    def pv(it, pi_):
                        gi, u, jj, ui_ = it
                        nq = NQs[gi]
                        nk2 = (128, nq)
                        qsl, kcol0, kstep, tstart = unit_geom(gi, u)
                        vi = uinfo[ui_]
                        ob, db = 4 + jj, 6 + jj
                        for k2 in range(2):
                            n2 = nk2[k2]
                            T(lambda: nc.tensor.matmul(bank(ob)[:, qsl], vwin[vi][0:n2, k2, jj * 128:(jj + 1) * 128],
                                                       PT[pi_][0:n2, k2, 0:nq], start=first_mm[jj], stop=False,
                                                       skip_group_check=True),
                              [B_vwin[vi], B_PT[pi_]], [PB[ob]])
                            T(lambda: nc.tensor.matmul(bank(db)[:, qsl], ones_b[0:n2, :], PT[pi_][0:n2, k2, 0:nq],
                                                       start=first_mm[jj], stop=False, skip_group_check=True),
                              [BC, B_PT[pi_]], [PB[db]])
                            first_mm[jj] = False

                    kwmap = {}
                    AHEAD = 3
                    for ui_ in range(min(AHEAD, len(unit_list))):
                        load_unit(ui_)
                    prev = None
                    for idx, it in enumerate(items):
                        if it[2] == 0 and it[3] + AHEAD < len(unit_list):
                            load_unit(it[3] + AHEAD)
                        pi_ = scores(it)
                        if prev is not None:
                            pv(*prev)
                        prev = (it, pi_)
                    pv(*prev)
                    for jj in range(2):
                        j = rnd * 2 + jj
                        V(lambda: nc.vector.tensor_scalar(rden[:], bank(6 + jj), 1e-30, None, ALU.add), [PB[6 + jj]], [B_rden])
                        V(lambda: nc.vector.reciprocal(rden[:], rden[:]), [B_rden], [B_rden])
                        V(lambda: nc.vector.tensor_tensor(yaT[:, j, :], bank(4 + jj), rden[:], ALU.mult),
                          [PB[4 + jj], B_rden], [B_ya])
                if dbg:
                    for j in range(4):
                        V(lambda: nc.vector.tensor_copy(out=rden[:], in_=yaT[:, j, :]), [B_ya], [B_rden])
                        S.dma(dbg_out["d_ya"][j * 128:(j + 1) * 128, t0:t0 + NB], rden[:], reads=[B_rden])

                for m0 in range(0, KT, 4):
                    wtg, wbg = wload(wb_in, KT, G_OFF + D + m0 * 128, 512)
                    wtb, wbb = wload(wb_bra, 4, m0 * 128, 512)
                    for mi in range(4):
                        mt = m0 + mi
                        bi = ps_rr()
                        for k in range(KT):
                            T(lambda: nc.tensor.matmul(bank(bi), wtg[:, k, mi * 128:(mi + 1) * 128], xnT[:, k, :],
                                                       start=(k == 0), stop=(k == KT - 1)), [wbg, B_xnT], [PB[bi]])
                        i = sgc[0] % 2
                        sgc[0] += 1
                        A(lambda: nc.scalar.activation(sg[i][:], bank(bi), AF.Sigmoid), [PB[bi]], [B_sg[i]])
                        bj = ps_rr()
                        for k in range(4):
                            T(lambda: nc.tensor.matmul(bank(bj), wtb[:, k, mi * 128:(mi + 1) * 128], yaT[:, k, :],
                                                       start=(k == 0), stop=(k == 3)), [wbb, B_ya], [PB[bj]])
                        V(lambda: nc.vector.tensor_tensor(sgt[i][:], bank(bj), sg[i][:], ALU.mult),
                          [PB[bj], B_sg[i]], [B_sgt[i]])
                        G(lambda: nc.gpsimd.tensor_tensor(mixT[:, mt, :], sgt[i][:], mixT[:, mt, :], ALU.add),
                          [B_sgt[i], B_mixT], [B_mixT])

                for cg in range(4):
                    wt, wb = wload(wb_out, KT, cg * 512, 512)
                    for tt in range(4):
                        bi = ps_rr()
                        for k in range(KT):
                            T(lambda: nc.tensor.matmul(bank(bi), mixT[:, k, tt * 128:(tt + 1) * 128], wt[:, k, :],
                                                       start=(k == 0), stop=(k == KT - 1)), [wb, B_mixT], [PB[bi]])
                        xsl = xb[:, tt, cg * 512:(cg + 1) * 512]
                        V(lambda: nc.vector.tensor_tensor(xsl, bank(bi), xsl, ALU.add), [PB[bi], B_x], [B_x])
                rmsnorm_to_T(xb, xnT, B_x, B_xnT)
                for qf in range(8):
                    def ev_ff1(mt, psap, psb):
                        i = sgc[0] % 2
                        sgc[0] += 1
                        A(lambda: nc.scalar.activation(relu_t[i][:], psap, AF.Relu), [psb], [B_relu[i]])
                        V(lambda: nc.vector.scalar_tensor_tensor(actq[:, mt, :], psap, 0.0, relu_t[i][:], ALU.max, ALU.mult),
                          [psb, B_relu[i]], [B_actq])
                    proj_ws(wload, wb_ff1, KT, qf * 1024, 8, xnT, B_xnT, ev_ff1)
                    for cg in range(4):
                        wt, wb = wload(wb_ff2[qf * 1024:(qf + 1) * 1024, :], 8, cg * 512, 512)
                        for tt in range(4):
                            bi = ps_rr()
                            for k in range(8):
                                T(lambda: nc.tensor.matmul(bank(bi), actq[:, k, tt * 128:(tt + 1) * 128], wt[:, k, :],
                                                           start=(k == 0), stop=(k == 7)), [wb, B_actq], [PB[bi]])
                            xsl = xb[:, tt, cg * 512:(cg + 1) * 512]
                            V(lambda: nc.vector.tensor_tensor(xsl, bank(bi), xsl, ALU.add), [PB[bi], B_x], [B_x])
                stats(xb, B_x)
                for tt in range(4):
                    V(lambda: nc.vector.scalar_tensor_tensor(xb[:, tt, :], xb[:, tt, :], rstd[:, tt:tt + 1], gfin[:],
                                                             ALU.mult, ALU.mult), [B_x, B_st, BC], [B_x])
                S.dma(ys[t0:t0 + NB, :].rearrange("(t p) d -> p t d", p=128), xb[:], reads=[B_x])
            S.barrier()
    return nc


_PROG = {}


def _get_prog(L, dbg=False, CTX=0):
    key = (L, dbg, CTX)
    if key not in _PROG:
        _PROG[key] = build_program(L, dbg, CTX)
    return _PROG[key]


def _kmask(L, nvalid):
    m = np.full((L + 2 * HALO, 1), NEG, np.float32)
    m[HALO:HALO + nvalid] = 0.0
    return m


def kernel(**inputs):
    f32 = lambda a: np.ascontiguousarray(np.asarray(a, dtype=np.float32))
    xp = f32(inputs["x_prompt"])
    xsm = f32(inputs["x_sample"])
    B, SL, _ = xp.shape
    LS = xsm.shape[1]
    assert xsm.shape[0] == 1 and LS == 2 * SL and B + 2 <= 8
    L = SL
    shared = {}
    for k in ("norm1_g", "w_in", "ssm_a_re", "ssm_a_im", "ssm_log_dt", "ssm_b_re", "ssm_b_im", "ssm_c_re",
              "ssm_c_im", "ssm_d", "w_glu", "b_glu", "w_br_ssm", "w_br_attn", "w_out", "norm2_g", "w_ff1",
              "w_ff2", "rel_bias", "final_g"):
        shared[k] = f32(inputs[k])
    shared["onehot"] = _t5_onehot()
    zeros_ctx = np.zeros((L, D), np.float32)

    def km(left, right):
        m = np.full((L + 2 * HALO, 1), NEG, np.float32)
        m[HALO:HALO + L] = 0.0
        if left:
            m[:HALO] = 0.0
        if right:
            m[HALO + L:] = 0.0
        return m

    def fl(a, b):
        f = np.zeros((128, 2), np.float32)
        f[:, 0] = a
        f[:, 1] = b
        return f
    in_maps = []
    for c in range(8):
        m = dict(shared)
        if c < B:
            m["xs"], m["xc"], m["kmask"], m["flags"] = xp[c], zeros_ctx, km(False, False), fl(0.0, 0.0)
        elif c == B:
            m["xs"], m["xc"], m["kmask"], m["flags"] = xsm[0, :L], xsm[0, L:], km(False, True), fl(0.0, 1.0)
        elif c == B + 1:
            m["xs"], m["xc"], m["kmask"], m["flags"] = xsm[0, L:], xsm[0, :L], km(True, False), fl(1.0, 0.0)
        else:
            m["xs"], m["xc"], m["kmask"], m["flags"] = zeros_ctx, zeros_ctx, km(False, False), fl(0.0, 0.0)
        in_maps.append(m)
    nc = _get_prog(L, CTX=L)
    res = run_bass_kernel_spmd(nc, in_maps, core_ids=list(range(8)))
    y_prompt = np.stack([np.asarray(res.results[c]["ys"], dtype=np.float32) for c in range(B)], axis=0)
    y_sample = np.concatenate([np.asarray(res.results[B]["ys"], dtype=np.float32),
                               np.asarray(res.results[B + 1]["ys"], dtype=np.float32)], axis=0)[None]
    return (y_prompt, y_sample)
```

```python
import math
from contextlib import ExitStack

import numpy as np
import concourse.bass as bass
import concourse.mybir as mybir
from concourse.bass_utils import run_bass_kernel_spmd

F32 = mybir.dt.float32
BF16 = mybir.dt.bfloat16
AF = mybir.ActivationFunctionType
ALU = mybir.AluOpType

D = 2048
KT = 16
SSMW = 1024
NG = 64
NS = 64
NH = 12
HD = 128
INC = 9728
Q_OFF, K_OFF, V_OFF, G_OFF = 1024, 2560, 4096, 5632
DFF = 8192
NB = 512
HALO = 1024
NEG = -30000.0
EPS = 1e-6
DIL = (1, 4, 16)
GELU_C = 2.0 * math.sqrt(2.0 / math.pi)


class Buf:
    __slots__ = ("w", "r", "name")

    def __init__(self, name=""):
        self.w = {}
        self.r = {}
        self.name = name


class Eng:
    def __init__(self, eng, sem, is_pe=False):
        self.eng = eng
        self.sem = sem
        self.cnt = 0
        self.known = {}
        self.is_pe = is_pe


class Sync:
    NR = 16

    def __init__(self, nc, es):
        self.nc = nc
        self.sems = {}

        def mk(name):
            s = es.enter_context(nc.semaphore(name))
            self.sems[id(s)] = s
            return s

        self.pe = Eng(nc.tensor, mk("s_pe"), is_pe=True)
        self.act = Eng(nc.scalar, mk("s_act"))
        self.dve = Eng(nc.vector, mk("s_dve"))
        self.pool = Eng(nc.gpsimd, mk("s_pool"))
        self.sp = Eng(nc.sync, mk("s_sp"))
        self.ring = [mk("s_dma%d" % i) for i in range(self.NR)]
        self.ring_cnt = [0] * self.NR
        self.n_dma = 0

    def _waits(self, E, reads, writes):
        need = {}
        for b in reads:
            for k, v in b.w.items():
                if need.get(k, 0) < v:
                    need[k] = v
        for b in writes:
            for k, v in b.w.items():
                if need.get(k, 0) < v:
                    need[k] = v
            for k, v in b.r.items():
                if need.get(k, 0) < v:
                    need[k] = v
        for k, v in need.items():
            if E.is_pe and k == id(E.sem):
                continue
            if E.known.get(k, 0) >= v:
                continue
            E.eng.wait_ge(self.sems[k], v)
            E.known[k] = v

    def op(self, E, fn, reads=(), writes=()):
        self._waits(E, reads, writes)
        inst = fn()
        E.cnt += 1
        inst.then_inc(E.sem, 1)
        k = id(E.sem)
        for b in writes:
            b.w = {k: E.cnt}
            b.r = {}
        for b in reads:
            if b.r.get(k, 0) < E.cnt:
                b.r[k] = E.cnt

    def dma(self, out, in_, reads=(), writes=(), q=None, **kw):
        E = q if q is not None else self.sp
        i = self.n_dma % self.NR
        self.n_dma += 1
        sem = self.ring[i]
        k = id(sem)
        prev = self.ring_cnt[i]
        if prev > 0 and E.known.get(k, 0) < prev:
            E.eng.wait_ge(sem, prev)
            E.known[k] = prev
        self._waits(E, reads, writes)
        E.eng.dma_start(out=out, in_=in_, **kw).then_inc(sem, 16)
        self.ring_cnt[i] = prev + 16
        for b in writes:
            b.w = {k: prev + 16}
            b.r = {}
        for b in reads:
            if b.r.get(k, 0) < prev + 16:
                b.r[k] = prev + 16

    def barrier(self):
        engs = (self.pe, self.act, self.dve, self.pool, self.sp)
        for E in engs:
            for X in engs:
                if X is E or X.cnt == 0:
                    continue
                k = id(X.sem)
                if E.known.get(k, 0) < X.cnt:
                    E.eng.wait_ge(X.sem, X.cnt)
                    E.known[k] = X.cnt
            for i in range(self.NR):
                c = self.ring_cnt[i]
                k = id(self.ring[i])
                if c > 0 and E.known.get(k, 0) < c:
                    E.eng.wait_ge(self.ring[i], c)
                    E.known[k] = c


def _t5_onehot():
    oh = np.zeros((3, 33, 384), np.float32)
    for gi, d in enumerate(DIL):
        for m in range(384):
            rel = 191 - m
            if abs(rel) > 64:
                oh[gi, 32, m] = 1.0
                continue
            r = rel * d
            ret = 16 if r > 0 else 0
            n = abs(r)
            nf = np.float32(max(n, 1))
            large = 8 + int(np.float32(np.log(nf / np.float32(8.0)) / np.float32(math.log(128.0)) * np.float32(8.0)))
            large = min(large, 15)
            b = ret + (n if n < 8 else large)
            oh[gi, b, m] = 1.0
    return oh


def build_program(L, dbg=False, CTX=0):
    assert L % NB == 0 and CTX % NB == 0 and (CTX == 0 or CTX >= 2 * HALO)
    NBLK = L // NB
    LP = L + 2 * HALO
    nc = bass.Bass("TRN2", target_bir_lowering=False)

    def din(name, shape, dt=F32):
        return nc.dram_tensor(name, list(shape), dt, kind="ExternalInput")

    def dscr_early(name, shape, dt):
        return nc.dram_tensor(name, list(shape), dt, kind="Internal")

    xs = din("xs", [L, D]).ap()
    kmask_h = din("kmask", [LP, 1])
    if CTX:
        xc = din("xc", [CTX, D]).ap()
        flags_in = din("flags", [128, 2]).ap()
        uc_scr = dscr_early("uc_scr", [SSMW, CTX], BF16).ap()
    norm1_g = din("norm1_g", [1, D]).ap()
    w_in = din("w_in", [1, D, INC]).ap()
    a_re = din("ssm_a_re", [1, 2, NG, NS]).ap()
    a_im = din("ssm_a_im", [1, 2, NG, NS]).ap()
    log_dt = din("ssm_log_dt", [1, 2, NG]).ap()
    b_re = din("ssm_b_re", [1, 2, NG, NS, 16]).ap()
    b_im = din("ssm_b_im", [1, 2, NG, NS, 16]).ap()
    c_re = din("ssm_c_re", [1, 2, NG, 16, NS]).ap()
    c_im = din("ssm_c_im", [1, 2, NG, 16, NS]).ap()
    ssm_d = din("ssm_d", [1, SSMW]).ap()
    w_glu = din("w_glu", [1, SSMW, SSMW]).ap()
    b_glu = din("b_glu", [1, SSMW]).ap()
    w_brs = din("w_br_ssm", [1, SSMW, D]).ap()
    w_bra = din("w_br_attn", [1, 512, D]).ap()
    w_out = din("w_out", [1, D, D]).ap()
    norm2_g = din("norm2_g", [1, D]).ap()
    w_ff1 = din("w_ff1", [1, D, DFF]).ap()
    w_ff2 = din("w_ff2", [1, DFF, D]).ap()
    rel_bias = din("rel_bias", [32, NH]).ap()
    final_g = din("final_g", [D]).ap()
    onehot = din("onehot", [3, 33, 384]).ap()
    ys = nc.dram_tensor("ys", [L, D], F32, kind="ExternalOutput").ap()
    dbg_out = {}
    if dbg:
        for nm, shp in (("d_yb", [SSMW, L]), ("d_ys", [SSMW, L]), ("d_ya", [512, L]), ("d_bias", [NH, 384])):
            dbg_out[nm] = nc.dram_tensor(nm, shp, F32, kind="ExternalOutput").ap()

    def dscr(name, shape, dt):
        return nc.dram_tensor(name, list(shape), dt, kind="Internal")

    wb_in = dscr("wb_in", [D, INC], BF16).ap()
    wb_glu = dscr("wb_glu", [SSMW, SSMW], BF16).ap()
    wb_brs = dscr("wb_brs", [SSMW, D], BF16).ap()
    wb_bra = dscr("wb_bra", [512, D], BF16).ap()
    wb_out = dscr("wb_out", [D, D], BF16).ap()
    wb_ff1 = dscr("wb_ff1", [D, DFF], BF16).ap()
    wb_ff2 = dscr("wb_ff2", [DFF, D], BF16).ap()
    kt_scr = dscr("kt_scr", [NH, HD, LP], BF16).ap()
    v_scr_h = dscr("v_scr", [LP, NH * HD], BF16)
    v_scr = v_scr_h.ap()
    u_scr = dscr("u_scr", [SSMW, L], BF16).ap()
    yb_scr = dscr("yb_scr", [SSMW, L], F32).ap()
    ys_scr = dscr("ys_scr", [SSMW, L], BF16).ap()
    bias_scr_h = dscr("bias_scr", [NH, 384], F32)
    bias_scr = bias_scr_h.ap()
    bias_rep_h = dscr("bias_rep", [NH, 128, 384], F32)
    bias_rep = bias_rep_h.ap()

    es = ExitStack()
    with es:
        S = Sync(nc, es)
        PE, ACT, DVE, POOL = S.pe, S.act, S.dve, S.pool
        es.enter_context(nc.Block())
        es.enter_context(nc.allow_non_contiguous_dma(reason="small parameter re-layouts"))

        uniq = [0]

        def sbt(stack, name, shape, dt=F32):
            uniq[0] += 1
            return stack.enter_context(nc.sbuf_tensor("%s_%d" % (name, uniq[0]), list(shape), dt))

        def V(fn, reads=(), writes=()):
            S.op(DVE, fn, reads, writes)

        def A(fn, reads=(), writes=()):
            S.op(ACT, fn, reads, writes)

        def G(fn, reads=(), writes=()):
            S.op(POOL, fn, reads, writes)

        def T(fn, reads=(), writes=()):
            S.op(PE, fn, reads, writes)

        def cp(i, out, in_, reads, writes):
            if i % 2 == 0:
                A(lambda: nc.scalar.copy(out, in_), reads, writes)
            else:
                V(lambda: nc.vector.tensor_copy(out=out, in_=in_), reads, writes)

        PS2 = [es.enter_context(nc.psum_tensor("ps2_%d" % i, [128, 1024], F32)) for i in range(4)]
        PB = [Buf("psb%d" % i) for i in range(8)]

        def bank(i):
            return PS2[i // 2][:, (i % 2) * 512:(i % 2) * 512 + 512]

        rr = [0]

        def ps_rr():
            i = rr[0] % 4
            rr[0] += 1
            return i

        ident_f = sbt(es, "ident_f", [128, 128], F32)
        ident_b = sbt(es, "ident_b", [128, 128], BF16)
        tri_f = sbt(es, "tri_f", [128, 128], BF16)
        tri_b = sbt(es, "tri_b", [128, 128], BF16)
        ntri_f = sbt(es, "ntri_f", [128, 128], BF16)
        ntri_b = sbt(es, "ntri_b", [128, 128], BF16)
        perm_f = sbt(es, "perm_f", [128, 128], F32)
        ones_b = sbt(es, "ones_b", [128, 128], BF16)
        sgn = sbt(es, "sgn", [128, 1], F32)
        gmask = sbt(es, "gmask", [128, 8], F32)
        iot = sbt(es, "iot", [128, 128], F32)
        gm_t = sbt(es, "gm_t", [128, 8], F32)
        BC = Buf("const")

        G(lambda: nc.gpsimd.iota(iot[:], [[1, 128]], base=0, channel_multiplier=-1,
                                 allow_small_or_imprecise_dtypes=True), writes=[BC])
        G(lambda: nc.gpsimd.iota(gm_t[:], [[16, 8]], base=0, channel_multiplier=-1,
                                 allow_small_or_imprecise_dtypes=True), writes=[BC])
        V(lambda: nc.vector.tensor_scalar(ident_f[:], iot[:], 0.0, None, ALU.is_equal), [BC], [BC])
        V(lambda: nc.vector.tensor_scalar(tri_f[:], iot[:], 0.0, None, ALU.is_ge), [BC], [BC])
        V(lambda: nc.vector.tensor_scalar(tri_b[:], iot[:], 0.0, None, ALU.is_le), [BC], [BC])
        V(lambda: nc.vector.tensor_scalar(ntri_f[:], tri_f[:], -1.0, None, ALU.mult), [BC], [BC])
        V(lambda: nc.vector.tensor_scalar(ntri_b[:], tri_b[:], -1.0, None, ALU.mult), [BC], [BC])
        V(lambda: nc.vector.tensor_scalar(perm_f[:], iot[:], 64.0, None, ALU.is_equal), [BC], [BC])
        V(lambda: nc.vector.tensor_scalar(ident_b[:], iot[:], -64.0, None, ALU.is_equal), [BC], [BC])
        V(lambda: nc.vector.tensor_tensor(perm_f[:], perm_f[:], ident_b[:], ALU.add), [BC], [BC])
        V(lambda: nc.vector.tensor_copy(out=ident_b[:], in_=ident_f[:]), [BC], [BC])
        V(lambda: nc.vector.memset(ones_b[:], 1.0), [BC], [BC])
        V(lambda: nc.vector.tensor_scalar(sgn[:], iot[:, 0:1], -63.5, None, ALU.is_le), [BC], [BC])
        V(lambda: nc.vector.tensor_scalar(sgn[:], sgn[:], 2.0, -1.0, ALU.mult, ALU.add), [BC], [BC])
        V(lambda: nc.vector.tensor_scalar(gmask[:], gm_t[:], 0.5, None, ALU.is_le), [BC], [BC])
        V(lambda: nc.vector.tensor_scalar(gm_t[:], gm_t[:], -15.5, None, ALU.is_ge), [BC], [BC])
        V(lambda: nc.vector.tensor_tensor(gmask[:], gmask[:], gm_t[:], ALU.mult), [BC], [BC])

        g1col = sbt(es, "g1col", [128, KT], F32)
        g2col = sbt(es, "g2col", [128, KT], F32)
        dcol = sbt(es, "dcol", [128, 8], F32)
        bgcol = sbt(es, "bgcol", [128, 8], F32)
        S.dma(g1col[:], norm1_g[0].rearrange("(k p) -> p k", p=128), writes=[BC])
        S.dma(g2col[:], norm2_g[0].rearrange("(k p) -> p k", p=128), writes=[BC])
        S.dma(dcol[:], ssm_d[0].rearrange("(k p) -> p k", p=128), writes=[BC])
        S.dma(bgcol[:], b_glu[0].rearrange("(k p) -> p k", p=128), writes=[BC])

        with ExitStack() as ps:
            NST = 3
            zero_b = sbt(ps, "zero_b", [128, 1536], BF16)
            V(lambda: nc.vector.memset(zero_b[:], 0.0), [BC], [BC])
            stg_f = [sbt(ps, "stgf%d" % i, [128, 2048], F32) for i in range(NST)]
            stg_b = [sbt(ps, "stgb%d" % i, [128, 2048], BF16) for i in range(NST)]
            bf_ = [Buf() for _ in range(NST)]
            bb_ = [Buf() for _ in range(NST)]
            cnt = [0]

            def conv(src, dst, rows, cols, scol=None):
                for r0 in range(0, rows, 128):
                    for c0 in range(0, cols, 2048):
                        cw = min(2048, cols - c0)
                        i = cnt[0] % NST
                        e = cnt[0] % 2
                        cnt[0] += 1
                        S.dma(stg_f[i][:, 0:cw], src[r0:r0 + 128, c0:c0 + cw], writes=[bf_[i]])
                        o, a = stg_b[i][:, 0:cw], stg_f[i][:, 0:cw]
                        if scol is not None:
                            sc = scol[:, r0 // 128:r0 // 128 + 1]
                            if e == 0:
                                V(lambda: nc.vector.tensor_scalar(o, a, sc, None, ALU.mult), [bf_[i], BC], [bb_[i]])
                            elif e == 1:
                                A(lambda: nc.scalar.activation(o, a, AF.Copy, scale=sc), [bf_[i], BC], [bb_[i]])
                            else:
                                G(lambda: nc.gpsimd.tensor_scalar(o, a, sc, None, ALU.mult), [bf_[i], BC], [bb_[i]])
                        else:
                            if e == 0:
                                V(lambda: nc.vector.tensor_copy(out=o, in_=a), [bf_[i]], [bb_[i]])
                            elif e == 1:
                                A(lambda: nc.scalar.copy(o, a), [bf_[i]], [bb_[i]])
                            else:
                                G(lambda: nc.gpsimd.tensor_copy(out=o, in_=a), [bf_[i]], [bb_[i]])
                        S.dma(dst[r0:r0 + 128, c0:c0 + cw], o, reads=[bb_[i]])

            conv(w_in[0], wb_in, D, INC, g1col)
            conv(w_glu[0], wb_glu, SSMW, SSMW)
            conv(w_brs[0], wb_brs, SSMW, D)
            conv(w_bra[0], wb_bra, 512, D)
            conv(w_out[0], wb_out, D, D)
            conv(w_ff1[0], wb_ff1, D, DFF, g2col)
            conv(w_ff2[0], wb_ff2, DFF, D)
            if not CTX:
                for h in range(NH):
                    S.dma(kt_scr[h, :, 0:HALO], zero_b[:, 0:HALO], reads=[BC])
                    S.dma(kt_scr[h, :, HALO + L:LP], zero_b[:, 0:HALO], reads=[BC])
                for r0 in list(range(0, HALO, 128)) + list(range(HALO + L, LP, 128)):
                    S.dma(v_scr[r0:r0 + 128, :], zero_b[:, :], reads=[BC])

            tab = sbt(ps, "tab33", [33, NH], F32)
            oh = sbt(ps, "oh33", [33, 3, 384], F32)
            bsb = sbt(ps, "bias_sb", [4, 3, 384], F32)
            btmp = Buf()
            BBT = Buf("bt")
            V(lambda: nc.vector.memset(tab[32:33, :], NEG), writes=[btmp])
            S.dma(tab[0:32, :], rel_bias[:, :], writes=[btmp])
            S.dma(oh[:], onehot.rearrange("g b m -> b g m"), writes=[btmp])
            for gi in range(3):
                T(lambda: nc.tensor.matmul(bank(gi)[0:4, 0:384], tab[:, gi * 4:(gi + 1) * 4], oh[:, gi, :],
                                           start=True, stop=True), [btmp], [PB[gi]])
                V(lambda: nc.vector.tensor_copy(out=bsb[:, gi, :], in_=bank(gi)[0:4, 0:384]), [PB[gi]], [btmp])
                S.dma(bias_scr[gi * 4:(gi + 1) * 4, :], bsb[:, gi, :], reads=[btmp], writes=[BBT])
            for h in range(NH):
                S.dma(bias_rep[h], bass.AP(bias_scr_h, h * 384, [[0, 128], [1, 384]]), reads=[BBT], writes=[BBT])
            if dbg:
                S.dma(dbg_out["d_bias"], bias_scr, reads=[BBT], writes=[BBT])
            S.barrier()

        def make_wring(stack):
            NW = 3
            wring = [sbt(stack, "wring%d" % i, [128, 16, 512], BF16) for i in range(NW)]
            wbuf = [Buf("w%d" % i) for i in range(NW)]
            wcnt = [0]

            def wload(src_rows, nk, c0, cw=512):
                i = wcnt[0] % NW
                wcnt[0] += 1
                S.dma(wring[i][:, 0:nk, 0:cw], src_rows.rearrange("(k p) c -> p k c", p=128)[:, :, c0:c0 + cw],
                      writes=[wbuf[i]])
                return wring[i], wbuf[i]
            return wload

        def make_norm(stack):
            xh = [sbt(stack, "xhat%d" % i, [128, D], BF16) for i in range(2)]
            ssq = sbt(stack, "ssq", [128, 4], F32)
            rstd = sbt(stack, "rstd", [128, 4], F32)
            B_xh = [Buf("xhat0"), Buf("xhat1")]
            B_st = Buf("st")

            def stats(src_tile, B_src):
                for tt in range(4):
                    A(lambda: nc.scalar.activation(xh[tt % 2][:], src_tile[:, tt, :], AF.Square, accum_out=ssq[:, tt:tt + 1]),
                      [B_src], [B_xh[tt % 2], B_st])
                A(lambda: nc.scalar.activation(rstd[:], ssq[:], AF.Sqrt, scale=1.0 / D, bias=EPS), [B_st], [B_st])
                V(lambda: nc.vector.reciprocal(rstd[:], rstd[:]), [B_st], [B_st])

            def rmsnorm_to_T(src_tile, dstT, B_src, B_dst):
                stats(src_tile, B_src)
                for tt in range(4):
                    xhat, B_xhat = xh[tt % 2], B_xh[tt % 2]
                    V(lambda: nc.vector.tensor_scalar(xhat[:], src_tile[:, tt, :], rstd[:, tt:tt + 1], None, ALU.mult),
                      [B_src, B_st], [B_xhat])
                    psv = PS2[tt % 2][:].bitcast(BF16)
                    pbs = [PB[2 * (tt % 2)], PB[2 * (tt % 2) + 1]]
                    for kt in range(KT):
                        T(lambda: nc.tensor.transpose(psv[:, kt * 128:(kt + 1) * 128], xhat[:, kt * 128:(kt + 1) * 128],
                                                      ident_b[:]), [B_xhat, BC], pbs)
                    A(lambda: nc.scalar.copy(dstT[:, :, tt * 128:(tt + 1) * 128],
                                             psv[:, 0:2048].rearrange("p (k t) -> p k t", k=KT)), pbs, [B_dst])
            return rmsnorm_to_T, stats, rstd, B_st

        def proj_ws(wload, wsrc_rows, nk, col0, n_mt, rhsT, B_rhs, evac):
            for m0 in range(0, n_mt, 4):
                nm = min(4, n_mt - m0)
                wt, wb = wload(wsrc_rows, nk, col0 + m0 * 128, nm * 128)
                for mi in range(nm):
                    bi = ps_rr()
                    for k in range(nk):
                        T(lambda: nc.tensor.matmul(bank(bi), wt[:, k, mi * 128:(mi + 1) * 128], rhsT[:, k, :],
                                                   start=(k == 0), stop=(k == nk - 1)), [wb, B_rhs], [PB[bi]])
                    evac(m0 + mi, bank(bi), PB[bi])

        with ExitStack() as ps:
            wload = make_wring(ps)
            rmsnorm_to_T, _, _, _ = make_norm(ps)
            xb = sbt(ps, "xb", [128, 4, D], F32)
            xnTs = [sbt(ps, "xnT%d" % i, [128, KT, NB], BF16) for i in range(2)]
            kst = [sbt(ps, "kst%d" % i, [128, NB], BF16) for i in range(4)]
            B_x = Buf("x")
            B_xnTs = [Buf("xnT0"), Buf("xnT1")]
            B_kst = [Buf() for _ in range(4)]
            kc = [0]
            NCB = CTX // NB
            jobs = [("own", b) for b in range(NBLK)] + [("ctx", b) for b in range(NCB)]
            for jb, (kind, b) in enumerate(jobs):
                xnT, B_xnT = xnTs[jb % 2], B_xnTs[jb % 2]
                t0 = b * NB
                if kind == "own":
                    xsrc, udst, kvoff = xs, u_scr, HALO + t0
                else:
                    xsrc, udst = xc, uc_scr
                    kvoff = (HALO + L + t0) if b < 2 else ((t0 - (CTX - HALO)) if b >= NCB - 2 else None)
                S.dma(xb[:], xsrc[t0:t0 + NB, :].rearrange("(t p) d -> p t d", p=128), writes=[B_x])
                rmsnorm_to_T(xb, xnT, B_x, B_xnT)

                def ev_u(mt, psap, psb):
                    i = kc[0] % 4
                    kc[0] += 1
                    cp(mt, kst[i][:], psap, [psb], [B_kst[i]])
                    S.dma(udst[mt * 128:(mt + 1) * 128, t0:t0 + NB], kst[i][:], reads=[B_kst[i]])
                proj_ws(wload, wb_in, KT, 0, 8, xnT, B_xnT, ev_u)
                if kvoff is None:
                    continue

                def ev_k(mt, psap, psb):
                    i = kc[0] % 4
                    kc[0] += 1
                    cp(mt, kst[i][:], psap, [psb], [B_kst[i]])
                    S.dma(kt_scr[mt, :, kvoff:kvoff + NB], kst[i][:], reads=[B_kst[i]])
                proj_ws(wload, wb_in, KT, K_OFF, NH, xnT, B_xnT, ev_k)
                for gi in range(3):
                    wt, wb = wload(wb_in, KT, V_OFF + gi * 512, 512)
                    for tt in range(4):
                        bi = ps_rr()
                        for k in range(KT):
                            T(lambda: nc.tensor.matmul(bank(bi), xnT[:, k, tt * 128:(tt + 1) * 128], wt[:, k, :],
                                                       start=(k == 0), stop=(k == KT - 1)), [wb, B_xnT], [PB[bi]])
                        i = kc[0] % 4
                        kc[0] += 1
                        cp(tt, kst[i][:], bank(bi), [PB[bi]], [B_kst[i]])
                        S.dma(v_scr[kvoff + tt * 128:kvoff + (tt + 1) * 128, gi * 512:(gi + 1) * 512], kst[i][:],
                              reads=[B_kst[i]])
            S.barrier()

        with ExitStack() as ps:
            T1re = sbt(ps, "T1re", [128, NG, NS], BF16)
            T1im = sbt(ps, "T1im", [128, NG, NS], BF16)
            T2re = sbt(ps, "T2re", [128, NG, 128], BF16)
            T2im = sbt(ps, "T2im", [128, NG, 128], BF16)
            G1 = sbt(ps, "G1", [128, NG], F32)
            G2 = sbt(ps, "G2", [128, NG], F32)
            Bmat = sbt(ps, "Bmat", [128, 8, 8, 128], BF16)
            Cm1 = sbt(ps, "Cm1", [128, NG, 16], BF16)
            Cm2 = sbt(ps, "Cm2", [128, NG, 16], BF16)
            BTAB = Buf("ssmtab")

            def cmul(o_re, o_im, x_re, x_im, y_re, y_im, t1, t2, bufs):
                f = lambda fn: S.op(DVE, fn, bufs, bufs)
                E = nc.vector
                f(lambda: E.tensor_tensor(t1, x_re, y_re, ALU.mult))
                f(lambda: E.tensor_tensor(t2, x_im, y_im, ALU.mult))
                f(lambda: E.tensor_tensor(t2, t1, t2, ALU.subtract))
                f(lambda: E.tensor_tensor(t1, x_re, y_im, ALU.mult))
                f(lambda: E.tensor_tensor(o_im, x_im, y_re, ALU.mult))
                f(lambda: E.tensor_tensor(o_im, o_im, t1, ALU.add))
                f(lambda: E.tensor_copy(out=o_re, in_=t2))

            def abar_of(ar, ai, ldt, shape, stack, tag):
                bufs = [BTAB]
                mk = lambda nm: sbt(stack, tag + nm, shape, F32)
                dt_, lr, th, mag, cs, sn, t1, t2 = (mk(n)[:] for n in ("dt", "lr", "th", "mag", "cs", "sn", "t1", "t2"))
                o = {k: mk(k)[:] for k in ("abr", "abi", "air", "aii", "fr", "fi")}
                f = lambda fn: S.op(DVE, fn, bufs, bufs)
                fa = lambda fn: S.op(ACT, fn, bufs, bufs)
                fa(lambda: nc.scalar.activation(dt_, ldt, AF.Exp))
                f(lambda: nc.vector.tensor_tensor(lr, ar, dt_, ALU.mult))
                f(lambda: nc.vector.tensor_tensor(th, ai, dt_, ALU.mult))
                f(lambda: nc.vector.tensor_copy(out=t2, in_=th))
                for jj in range(1, 8):
                    f(lambda: nc.vector.tensor_scalar(t1, th, (2 * jj - 1) * math.pi, -2.0 * math.pi, ALU.is_gt, ALU.mult))
                    f(lambda: nc.vector.tensor_tensor(t2, t2, t1, ALU.add))
                fa(lambda: nc.scalar.activation(sn, t2, AF.Sin))
                f(lambda: nc.vector.tensor_scalar(t1, t2, -1.0, None, ALU.mult))
                f(lambda: nc.vector.tensor_tensor(t1, t1, t2, ALU.max))
                f(lambda: nc.vector.tensor_scalar(t1, t1, -1.0, math.pi / 2, ALU.mult, ALU.add))
                fa(lambda: nc.scalar.activation(cs, t1, AF.Sin))
                fa(lambda: nc.scalar.activation(mag, lr, AF.Exp))
                f(lambda: nc.vector.tensor_tensor(o["abr"], mag, cs, ALU.mult))
                f(lambda: nc.vector.tensor_tensor(o["abi"], mag, sn, ALU.mult))
                fa(lambda: nc.scalar.activation(mag, lr, AF.Exp, scale=-1.0))
                f(lambda: nc.vector.tensor_tensor(o["air"], mag, cs, ALU.mult))
                f(lambda: nc.vector.tensor_tensor(o["aii"], mag, sn, ALU.mult))
                f(lambda: nc.vector.tensor_scalar(o["aii"], o["aii"], -1.0, None, ALU.mult))
                f(lambda: nc.vector.tensor_tensor(t1, ar, ar, ALU.mult))
                f(lambda: nc.vector.tensor_tensor(t2, ai, ai, ALU.mult))
                f(lambda: nc.vector.tensor_tensor(t1, t1, t2, ALU.add))
                f(lambda: nc.vector.reciprocal(t1, t1))
                f(lambda: nc.vector.tensor_scalar(cs, o["abr"], -1.0, None, ALU.add))
                f(lambda: nc.vector.tensor_tensor(t2, cs, ar, ALU.mult))
                f(lambda: nc.vector.tensor_tensor(mag, o["abi"], ai, ALU.mult))
                f(lambda: nc.vector.tensor_tensor(t2, t2, mag, ALU.add))
                f(lambda: nc.vector.tensor_tensor(o["fr"], t2, t1, ALU.mult))
                f(lambda: nc.vector.tensor_tensor(t2, o["abi"], ar, ALU.mult))
                f(lambda: nc.vector.tensor_tensor(mag, cs, ai, ALU.mult))
                f(lambda: nc.vector.tensor_tensor(t2, t2, mag, ALU.subtract))
                f(lambda: nc.vector.tensor_tensor(o["fi"], t2, t1, ALU.mult))
                return o

            def gen_ssm_tables(dr):
                rev = (dr == 1)
                bufs = [BTAB]
                f = lambda fn: S.op(DVE, fn, bufs, bufs)
                with ExitStack() as p2:
                    arT = sbt(p2, "arT", [128, NG], F32)
                    aiT = sbt(p2, "aiT", [128, NG], F32)
                    ldT = sbt(p2, "ldT", [128, NG], F32)
                    for half in (0, 64):
                        S.dma(arT[half:half + 64, :], a_re[0, dr].rearrange("g n -> n g"), writes=bufs)
                        S.dma(aiT[half:half + 64, :], a_im[0, dr].rearrange("g n -> n g"), writes=bufs)
                    S.dma(ldT[:], log_dt[0, dr].partition_broadcast(128), writes=bufs)
                    st = abar_of(arT[:], aiT[:], ldT[:], [128, NG], p2, "s_")
                    GB = 16
                    Are = sbt(p2, "Are", [128, GB, 128], F32)
                    Aim = sbt(p2, "Aim", [128, GB, 128], F32)
                    Nre = sbt(p2, "Nre", [128, GB, 128], F32)
                    Nim = sbt(p2, "Nim", [128, GB, 128], F32)
                    tA = sbt(p2, "tA", [128, GB, 64], F32)
                    tB = sbt(p2, "tB", [128, GB, 64], F32)
                    cur = [sbt(p2, "cur%d" % i, [128, GB], F32) for i in range(4)]
                    tc1 = sbt(p2, "tc1", [128, GB], F32)
                    tc2 = sbt(p2, "tc2", [128, GB], F32)

                    def sl(lo, hi):
                        return slice(128 - hi, 128 - lo) if rev else slice(lo, hi)
                    for gb in range(NG // GB):
                        gs = slice(gb * GB, (gb + 1) * GB)
                        for (Pre, Pim, b_re_, b_im_, one) in ((Are, Aim, st["abr"], st["abi"], True),
                                                               (Nre, Nim, st["air"], st["aii"], False)):
                            if one:
                                f(lambda: nc.vector.memset(Pre[:, :, sl(0, 1)], 1.0))
                                f(lambda: nc.vector.memset(Pim[:, :, sl(0, 1)], 0.0))
                            else:
                                f(lambda: nc.vector.tensor_copy(out=Pre[:, :, sl(0, 1)], in_=st["fr"][:, gs].unsqueeze(2)))
                                f(lambda: nc.vector.tensor_copy(out=Pim[:, :, sl(0, 1)], in_=st["fi"][:, gs].unsqueeze(2)))
                            f(lambda: nc.vector.tensor_copy(out=cur[0][:], in_=b_re_[:, gs]))
                            f(lambda: nc.vector.tensor_copy(out=cur[1][:], in_=b_im_[:, gs]))
                            cr, ci, nr, ni = cur[0], cur[1], cur[2], cur[3]
                            w = 1
                            while w < 128:
                                bc = lambda t: t[:].unsqueeze(2).broadcast_to([128, GB, w])
                                cmul(Pre[:, :, sl(w, 2 * w)], Pim[:, :, sl(w, 2 * w)], Pre[:, :, sl(0, w)], Pim[:, :, sl(0, w)],
                                     bc(cr), bc(ci), tA[:, :, 0:w], tB[:, :, 0:w], bufs)
                                if 2 * w < 128:
                                    cmul(nr[:], ni[:], cr[:], ci[:], cr[:], ci[:], tc1[:], tc2[:], bufs)
                                    cr, ci, nr, ni = nr, ni, cr, ci
                                w *= 2
                        f(lambda: nc.vector.tensor_copy(out=T2re[:, gs, :], in_=Are[:]))
                        f(lambda: nc.vector.tensor_copy(out=T2im[:, gs, :], in_=Aim[:]))
                        e127 = 0 if rev else 127
                        cmul(G1[:, gs], G2[:, gs], Are[:, :, e127], Aim[:, :, e127], st["abr"][:, gs], st["abi"][:, gs],
                             tc1[:], tc2[:], bufs)
                        for src_t, dst_t in ((Nre, T1re), (Nim, T1im)):
                            for q4 in range(GB // 8):
                                for gg in range(8):
                                    g_l = q4 * 8 + gg
                                    T(lambda: nc.tensor.transpose(PS2[0][:, gg * 64:(gg + 1) * 64], src_t[0:64, g_l, :],
                                                                  ident_f[0:64, 0:64]), bufs + [BC], [PB[0]])
                                g0 = gb * GB + q4 * 8
                                A(lambda: nc.scalar.copy(dst_t[:, g0:g0 + 8, :].rearrange("p g n -> p (g n)"),
                                                         PS2[0][:, 0:512]), [PB[0]], bufs)
                    f(lambda: nc.vector.tensor_scalar(G2[:], G2[:], sgn[:, 0:1], None, ALU.mult))
                with ExitStack() as p2:
                    bl = sbt(p2, "bl", [64, 2, NG, 16], F32)
                    S.dma(bl[:, 0], b_re[0, dr].rearrange("g n c -> n g c"), writes=bufs)
                    S.dma(bl[:, 1], b_im[0, dr].rearrange("g n c -> n g c"), writes=bufs)
                    bcomp = sbt(p2, "bcomp", [128, 8, 128], F32)
                    for kt in range(8):
                        for ri in range(2):
                            T(lambda: nc.tensor.transpose(PS2[0][:, ri * 64:(ri + 1) * 64],
                                                          bl[:, ri, kt * 8:(kt + 1) * 8, :].rearrange("p g c -> p (g c)"),
                                                          ident_f[0:64, 0:64]), bufs + [BC], [PB[0]])
                        A(lambda: nc.scalar.copy(bcomp[:, kt, :], PS2[0][:, 0:128]), [PB[0]], bufs)
                    for g8 in range(8):
                        f(lambda: nc.vector.tensor_scalar(Bmat[:, :, g8, :], bcomp[:], gmask[:, g8:g8 + 1], None, ALU.mult))
                    cl = sbt(p2, "cl", [128, 8, 2, NS], F32)
                    for cm, (top, bot) in ((Cm1, (c_re, c_im)), (Cm2, (c_im, c_re))):
                        S.dma(cl[:, :, 0, :], top[0, dr].rearrange("(k g) c n -> (g c) k n", g=8), writes=bufs)
                        S.dma(cl[:, :, 1, :], bot[0, dr].rearrange("(k g) c n -> (g c) k n", g=8), writes=bufs)
                        for kt in range(8):
                            T(lambda: nc.tensor.transpose(PS2[kt // 4][:, (kt % 4) * 128:(kt % 4 + 1) * 128],
                                                          cl[:, kt].rearrange("p r n -> p (r n)"), ident_f[:]),
                              bufs + [BC], [PB[0], PB[2]])
                        for hf in range(2):
                            A(lambda: nc.scalar.copy(cm[:, hf * 32:(hf + 1) * 32, :].rearrange("p g c -> p (g c)"),
                                                     PS2[hf][:, 0:512]), [PB[0], PB[2]], bufs)
                    f(lambda: nc.vector.tensor_scalar(Cm1[64:128], Cm1[64:128], -1.0, None, ALU.mult))
                    f(lambda: nc.vector.tensor_scalar(Cm2[:], Cm2[:], -1.0, None, ALU.mult))
                S.barrier()

            uTb = [sbt(ps, "uT%d" % i, [128, 8, NB], BF16) for i in range(3)]
            B_uTb = [Buf() for _ in range(3)]
            yaccs = [sbt(ps, "yacc%d" % i, [128, 8, NB], F32) for i in range(2)]
            B_yaccs = [Buf("yacc0"), Buf("yacc1")]
            NMB = 4
            M1 = [sbt(ps, "M1_%d" % i, [128, 4, 128], BF16) for i in range(NMB)]
            M2 = [sbt(ps, "M2_%d" % i, [128, 4, 128], BF16) for i in range(NMB)]
            H1 = [sbt(ps, "H1_%d" % i, [128, 4, 128], BF16) for i in range(NMB)]
            H2 = [sbt(ps, "H2_%d" % i, [128, 4, 128], BF16) for i in range(NMB)]
            Wp = [sbt(ps, "Wp_%d" % i, [128, 4, 128], BF16) for i in range(NMB)]
            B_Wp = [Buf() for _ in range(NMB)]
            B_M = [Buf() for _ in range(NMB)]
            B_H = [Buf() for _ in range(NMB)]
            B_H2 = [Buf() for _ in range(NMB)]
            ccar = [sbt(ps, "ccar%d" % i, [128, NG], F32) for i in range(2)]
            xcar = sbt(ps, "xcar", [128, NG], F32)
            tcar = sbt(ps, "tcar", [128, NG], F32)
            ytok = sbt(ps, "ytok", [128, SSMW], F32)
            B_car, B_ytok = Buf("car"), Buf("ytok")
            flg = sbt(ps, "flg", [128, 2], F32)
            if CTX:
                S.dma(flg[:], flags_in, writes=[BC])

            import os as _os
            NWARM = int(_os.environ.get("NWARM", "0"))

            def ssm_run(dr, chunks):
                tri, ntri = (tri_f, ntri_f) if dr == 0 else (tri_b, ntri_b)
                last = 127 if dr == 0 else 0
                units = [(ci, j) for ci in range(len(chunks)) for j in range(16)]
                NU = len(units)
                cprev = ccar[0]

                def Bu(t):
                    ci, j = units[t]
                    ck = chunks[ci]
                    if j == 0 and ck.get("pre"):
                        ck["pre"]()
                    kt, hf = j // 2, j % 2
                    tok = slice(ck["c"] * 128, (ck["c"] + 1) * 128)
                    T(lambda: nc.tensor.matmul(bank(t % 2), ck["uT"][:, kt, tok],
                                               Bmat[:, kt, hf * 4:(hf + 1) * 4, :].rearrange("p g x -> p (g x)"),
                                               start=True, stop=True), [ck["B_uT"], BTAB], [PB[t % 2]])

                def st1(t):
                    ci, j = units[t]
                    gs = slice(j * 4, (j + 1) * 4)
                    mi = t % NMB
                    pv = bank(t % 2).rearrange("p (g r n) -> p g r n", g=4, r=2)
                    V(lambda: nc.vector.tensor_tensor(M1[mi][:].rearrange("p g (r n) -> p g r n", r=2), pv,
                                                      T1re[:, gs, :].unsqueeze(2).broadcast_to([128, 4, 2, NS]), ALU.mult),
                      [PB[t % 2], BTAB], [B_M[mi]])
                    V(lambda: nc.vector.tensor_tensor(M2[mi][:].rearrange("p g (r n) -> p g r n", r=2), pv,
                                                      T1im[:, gs, :].unsqueeze(2).broadcast_to([128, 4, 2, NS]), ALU.mult),
                      [PB[t % 2], BTAB], [B_M[mi]])

                def csum(t):
                    ci, j = units[t]
                    ck = chunks[ci]
                    mi = t % NMB
                    if ck["summary"]:
                        for g4 in range(4):
                            g = j * 4 + g4
                            o = PS2[3][:, 512 + g:512 + g + 1]
                            T(lambda: nc.tensor.matmul(o, M1[mi][:, g4, :], tri_f[:, 127:128], start=True, stop=False,
                                                       skip_group_check=True), [B_M[mi], BC], [PB[7]])
                            T(lambda: nc.tensor.matmul(o[0:64, :], M2[mi][:, g4, 64:128], ntri_f[:, 127:128], start=False,
                                                       stop=False, skip_group_check=True), [B_M[mi], BC], [PB[7]])
                            T(lambda: nc.tensor.matmul(o[64:128, :], M2[mi][:, g4, 0:64], tri_f[:, 127:128], start=False,
                                                       stop=True, skip_group_check=True), [B_M[mi], BC], [PB[7]])
                        return
                    wbk = 2 + t % 2
                    for g4 in range(4):
                        o = bank(wbk)[:, g4 * 128:(g4 + 1) * 128]
                        T(lambda: nc.tensor.matmul(o, M1[mi][:, g4, :], tri[:], start=True, stop=False,
                                                   skip_group_check=True), [B_M[mi], BC], [PB[wbk]])
                        T(lambda: nc.tensor.matmul(o[0:64, :], M2[mi][:, g4, 64:128], ntri[:], start=False, stop=False,
                                                   skip_group_check=True), [B_M[mi], BC], [PB[wbk]])
                        T(lambda: nc.tensor.matmul(o[64:128, :], M2[mi][:, g4, 0:64], tri[:], start=False, stop=True,
                                                   skip_group_check=True), [B_M[mi], BC], [PB[wbk]])

                def carry_update(ck):
                    T(lambda: nc.tensor.matmul(PS2[3][:, 640:640 + NG], perm_f[:], xcar[:], start=True, stop=True,
                                               skip_group_check=True), [B_car, BC], [PB[7]])
                    V(lambda: nc.vector.tensor_tensor(tcar[:], G2[:], PS2[3][:, 640:640 + NG], ALU.mult),
                      [PB[7], BTAB, B_car], [B_car])
                    V(lambda: nc.vector.tensor_tensor(ccar[1][:], G1[:], xcar[:], ALU.mult), [B_car, BTAB], [B_car])
                    if ck.get("scale") is not None:
                        V(lambda: nc.vector.tensor_tensor(ccar[1][:], ccar[1][:], tcar[:], ALU.add), [B_car], [B_car])
                        V(lambda: nc.vector.tensor_scalar(ccar[0][:], ccar[1][:], ck["scale"], None, ALU.mult),
                          [B_car, BC], [B_car])
                    else:
                        V(lambda: nc.vector.tensor_tensor(ccar[0][:], ccar[1][:], tcar[:], ALU.add), [B_car], [B_car])

                def st2(t):
                    ci, j = units[t]
                    ck = chunks[ci]
                    mi = t % NMB
                    if ck["summary"]:
                        if j == 15:
                            if ck["first"]:
                                V(lambda: nc.vector.tensor_copy(out=xcar[:], in_=PS2[3][:, 512:512 + NG]), [PB[7]], [B_car])
                            else:
                                V(lambda: nc.vector.tensor_tensor(xcar[:], PS2[3][:, 512:512 + NG], cprev[:], ALU.add),
                                  [PB[7], B_car], [B_car])
                            carry_update(ck)
                        return
                    wbk = 2 + t % 2
                    gs = slice(j * 4, (j + 1) * 4)
                    wv = bank(wbk).rearrange("p (g t) -> p g t", g=4)
                    if ck["first"]:
                        V(lambda: nc.vector.tensor_copy(out=xcar[:, gs], in_=wv[:, :, last]), [PB[wbk]], [B_car])
                    else:
                        V(lambda: nc.vector.tensor_tensor(xcar[:, gs], wv[:, :, last], cprev[:, gs], ALU.add),
                          [PB[wbk], B_car], [B_car])
                    for g4 in range(4):
                        g = j * 4 + g4
                        wsl = bank(wbk)[:, g4 * 128:(g4 + 1) * 128]
                        if ck["first"]:
                            A(lambda: nc.scalar.copy(Wp[mi][:, g4, :], wsl), [PB[wbk]], [B_Wp[mi]])
                        else:
                            A(lambda: nc.scalar.activation(Wp[mi][:, g4, :], wsl, AF.Identity, bias=cprev[:, g:g + 1]),
                              [PB[wbk], B_car], [B_Wp[mi]])
                    if j == 15:
                        carry_update(ck)

                def st2b(t):
                    ci, j = units[t]
                    ck = chunks[ci]
                    if ck["summary"]:
                        return
                    mi = t % NMB
                    gs = slice(j * 4, (j + 1) * 4)
                    V(lambda: nc.vector.tensor_tensor(H1[mi][:], Wp[mi][:], T2re[:, gs, :], ALU.mult),
                      [B_Wp[mi], BTAB], [B_H[mi]])
                    G(lambda: nc.gpsimd.tensor_tensor(H2[mi][:], Wp[mi][:], T2im[:, gs, :], ALU.mult),
                      [B_Wp[mi], BTAB], [B_H2[mi]])

                def cproj(t):
                    ci, j = units[t]
                    ck = chunks[ci]
                    if ck["summary"]:
                        return
                    mi = t % NMB
                    for g4 in range(4):
                        g = j * 4 + g4
                        o = PS2[2][:, g * 16:(g + 1) * 16]
                        T(lambda: nc.tensor.matmul(o, H1[mi][:, g4, :], Cm1[:, g, :], start=True, stop=False,
                                                   skip_group_check=True), [B_H[mi], BTAB], [PB[4 + g // 32]])
                        T(lambda: nc.tensor.matmul(o, H2[mi][:, g4, :], Cm2[:, g, :], start=False, stop=True,
                                                   skip_group_check=True), [B_H2[mi], BTAB], [PB[4 + g // 32]])
                    if j == 15:
                        tok = slice(ck["c"] * 128, (ck["c"] + 1) * 128)
                        A(lambda: nc.scalar.copy(ytok[:], PS2[2][:]), [PB[4], PB[5]], [B_ytok])
                        for hf in range(2):
                            for k in range(4):
                                kk = hf * 4 + k
                                T(lambda: nc.tensor.transpose(bank(6)[:, k * 128:(k + 1) * 128],
                                                              ytok[:, kk * 128:(kk + 1) * 128], ident_f[:]),
                                  [B_ytok, BC], [PB[6]])
                            A(lambda: nc.scalar.copy(ck["yacc"][:, hf * 4:(hf + 1) * 4, tok],
                                                     bank(6).rearrange("p (k t) -> p k t", k=4)), [PB[6]], [ck["B_yacc"]])
                        if ck.get("post"):
                            ck["post"]()

                def warm(n):
                    for _ in range(n):
                        T(lambda: nc.tensor.matmul(bank(6), ident_b[:], T2re[:, 0:4, :].rearrange("p g t -> p (g t)"),
                                                   start=True, stop=True, skip_group_check=True), [BTAB, BC], [PB[6]])

                for t in range(-1, NU + 2):
                    if t + 1 < NU:
                        Bu(t + 1)
                        warm(NWARM)
                        st1(t + 1)
                    if 0 <= t < NU:
                        csum(t)
                    if 0 <= t - 2 < NU:
                        cproj(t - 2)
                    if 0 <= t < NU:
                        st2(t)
                    if 0 <= t - 1 < NU:
                        st2b(t - 1)

            def mk_chunks(dr, own_posts):
                chunks = []
                ub = [0]
                nctx = CTX // NB
                order = range(nctx - 1, -1, -1) if dr == 1 else range(nctx)
                corder = range(3, -1, -1) if dr == 1 else range(4)
                for ib, b in enumerate(order):
                    ui = ub[0] % 3
                    ub[0] += 1

                    def pre(ui=ui, b=b):
                        S.dma(uTb[ui][:], uc_scr[:, b * NB:(b + 1) * NB].rearrange("(k p) t -> p k t", p=128),
                              writes=[B_uTb[ui]])
                    for ic, c in enumerate(corder):
                        lastc = (ib == nctx - 1 and ic == 3)
                        chunks.append(dict(uT=uTb[ui], B_uT=B_uTb[ui], c=c, summary=True, first=(ib == 0 and ic == 0),
                                           pre=pre if ic == 0 else None,
                                           scale=(flg[:, 1:2] if dr == 1 else flg[:, 0:1]) if lastc else None))
                border = range(NBLK - 1, -1, -1) if dr == 1 else range(NBLK)
                for ib, b in enumerate(border):
                    ui = ub[0] % 3
                    ub[0] += 1
                    yi = ib % 2

                    def pre(ui=ui, b=b):
                        S.dma(uTb[ui][:], u_scr[:, b * NB:(b + 1) * NB].rearrange("(k p) t -> p k t", p=128),
                              writes=[B_uTb[ui]])
                    for ic, c in enumerate(corder):
                        post = None
                        if ic == 3:
                            post = (lambda b=b, ui=ui, yi=yi: own_posts(b, ui, yi))
                        chunks.append(dict(uT=uTb[ui], B_uT=B_uTb[ui], c=c, summary=False,
                                           first=(ib == 0 and ic == 0 and not CTX), pre=pre if ic == 0 else None,
                                           post=post, yacc=yaccs[yi], B_yacc=B_yaccs[yi]))
                return chunks

            gen_ssm_tables(1)

            def post_bwd(b, ui, yi):
                S.dma(yb_scr[:, b * NB:(b + 1) * NB].rearrange("(k p) t -> p k t", p=128), yaccs[yi][:],
                      reads=[B_yaccs[yi]])
            ssm_run(1, mk_chunks(1, post_bwd))
            S.barrier()

            gen_ssm_tables(0)
            wglu = sbt(ps, "wglu", [128, 8, SSMW], BF16)
            B_wglu = Buf()
            S.dma(wglu[:], wb_glu.rearrange("(k p) c -> p k c", p=128), writes=[B_wglu])
            ybt = [sbt(ps, "ybt%d" % i, [128, NB], F32) for i in range(2)]
            B_ybt = [Buf() for _ in range(2)]
            gt = [sbt(ps, "gt%d" % i, [128, NB], F32) for i in range(2)]
            B_gt = [Buf() for _ in range(2)]
            zT = sbt(ps, "zT", [128, 8, NB], BF16)
            ysT = sbt(ps, "ysT", [128, 8, NB], BF16)
            sgl = [sbt(ps, "sgl%d" % i, [128, NB], BF16) for i in range(2)]
            B_sgl = [Buf() for _ in range(2)]
            B_z, B_ys = Buf("z"), Buf("ys")
            yc = [0]

            def post_fwd(b, ui, yi):
                t0 = b * NB
                yacc, B_yacc = yaccs[yi], B_yaccs[yi]
                for k in range(8):
                    i = yc[0] % 2
                    yc[0] += 1
                    S.dma(ybt[i][:], yb_scr[k * 128:(k + 1) * 128, t0:t0 + NB], writes=[B_ybt[i]])
                    yk = yacc[:, k, :]
                    G(lambda: nc.gpsimd.tensor_tensor(yk, yk, ybt[i][:], ALU.add), [B_yacc, B_ybt[i]], [B_yacc])
                    G(lambda: nc.gpsimd.tensor_scalar(gt[0][:], uTb[ui][:, k, :], dcol[:, k:k + 1], None, ALU.mult),
                      [B_uTb[ui], BC], [B_gt[0]])
                    G(lambda: nc.gpsimd.tensor_tensor(yk, yk, gt[0][:], ALU.add), [B_yacc, B_gt[0]], [B_yacc])
                    if dbg:
                        S.dma(dbg_out["d_yb"][k * 128:(k + 1) * 128, t0:t0 + NB], ybt[i][:], reads=[B_ybt[i]])
                    G(lambda: nc.gpsimd.tensor_tensor(gt[0][:], yk, yk, ALU.mult), [B_yacc], [B_gt[0]])
                    G(lambda: nc.gpsimd.tensor_scalar(gt[0][:], gt[0][:], 0.044715, 1.0, ALU.mult, ALU.add), [B_gt[0]], [B_gt[0]])
                    G(lambda: nc.gpsimd.tensor_tensor(gt[0][:], gt[0][:], yk, ALU.mult), [B_gt[0], B_yacc], [B_gt[0]])
                    A(lambda: nc.scalar.activation(gt[1][:], gt[0][:], AF.Sigmoid, scale=GELU_C), [B_gt[0]], [B_gt[1]])
                    G(lambda: nc.gpsimd.tensor_tensor(zT[:, k, :], yk, gt[1][:], ALU.mult), [B_gt[1], B_yacc], [B_z])
                for mt in range(8):
                    for k in range(8):
                        T(lambda: nc.tensor.matmul(bank(7), wglu[:, k, mt * 128:(mt + 1) * 128], zT[:, k, :],
                                                   start=(k == 0), stop=(k == 7), skip_group_check=True),
                          [B_wglu, B_z], [PB[7]])
                    i = mt % 2
                    A(lambda: nc.scalar.activation(sgl[i][:], bank(7), AF.Sigmoid, bias=bgcol[:, mt:mt + 1]),
                      [PB[7], BC], [B_sgl[i]])
                    G(lambda: nc.gpsimd.tensor_tensor(ysT[:, mt, :], zT[:, mt, :], sgl[i][:], ALU.mult),
                      [B_z, B_sgl[i]], [B_ys])
                S.dma(ys_scr[:, t0:t0 + NB].rearrange("(k p) t -> p k t", p=128), ysT[:], reads=[B_ys])
                if dbg:
                    for k in range(8):
                        G(lambda: nc.gpsimd.tensor_copy(out=gt[0][:], in_=ysT[:, k, :]), [B_ys], [B_gt[0]])
                        S.dma(dbg_out["d_ys"][k * 128:(k + 1) * 128, t0:t0 + NB], gt[0][:], reads=[B_gt[0]])
            ssm_run(0, mk_chunks(0, post_fwd))
            S.barrier()

        with ExitStack() as ps:
            wload = make_wring(ps)
            rmsnorm_to_T, stats, rstd, B_st = make_norm(ps)
            gfin = sbt(ps, "gfin", [128, D], F32)
            S.dma(gfin[:], final_g.partition_broadcast(128), writes=[BC])
            NQs = (128, 128, 32)
            bt_hi = sbt(ps, "bt_hi", [128, 24 * 128], BF16)
            bt_lo = sbt(ps, "bt_lo", [128, 24 * 128], BF16)
            BT = {}
            with ExitStack() as p2:
                bst = sbt(p2, "bst", [128, 24 * 128], F32)
                V(lambda: nc.vector.memset(bst[:], 0.0), writes=[BBT])
                for gi in range(3):
                    nq = NQs[gi]
                    for j in range(4):
                        for k2 in range(2):
                            col = ((gi * 4 + j) * 2 + k2) * 128
                            src = bass.AP(bias_rep_h, (gi * 4 + j) * 128 * 384 + 255 - 128 * k2, [[383, 128], [1, nq]])
                            S.dma(bst[:, col:col + nq], src, reads=[BBT], writes=[BBT])
                            BT[(gi, j, k2)] = (bt_hi[:, col:col + nq], bt_lo[:, col:col + nq])
                V(lambda: nc.vector.tensor_copy(out=bt_hi[:], in_=bst[:]), [BBT], [BBT])
                V(lambda: nc.vector.tensor_tensor(bst[:], bst[:], bt_hi[:], ALU.subtract), [BBT], [BBT])
                V(lambda: nc.vector.tensor_copy(out=bt_lo[:], in_=bst[:]), [BBT], [BBT])
                S.barrier()
            xb = sbt(ps, "xb", [128, 4, D], F32)
            xnT = sbt(ps, "xnT", [128, KT, NB], BF16)
            ysT = sbt(ps, "ysT2", [128, 8, NB], BF16)
            qT = sbt(ps, "qT", [128, NH, NB], BF16)
            mixT = sbt(ps, "mixT", [128, KT, NB], BF16)
            yaT = sbt(ps, "yaT", [128, 4, NB], BF16)
            sg = [sbt(ps, "sg%d" % i, [128, NB], BF16) for i in range(2)]
            sgt = [sbt(ps, "sgt%d" % i, [128, NB], BF16) for i in range(2)]
            B_x, B_xnT, B_ys, B_q, B_mixT, B_ya = (Buf(n) for n in "x xnT ys q mixT ya".split())
            B_sg = [Buf() for _ in range(2)]
            B_sgt = [Buf() for _ in range(2)]
            sgc = [0]
            kwin, B_kwin = [], []
            for gi_ in range(3):
                for i_ in range(2):
                    kwin.append(sbt(ps, "kwin%d_%d" % (gi_, i_), [128, NB + 128 * DIL[gi_]], BF16))
                    B_kwin.append(Buf())
            NVW, NPT = 5, 4
            vwin = [sbt(ps, "vwin%d" % i, [128, 2, 256], BF16) for i in range(NVW)]
            B_vwin = [Buf() for _ in range(NVW)]
            kmc = [sbt(ps, "kmc%d" % i, [128, 2], F32) for i in range(NVW)]
            PT = [sbt(ps, "PT%d" % i, [128, 2, 128], BF16) for i in range(NPT)]
            B_PT = [Buf() for _ in range(NPT)]
            rden = sbt(ps, "rden", [128, NB], F32)
            B_rden = Buf()
            print("pass2b sbuf remaining before actq:", nc.sbuf_bytes_remaining)
            actq = sbt(ps, "actq", [128, 8, NB], BF16)
            relu_t = [sbt(ps, "relu%d" % i, [128, NB], BF16) for i in range(2)]
            B_relu = [Buf() for _ in range(2)]
            B_actq = Buf()
            kwc, vwc, ptc = [0], [0], [0]

            for b in range(NBLK):
                t0 = b * NB
                S.dma(xb[:], xs[t0:t0 + NB, :].rearrange("(t p) d -> p t d", p=128), writes=[B_x])
                S.dma(ysT[:], ys_scr[:, t0:t0 + NB].rearrange("(k p) t -> p k t", p=128), writes=[B_ys])
                rmsnorm_to_T(xb, xnT, B_x, B_xnT)
                for m0 in range(0, KT, 4):
                    wtg, wbg = wload(wb_in, KT, G_OFF + m0 * 128, 512)
                    wtb, wbb = wload(wb_brs, 8, m0 * 128, 512)
                    for mi in range(4):
                        mt = m0 + mi
                        bi = ps_rr()
                        for k in range(KT):
                            T(lambda: nc.tensor.matmul(bank(bi), wtg[:, k, mi * 128:(mi + 1) * 128], xnT[:, k, :],
                                                       start=(k == 0), stop=(k == KT - 1)), [wbg, B_xnT], [PB[bi]])
                        i = sgc[0] % 2
                        sgc[0] += 1
                        A(lambda: nc.scalar.activation(sg[i][:], bank(bi), AF.Sigmoid), [PB[bi]], [B_sg[i]])
                        bj = ps_rr()
                        for k in range(8):
                            T(lambda: nc.tensor.matmul(bank(bj), wtb[:, k, mi * 128:(mi + 1) * 128], ysT[:, k, :],
                                                       start=(k == 0), stop=(k == 7)), [wbb, B_ys], [PB[bj]])
                        V(lambda: nc.vector.tensor_tensor(mixT[:, mt, :], bank(bj), sg[i][:], ALU.mult),
                          [PB[bj], B_sg[i]], [B_mixT])

                def ev_q(mt, psap, psb):
                    A(lambda: nc.scalar.activation(qT[:, mt, :], psap, AF.Copy, scale=HD ** -0.5), [psb], [B_q])
                proj_ws(wload, wb_in, KT, Q_OFF, NH, xnT, B_xnT, ev_q)

                for rnd in range(2):
                    first_mm = {0: True, 1: True}
                    items = []
                    unit_list = []
                    for gi in range(3):
                        for u in range(4 if gi == 0 else DIL[gi]):
                            unit_list.append((gi, u))
                            for jj in range(2):
                                items.append((gi, u, jj, len(unit_list) - 1))
                    uinfo = {}

                    def unit_geom(gi, u):
                        d = DIL[gi]
                        reach = 64 * d
                        if gi == 0:
                            return slice(u * 128, (u + 1) * 128), u * 128, 1, t0 + u * 128 - 64
                        return slice(u, NB, d), u, d, t0 + u - reach

                    def load_unit(ui_):
                        gi, u = unit_list[ui_]
                        nq = NQs[gi]
                        qsl, kcol0, kstep, tstart = unit_geom(gi, u)
                        if u == 0:
                            reach = 64 * DIL[gi]
                            for jj in range(2):
                                h = gi * 4 + rnd * 2 + jj
                                i = gi * 2 + jj
                                S.dma(kwin[i][:, 0:NB + 2 * reach], kt_scr[h, :, HALO + t0 - reach:HALO + t0 + NB + reach],
                                      writes=[B_kwin[i]])
                                kwmap[(gi, jj)] = i
                        vi = vwc[0] % NVW
                        vwc[0] += 1
                        c0 = gi * 512 + rnd * 256
                        r0 = HALO + tstart
                        RS = NH * HD
                        if nq == 128:
                            src = bass.AP(v_scr_h, r0 * RS + c0, [[kstep * RS, 128], [128 * kstep * RS, 2], [1, 256]])
                            S.dma(vwin[vi][:, :, :], src, writes=[B_vwin[vi]])
                            srcm = bass.AP(kmask_h, r0, [[kstep, 128], [128 * kstep, 2]])
                            S.dma(kmc[vi][:, :], srcm, writes=[B_vwin[vi]])
                        else:
                            for k2, n2 in ((0, 128), (1, nq)):
                                rr0 = r0 + k2 * 128 * kstep
                                src = bass.AP(v_scr_h, rr0 * RS + c0, [[kstep * RS, n2], [1, 256]])
                                S.dma(vwin[vi][0:n2, k2, :], src, writes=[B_vwin[vi]])
                                srcm = bass.AP(kmask_h, rr0, [[kstep, n2], [1, 1]])
                                S.dma(kmc[vi][0:n2, k2:k2 + 1], srcm, writes=[B_vwin[vi]])
                        uinfo[ui_] = vi

                    def scores(it):
                        gi, u, jj, ui_ = it
                        nq = NQs[gi]
                        nk2 = (128, nq)
                        qsl, kcol0, kstep, tstart = unit_geom(gi, u)
                        vi = uinfo[ui_]
                        j = rnd * 2 + jj
                        h = gi * 4 + j
                        ki = kwmap[(gi, jj)]
                        pi_ = ptc[0] % NPT
                        ptc[0] += 1
                        sb_i = ps_rr()
                        for k2 in range(2):
                            n2 = nk2[k2]
                            kc0 = kcol0 + k2 * 128 * kstep
                            ksl = slice(kc0, kc0 + (n2 - 1) * kstep + 1, kstep)
                            so = bank(sb_i)[0:n2, k2 * 128:k2 * 128 + nq]
                            bhi, blo = BT[(gi, j, k2)]
                            T(lambda: nc.tensor.matmul(so, kwin[ki][:, ksl], qT[:, h, qsl], start=True, stop=False,
                                                       skip_group_check=True), [B_kwin[ki], B_q], [PB[sb_i]])
                            T(lambda: nc.tensor.matmul(so, ident_b[0:n2, 0:n2], bhi[0:n2, :], start=False, stop=False,
                                                       skip_group_check=True), [BBT, BC], [PB[sb_i]])
                            T(lambda: nc.tensor.matmul(so, ident_b[0:n2, 0:n2], blo[0:n2, :], start=False, stop=True,
                                                       skip_group_check=True), [BBT, BC], [PB[sb_i]])
                            A(lambda: nc.scalar.activation(PT[pi_][0:n2, k2, 0:nq], so, AF.Exp,
                                                           bias=kmc[vi][0:n2, k2:k2 + 1]),
                              [PB[sb_i], B_vwin[vi]], [B_PT[pi_]])
                        return pi_

                    def pv(it, pi_):
                        gi, u, jj, ui_ = it
                        nq = NQs[gi]
                        nk2 = (128, nq)
                        qsl, kcol0, kstep, tstart = unit_geom(gi, u)
                        vi = uinfo[ui_]
                        ob, db = 4 + jj, 6 + jj
                        for k2 in range(2):
                            n2 = nk2[k2]
                            T(lambda: nc.tensor.matmul(bank(ob)[:, qsl], vwin[vi][0:n2, k2, jj * 128:(jj + 1) * 128],
                                                       PT[pi_][0:n2, k2, 0:nq], start=first_mm[jj], stop=False,
                                                       skip_group_check=True),
                              [B_vwin[vi], B_PT[pi_]], [PB[ob]])
                            T(lambda: nc.tensor.matmul(bank(db)[:, qsl], ones_b[0:n2, :], PT[pi_][0:n2, k2, 0:nq],
                                                       start=first_mm[jj], stop=False, skip_group_check=True),
                              [BC, B_PT[pi_]], [PB[db]])
                            first_mm[jj] = False

                    kwmap = {}
                    AHEAD = 3
                    for ui_ in range(min(AHEAD, len(unit_list))):
                        load_unit(ui_)
                    pend = []
                    for idx, it in enumerate(items):
                        if it[2] == 0 and it[3] + AHEAD < len(unit_list):
                            load_unit(it[3] + AHEAD)
                        pi_ = scores(it)
                        pend.append((it, pi_))
                        if len(pend) > 2:
                            pv(*pend.pop(0))
                    while pend:
                        pv(*pend.pop(0))
                    for jj in range(2):
                        j = rnd * 2 + jj
                        V(lambda: nc.vector.tensor_scalar(rden[:], bank(6 + jj), 1e-30, None, ALU.add), [PB[6 + jj]], [B_rden])
                        V(lambda: nc.vector.reciprocal(rden[:], rden[:]), [B_rden], [B_rden])
                        V(lambda: nc.vector.tensor_tensor(yaT[:, j, :], bank(4 + jj), rden[:], ALU.mult),
                          [PB[4 + jj], B_rden], [B_ya])
                if dbg:
                    for j in range(4):
                        V(lambda: nc.vector.tensor_copy(out=rden[:], in_=yaT[:, j, :]), [B_ya], [B_rden])
                        S.dma(dbg_out["d_ya"][j * 128:(j + 1) * 128, t0:t0 + NB], rden[:], reads=[B_rden])

                for m0 in range(0, KT, 4):
                    wtg, wbg = wload(wb_in, KT, G_OFF + D + m0 * 128, 512)
                    wtb, wbb = wload(wb_bra, 4, m0 * 128, 512)
                    for mi in range(4):
                        mt = m0 + mi
                        bi = ps_rr()
                        for k in range(KT):
                            T(lambda: nc.tensor.matmul(bank(bi), wtg[:, k, mi * 128:(mi + 1) * 128], xnT[:, k, :],
                                                       start=(k == 0), stop=(k == KT - 1)), [wbg, B_xnT], [PB[bi]])
                        i = sgc[0] % 2
                        sgc[0] += 1
                        A(lambda: nc.scalar.activation(sg[i][:], bank(bi), AF.Sigmoid), [PB[bi]], [B_sg[i]])
                        bj = ps_rr()
                        for k in range(4):
                            T(lambda: nc.tensor.matmul(bank(bj), wtb[:, k, mi * 128:(mi + 1) * 128], yaT[:, k, :],
                                                       start=(k == 0), stop=(k == 3)), [wbb, B_ya], [PB[bj]])
                        V(lambda: nc.vector.tensor_tensor(sgt[i][:], bank(bj), sg[i][:], ALU.mult),
                          [PB[bj], B_sg[i]], [B_sgt[i]])
                        G(lambda: nc.gpsimd.tensor_tensor(mixT[:, mt, :], sgt[i][:], mixT[:, mt, :], ALU.add),
                          [B_sgt[i], B_mixT], [B_mixT])

                for cg in range(4):
                    wt, wb = wload(wb_out, KT, cg * 512, 512)
                    for tt in range(4):
                        bi = ps_rr()
                        for k in range(KT):
                            T(lambda: nc.tensor.matmul(bank(bi), mixT[:, k, tt * 128:(tt + 1) * 128], wt[:, k, :],
                                                       start=(k == 0), stop=(k == KT - 1)), [wb, B_mixT], [PB[bi]])
                        xsl = xb[:, tt, cg * 512:(cg + 1) * 512]
                        V(lambda: nc.vector.tensor_tensor(xsl, bank(bi), xsl, ALU.add), [PB[bi], B_x], [B_x])
                rmsnorm_to_T(xb, xnT, B_x, B_xnT)
                for qf in range(8):
                    def ev_ff1(mt, psap, psb):
                        i = sgc[0] % 2
                        sgc[0] += 1
                        A(lambda: nc.scalar.activation(relu_t[i][:], psap, AF.Relu), [psb], [B_relu[i]])
                        V(lambda: nc.vector.scalar_tensor_tensor(actq[:, mt, :], psap, 0.0, relu_t[i][:], ALU.max, ALU.mult),
                          [psb, B_relu[i]], [B_actq])
                    proj_ws(wload, wb_ff1, KT, qf * 1024, 8, xnT, B_xnT, ev_ff1)
                    for cg in range(4):
                        wt, wb = wload(wb_ff2[qf * 1024:(qf + 1) * 1024, :], 8, cg * 512, 512)
                        for tt in range(4):
                            bi = ps_rr()
                            for k in range(8):
                                T(lambda: nc.tensor.matmul(bank(bi), actq[:, k, tt * 128:(tt + 1) * 128], wt[:, k, :],
                                                           start=(k == 0), stop=(k == 7)), [wb, B_actq], [PB[bi]])
                            xsl = xb[:, tt, cg * 512:(cg + 1) * 512]
                            V(lambda: nc.vector.tensor_tensor(xsl, bank(bi), xsl, ALU.add), [PB[bi], B_x], [B_x])
                stats(xb, B_x)
                for tt in range(4):
                    V(lambda: nc.vector.scalar_tensor_tensor(xb[:, tt, :], xb[:, tt, :], rstd[:, tt:tt + 1], gfin[:],
                                                             ALU.mult, ALU.mult), [B_x, B_st, BC], [B_x])
                S.dma(ys[t0:t0 + NB, :].rearrange("(t p) d -> p t d", p=128), xb[:], reads=[B_x])
            S.barrier()
    return nc


_PROG = {}


def _get_prog(L, dbg=False, CTX=0):
    key = (L, dbg, CTX)
    if key not in _PROG:
        _PROG[key] = build_program(L, dbg, CTX)
    return _PROG[key]


def _kmask(L, nvalid):
    m = np.full((L + 2 * HALO, 1), NEG, np.float32)
    m[HALO:HALO + nvalid] = 0.0
    return m


def kernel(**inputs):
    f32 = lambda a: np.ascontiguousarray(np.asarray(a, dtype=np.float32))
    xp = f32(inputs["x_prompt"])
    xsm = f32(inputs["x_sample"])
    B, SL, _ = xp.shape
    LS = xsm.shape[1]
    assert xsm.shape[0] == 1 and LS == 2 * SL and B + 2 <= 8
    L = SL
    shared = {}
    for k in ("norm1_g", "w_in", "ssm_a_re", "ssm_a_im", "ssm_log_dt", "ssm_b_re", "ssm_b_im", "ssm_c_re",
              "ssm_c_im", "ssm_d", "w_glu", "b_glu", "w_br_ssm", "w_br_attn", "w_out", "norm2_g", "w_ff1",
              "w_ff2", "rel_bias", "final_g"):
        shared[k] = f32(inputs[k])
    shared["onehot"] = _t5_onehot()
    zeros_ctx = np.zeros((L, D), np.float32)

    def km(left, right):
        m = np.full((L + 2 * HALO, 1), NEG, np.float32)
        m[HALO:HALO + L] = 0.0
        if left:
            m[:HALO] = 0.0
        if right:
            m[HALO + L:] = 0.0
        return m

    def fl(a, b):
        f = np.zeros((128, 2), np.float32)
        f[:, 0] = a
        f[:, 1] = b
        return f
    in_maps = []
    for c in range(8):
        m = dict(shared)
        if c < B:
            m["xs"], m["xc"], m["kmask"], m["flags"] = xp[c], zeros_ctx, km(False, False), fl(0.0, 0.0)
        elif c == B:
            m["xs"], m["xc"], m["kmask"], m["flags"] = xsm[0, :L], xsm[0, L:], km(False, True), fl(0.0, 1.0)
        elif c == B + 1:
            m["xs"], m["xc"], m["kmask"], m["flags"] = xsm[0, L:], xsm[0, :L], km(True, False), fl(1.0, 0.0)
        else:
            m["xs"], m["xc"], m["kmask"], m["flags"] = zeros_ctx, zeros_ctx, km(False, False), fl(0.0, 0.0)
        in_maps.append(m)
    nc = _get_prog(L, CTX=L)
    res = run_bass_kernel_spmd(nc, in_maps, core_ids=list(range(8)))
    y_prompt = np.stack([np.asarray(res.results[c]["ys"], dtype=np.float32) for c in range(B)], axis=0)
    y_sample = np.concatenate([np.asarray(res.results[B]["ys"], dtype=np.float32),
                               np.asarray(res.results[B + 1]["ys"], dtype=np.float32)], axis=0)[None]
    return (y_prompt, y_sample)
```

```python
import math
from contextlib import ExitStack

import numpy as np
import concourse.bass as bass
import concourse.mybir as mybir
from concourse.bass_utils import run_bass_kernel_spmd

F32 = mybir.dt.float32
BF16 = mybir.dt.bfloat16
AF = mybir.ActivationFunctionType
ALU = mybir.AluOpType

D = 2048
KT = 16
SSMW = 1024
NG = 64
NS = 64
NH = 12
HD = 128
INC = 9728
Q_OFF, K_OFF, V_OFF, G_OFF = 1024, 2560, 4096, 5632
DFF = 8192
NB = 512
HALO = 1024
NEG = -30000.0
EPS = 1e-6
DIL = (1, 4, 16)
GELU_C = 2.0 * math.sqrt(2.0 / math.pi)


class Buf:
    __slots__ = ("w", "r", "name")

    def __init__(self, name=""):
        self.w = {}
        self.r = {}
        self.name = name


class Eng:
    def __init__(self, eng, sem, is_pe=False):
        self.eng = eng
        self.sem = sem
        self.cnt = 0
        self.known = {}
        self.is_pe = is_pe


class Sync:
    NR = 16

    def __init__(self, nc, es):
        self.nc = nc
        self.sems = {}

        def mk(name):
            s = es.enter_context(nc.semaphore(name))
            self.sems[id(s)] = s
            return s

        self.pe = Eng(nc.tensor, mk("s_pe"), is_pe=True)
        self.act = Eng(nc.scalar, mk("s_act"))
        self.dve = Eng(nc.vector, mk("s_dve"))
        self.pool = Eng(nc.gpsimd, mk("s_pool"))
        self.sp = Eng(nc.sync, mk("s_sp"))
        self.ring = [mk("s_dma%d" % i) for i in range(self.NR)]
        self.ring_cnt = [0] * self.NR
        self.n_dma = 0

    def _waits(self, E, reads, writes):
        need = {}
        for b in reads:
            for k, v in b.w.items():
                if need.get(k, 0) < v:
                    need[k] = v
        for b in writes:
            for k, v in b.w.items():
                if need.get(k, 0) < v:
                    need[k] = v
            for k, v in b.r.items():
                if need.get(k, 0) < v:
                    need[k] = v
        for k, v in need.items():
            if E.is_pe and k == id(E.sem):
                continue
            if E.known.get(k, 0) >= v:
                continue
            E.eng.wait_ge(self.sems[k], v)
            E.known[k] = v

    def op(self, E, fn, reads=(), writes=()):
        self._waits(E, reads, writes)
        inst = fn()
        E.cnt += 1
        inst.then_inc(E.sem, 1)
        k = id(E.sem)
        for b in writes:
            b.w = {k: E.cnt}
            b.r = {}
        for b in reads:
            if b.r.get(k, 0) < E.cnt:
                b.r[k] = E.cnt

    def dma(self, out, in_, reads=(), writes=(), q=None, **kw):
        E = q if q is not None else self.sp
        i = self.n_dma % self.NR
        self.n_dma += 1
        sem = self.ring[i]
        k = id(sem)
        prev = self.ring_cnt[i]
        if prev > 0 and E.known.get(k, 0) < prev:
            E.eng.wait_ge(sem, prev)
            E.known[k] = prev
        self._waits(E, reads, writes)
        E.eng.dma_start(out=out, in_=in_, **kw).then_inc(sem, 16)
        self.ring_cnt[i] = prev + 16
        for b in writes:
            b.w = {k: prev + 16}
            b.r = {}
        for b in reads:
            if b.r.get(k, 0) < prev + 16:
                b.r[k] = prev + 16

    def barrier(self):
        engs = (self.pe, self.act, self.dve, self.pool, self.sp)
        for E in engs:
            for X in engs:
                if X is E or X.cnt == 0:
                    continue
                k = id(X.sem)
                if E.known.get(k, 0) < X.cnt:
                    E.eng.wait_ge(X.sem, X.cnt)
                    E.known[k] = X.cnt
            for i in range(self.NR):
                c = self.ring_cnt[i]
                k = id(self.ring[i])
                if c > 0 and E.known.get(k, 0) < c:
                    E.eng.wait_ge(self.ring[i], c)
                    E.known[k] = c


def _t5_onehot():
    oh = np.zeros((3, 33, 384), np.float32)
    for gi, d in enumerate(DIL):
        for m in range(384):
            rel = 191 - m
            if abs(rel) > 64:
                oh[gi, 32, m] = 1.0
                continue
            r = rel * d
            ret = 16 if r > 0 else 0
            n = abs(r)
            nf = np.float32(max(n, 1))
            large = 8 + int(np.float32(np.log(nf / np.float32(8.0)) / np.float32(math.log(128.0)) * np.float32(8.0)))
            large = min(large, 15)
            b = ret + (n if n < 8 else large)
            oh[gi, b, m] = 1.0
    return oh


def build_program(L, dbg=False, CTX=0):
    assert L % NB == 0 and CTX % NB == 0 and (CTX == 0 or CTX >= 2 * HALO)
    NBLK = L // NB
    LP = L + 2 * HALO
    nc = bass.Bass("TRN2", target_bir_lowering=False)

    def din(name, shape, dt=F32):
        return nc.dram_tensor(name, list(shape), dt, kind="ExternalInput")

    def dscr_early(name, shape, dt):
        return nc.dram_tensor(name, list(shape), dt, kind="Internal")

    xs = din("xs", [L, D]).ap()
    kmask_h = din("kmask", [LP, 1])
    if CTX:
        xc = din("xc", [CTX, D]).ap()
        flags_in = din("flags", [128, 2]).ap()
        uc_scr = dscr_early("uc_scr", [SSMW, CTX], BF16).ap()
    norm1_g = din("norm1_g", [1, D]).ap()
    w_in = din("w_in", [1, D, INC]).ap()
    a_re = din("ssm_a_re", [1, 2, NG, NS]).ap()
    a_im = din("ssm_a_im", [1, 2, NG, NS]).ap()
    log_dt = din("ssm_log_dt", [1, 2, NG]).ap()
    b_re = din("ssm_b_re", [1, 2, NG, NS, 16]).ap()
    b_im = din("ssm_b_im", [1, 2, NG, NS, 16]).ap()
    c_re = din("ssm_c_re", [1, 2, NG, 16, NS]).ap()
    c_im = din("ssm_c_im", [1, 2, NG, 16, NS]).ap()
    ssm_d = din("ssm_d", [1, SSMW]).ap()
    w_glu = din("w_glu", [1, SSMW, SSMW]).ap()
    b_glu = din("b_glu", [1, SSMW]).ap()
    w_brs = din("w_br_ssm", [1, SSMW, D]).ap()
    w_bra = din("w_br_attn", [1, 512, D]).ap()
    w_out = din("w_out", [1, D, D]).ap()
    norm2_g = din("norm2_g", [1, D]).ap()
    w_ff1 = din("w_ff1", [1, D, DFF]).ap()
    w_ff2 = din("w_ff2", [1, DFF, D]).ap()
    rel_bias = din("rel_bias", [32, NH]).ap()
    final_g = din("final_g", [D]).ap()
    onehot = din("onehot", [3, 33, 384]).ap()
    ys = nc.dram_tensor("ys", [L, D], F32, kind="ExternalOutput").ap()
    dbg_out = {}
    if dbg:
        for nm, shp in (("d_yb", [SSMW, L]), ("d_ys", [SSMW, L]), ("d_ya", [512, L]), ("d_bias", [NH, 384])):
            dbg_out[nm] = nc.dram_tensor(nm, shp, F32, kind="ExternalOutput").ap()

    def dscr(name, shape, dt):
        return nc.dram_tensor(name, list(shape), dt, kind="Internal")

    wb_in = dscr("wb_in", [D, INC], BF16).ap()
    wb_glu = dscr("wb_glu", [SSMW, SSMW], BF16).ap()
    wb_brs = dscr("wb_brs", [SSMW, D], BF16).ap()
    wb_bra = dscr("wb_bra", [512, D], BF16).ap()
    wb_out = dscr("wb_out", [D, D], BF16).ap()
    wb_ff1 = dscr("wb_ff1", [D, DFF], BF16).ap()
    wb_ff2 = dscr("wb_ff2", [DFF, D], BF16).ap()
    kt_scr = dscr("kt_scr", [NH, HD, LP], BF16).ap()
    v_scr_h = dscr("v_scr", [LP, NH * HD], BF16)
    v_scr = v_scr_h.ap()
    u_scr = dscr("u_scr", [SSMW, L], BF16).ap()
    yb_scr = dscr("yb_scr", [SSMW, L], F32).ap()
    ys_scr = dscr("ys_scr", [SSMW, L], BF16).ap()
    bias_scr_h = dscr("bias_scr", [NH, 384], F32)
    bias_scr = bias_scr_h.ap()
    bias_rep_h = dscr("bias_rep", [NH, 128, 384], F32)
    bias_rep = bias_rep_h.ap()

    es = ExitStack()
    with es:
        S = Sync(nc, es)
        PE, ACT, DVE, POOL = S.pe, S.act, S.dve, S.pool
        es.enter_context(nc.Block())
        es.enter_context(nc.allow_non_contiguous_dma(reason="small parameter re-layouts"))

        uniq = [0]

        def sbt(stack, name, shape, dt=F32):
            uniq[0] += 1
            return stack.enter_context(nc.sbuf_tensor("%s_%d" % (name, uniq[0]), list(shape), dt))

        def V(fn, reads=(), writes=()):
            S.op(DVE, fn, reads, writes)

        def A(fn, reads=(), writes=()):
            S.op(ACT, fn, reads, writes)

        def G(fn, reads=(), writes=()):
            S.op(POOL, fn, reads, writes)

        def T(fn, reads=(), writes=()):
            S.op(PE, fn, reads, writes)

        def cp(i, out, in_, reads, writes):
            if i % 2 == 0:
                A(lambda: nc.scalar.copy(out, in_), reads, writes)
            else:
                V(lambda: nc.vector.tensor_copy(out=out, in_=in_), reads, writes)

        PS2 = [es.enter_context(nc.psum_tensor("ps2_%d" % i, [128, 1024], F32)) for i in range(4)]
        PB = [Buf("psb%d" % i) for i in range(8)]

        def bank(i):
            return PS2[i // 2][:, (i % 2) * 512:(i % 2) * 512 + 512]

        rr = [0]

        def ps_rr():
            i = rr[0] % 4
            rr[0] += 1
            return i

        ident_f = sbt(es, "ident_f", [128, 128], F32)
        ident_b = sbt(es, "ident_b", [128, 128], BF16)
        tri_f = sbt(es, "tri_f", [128, 128], BF16)
        tri_b = sbt(es, "tri_b", [128, 128], BF16)
        ntri_f = sbt(es, "ntri_f", [128, 128], BF16)
        ntri_b = sbt(es, "ntri_b", [128, 128], BF16)
        perm_f = sbt(es, "perm_f", [128, 128], F32)
        ones_b = sbt(es, "ones_b", [128, 128], BF16)
        sgn = sbt(es, "sgn", [128, 1], F32)
        gmask = sbt(es, "gmask", [128, 8], F32)
        iot = sbt(es, "iot", [128, 128], F32)
        gm_t = sbt(es, "gm_t", [128, 8], F32)
        BC = Buf("const")

        G(lambda: nc.gpsimd.iota(iot[:], [[1, 128]], base=0, channel_multiplier=-1,
                                 allow_small_or_imprecise_dtypes=True), writes=[BC])
        G(lambda: nc.gpsimd.iota(gm_t[:], [[16, 8]], base=0, channel_multiplier=-1,
                                 allow_small_or_imprecise_dtypes=True), writes=[BC])
        V(lambda: nc.vector.tensor_scalar(ident_f[:], iot[:], 0.0, None, ALU.is_equal), [BC], [BC])
        V(lambda: nc.vector.tensor_scalar(tri_f[:], iot[:], 0.0, None, ALU.is_ge), [BC], [BC])
        V(lambda: nc.vector.tensor_scalar(tri_b[:], iot[:], 0.0, None, ALU.is_le), [BC], [BC])
        V(lambda: nc.vector.tensor_scalar(ntri_f[:], tri_f[:], -1.0, None, ALU.mult), [BC], [BC])
        V(lambda: nc.vector.tensor_scalar(ntri_b[:], tri_b[:], -1.0, None, ALU.mult), [BC], [BC])
        V(lambda: nc.vector.tensor_scalar(perm_f[:], iot[:], 64.0, None, ALU.is_equal), [BC], [BC])
        V(lambda: nc.vector.tensor_scalar(ident_b[:], iot[:], -64.0, None, ALU.is_equal), [BC], [BC])
        V(lambda: nc.vector.tensor_tensor(perm_f[:], perm_f[:], ident_b[:], ALU.add), [BC], [BC])
        V(lambda: nc.vector.tensor_copy(out=ident_b[:], in_=ident_f[:]), [BC], [BC])
        V(lambda: nc.vector.memset(ones_b[:], 1.0), [BC], [BC])
        V(lambda: nc.vector.tensor_scalar(sgn[:], iot[:, 0:1], -63.5, None, ALU.is_le), [BC], [BC])
        V(lambda: nc.vector.tensor_scalar(sgn[:], sgn[:], 2.0, -1.0, ALU.mult, ALU.add), [BC], [BC])
        V(lambda: nc.vector.tensor_scalar(gmask[:], gm_t[:], 0.5, None, ALU.is_le), [BC], [BC])
        V(lambda: nc.vector.tensor_scalar(gm_t[:], gm_t[:], -15.5, None, ALU.is_ge), [BC], [BC])
        V(lambda: nc.vector.tensor_tensor(gmask[:], gmask[:], gm_t[:], ALU.mult), [BC], [BC])

        g1col = sbt(es, "g1col", [128, KT], F32)
        g2col = sbt(es, "g2col", [128, KT], F32)
        dcol = sbt(es, "dcol", [128, 8], F32)
        bgcol = sbt(es, "bgcol", [128, 8], F32)
        S.dma(g1col[:], norm1_g[0].rearrange("(k p) -> p k", p=128), writes=[BC])
        S.dma(g2col[:], norm2_g[0].rearrange("(k p) -> p k", p=128), writes=[BC])
        S.dma(dcol[:], ssm_d[0].rearrange("(k p) -> p k", p=128), writes=[BC])
        S.dma(bgcol[:], b_glu[0].rearrange("(k p) -> p k", p=128), writes=[BC])

        with ExitStack() as ps:
            NST = 3
            zero_b = sbt(ps, "zero_b", [128, 1536], BF16)
            V(lambda: nc.vector.memset(zero_b[:], 0.0), [BC], [BC])
            stg_f = [sbt(ps, "stgf%d" % i, [128, 2048], F32) for i in range(NST)]
            stg_b = [sbt(ps, "stgb%d" % i, [128, 2048], BF16) for i in range(NST)]
            bf_ = [Buf() for _ in range(NST)]
            bb_ = [Buf() for _ in range(NST)]
            cnt = [0]

            def conv(src, dst, rows, cols, scol=None):
                for r0 in range(0, rows, 128):
                    for c0 in range(0, cols, 2048):
                        cw = min(2048, cols - c0)
                        i = cnt[0] % NST
                        e = cnt[0] % 2
                        cnt[0] += 1
                        S.dma(stg_f[i][:, 0:cw], src[r0:r0 + 128, c0:c0 + cw], writes=[bf_[i]])
                        o, a = stg_b[i][:, 0:cw], stg_f[i][:, 0:cw]
                        if scol is not None:
                            sc = scol[:, r0 // 128:r0 // 128 + 1]
                            if e == 0:
                                V(lambda: nc.vector.tensor_scalar(o, a, sc, None, ALU.mult), [bf_[i], BC], [bb_[i]])
                            elif e == 1:
                                A(lambda: nc.scalar.activation(o, a, AF.Copy, scale=sc), [bf_[i], BC], [bb_[i]])
                            else:
                                G(lambda: nc.gpsimd.tensor_scalar(o, a, sc, None, ALU.mult), [bf_[i], BC], [bb_[i]])
                        else:
                            if e == 0:
                                V(lambda: nc.vector.tensor_copy(out=o, in_=a), [bf_[i]], [bb_[i]])
                            elif e == 1:
                                A(lambda: nc.scalar.copy(o, a), [bf_[i]], [bb_[i]])
                            else:
                                G(lambda: nc.gpsimd.tensor_copy(out=o, in_=a), [bf_[i]], [bb_[i]])
                        S.dma(dst[r0:r0 + 128, c0:c0 + cw], o, reads=[bb_[i]])

            conv(w_in[0], wb_in, D, INC, g1col)
            conv(w_glu[0], wb_glu, SSMW, SSMW)
            conv(w_brs[0], wb_brs, SSMW, D)
            conv(w_bra[0], wb_bra, 512, D)
            conv(w_out[0], wb_out, D, D)
            conv(w_ff1[0], wb_ff1, D, DFF, g2col)
            conv(w_ff2[0], wb_ff2, DFF, D)
            if not CTX:
                for h in range(NH):
                    S.dma(kt_scr[h, :, 0:HALO], zero_b[:, 0:HALO], reads=[BC])
                    S.dma(kt_scr[h, :, HALO + L:LP], zero_b[:, 0:HALO], reads=[BC])
                for r0 in list(range(0, HALO, 128)) + list(range(HALO + L, LP, 128)):
                    S.dma(v_scr[r0:r0 + 128, :], zero_b[:, :], reads=[BC])

            tab = sbt(ps, "tab33", [33, NH], F32)
            oh = sbt(ps, "oh33", [33, 3, 384], F32)
            bsb = sbt(ps, "bias_sb", [4, 3, 384], F32)
            btmp = Buf()
            BBT = Buf("bt")
            V(lambda: nc.vector.memset(tab[32:33, :], NEG), writes=[btmp])
            S.dma(tab[0:32, :], rel_bias[:, :], writes=[btmp])
            S.dma(oh[:], onehot.rearrange("g b m -> b g m"), writes=[btmp])
            for gi in range(3):
                T(lambda: nc.tensor.matmul(bank(gi)[0:4, 0:384], tab[:, gi * 4:(gi + 1) * 4], oh[:, gi, :],
                                           start=True, stop=True), [btmp], [PB[gi]])
                V(lambda: nc.vector.tensor_copy(out=bsb[:, gi, :], in_=bank(gi)[0:4, 0:384]), [PB[gi]], [btmp])
                S.dma(bias_scr[gi * 4:(gi + 1) * 4, :], bsb[:, gi, :], reads=[btmp], writes=[BBT])
            for h in range(NH):
                S.dma(bias_rep[h], bass.AP(bias_scr_h, h * 384, [[0, 128], [1, 384]]), reads=[BBT], writes=[BBT])
            if dbg:
                S.dma(dbg_out["d_bias"], bias_scr, reads=[BBT], writes=[BBT])
            S.barrier()

        def make_wring(stack):
            NW = 3
            wring = [sbt(stack, "wring%d" % i, [128, 16, 512], BF16) for i in range(NW)]
            wbuf = [Buf("w%d" % i) for i in range(NW)]
            wcnt = [0]

            def wload(src_rows, nk, c0, cw=512):
                i = wcnt[0] % NW
                wcnt[0] += 1
                S.dma(wring[i][:, 0:nk, 0:cw], src_rows.rearrange("(k p) c -> p k c", p=128)[:, :, c0:c0 + cw],
                      writes=[wbuf[i]])
                return wring[i], wbuf[i]
            return wload

        def make_norm(stack):
            xh = [sbt(stack, "xhat%d" % i, [128, D], BF16) for i in range(2)]
            ssq = sbt(stack, "ssq", [128, 4], F32)
            rstd = sbt(stack, "rstd", [128, 4], F32)
            B_xh = [Buf("xhat0"), Buf("xhat1")]
            B_st = Buf("st")

            def stats(src_tile, B_src):
                for tt in range(4):
                    A(lambda: nc.scalar.activation(xh[tt % 2][:], src_tile[:, tt, :], AF.Square, accum_out=ssq[:, tt:tt + 1]),
                      [B_src], [B_xh[tt % 2], B_st])
                A(lambda: nc.scalar.activation(rstd[:], ssq[:], AF.Sqrt, scale=1.0 / D, bias=EPS), [B_st], [B_st])
                V(lambda: nc.vector.reciprocal(rstd[:], rstd[:]), [B_st], [B_st])

            def rmsnorm_to_T(src_tile, dstT, B_src, B_dst):
                stats(src_tile, B_src)
                for tt in range(4):
                    xhat, B_xhat = xh[tt % 2], B_xh[tt % 2]
                    V(lambda: nc.vector.tensor_scalar(xhat[:], src_tile[:, tt, :], rstd[:, tt:tt + 1], None, ALU.mult),
                      [B_src, B_st], [B_xhat])
                    psv = PS2[tt % 2][:].bitcast(BF16)
                    pbs = [PB[2 * (tt % 2)], PB[2 * (tt % 2) + 1]]
                    for kt in range(KT):
                        T(lambda: nc.tensor.transpose(psv[:, kt * 128:(kt + 1) * 128], xhat[:, kt * 128:(kt + 1) * 128],
                                                      ident_b[:]), [B_xhat, BC], pbs)
                    A(lambda: nc.scalar.copy(dstT[:, :, tt * 128:(tt + 1) * 128],
                                             psv[:, 0:2048].rearrange("p (k t) -> p k t", k=KT)), pbs, [B_dst])
            return rmsnorm_to_T, stats, rstd, B_st

        def proj_ws(wload, wsrc_rows, nk, col0, n_mt, rhsT, B_rhs, evac):
            for m0 in range(0, n_mt, 4):
                nm = min(4, n_mt - m0)
                wt, wb = wload(wsrc_rows, nk, col0 + m0 * 128, nm * 128)
                for mi in range(nm):
                    bi = ps_rr()
                    for k in range(nk):
                        T(lambda: nc.tensor.matmul(bank(bi), wt[:, k, mi * 128:(mi + 1) * 128], rhsT[:, k, :],
                                                   start=(k == 0), stop=(k == nk - 1)), [wb, B_rhs], [PB[bi]])
                    evac(m0 + mi, bank(bi), PB[bi])

        with ExitStack() as ps:
            wload = make_wring(ps)
            rmsnorm_to_T, _, _, _ = make_norm(ps)
            xb = sbt(ps, "xb", [128, 4, D], F32)
            xnTs = [sbt(ps, "xnT%d" % i, [128, KT, NB], BF16) for i in range(2)]
            kst = [sbt(ps, "kst%d" % i, [128, NB], BF16) for i in range(4)]
            B_x = Buf("x")
            B_xnTs = [Buf("xnT0"), Buf("xnT1")]
            B_kst = [Buf() for _ in range(4)]
            kc = [0]
            NCB = CTX // NB
            jobs = [("own", b) for b in range(NBLK)] + [("ctx", b) for b in range(NCB)]
            for jb, (kind, b) in enumerate(jobs):
                xnT, B_xnT = xnTs[jb % 2], B_xnTs[jb % 2]
                t0 = b * NB
                if kind == "own":
                    xsrc, udst, kvoff = xs, u_scr, HALO + t0
                else:
                    xsrc, udst = xc, uc_scr
                    kvoff = (HALO + L + t0) if b < 2 else ((t0 - (CTX - HALO)) if b >= NCB - 2 else None)
                S.dma(xb[:], xsrc[t0:t0 + NB, :].rearrange("(t p) d -> p t d", p=128), writes=[B_x])
                rmsnorm_to_T(xb, xnT, B_x, B_xnT)

                def ev_u(mt, psap, psb):
                    i = kc[0] % 4
                    kc[0] += 1
                    cp(mt, kst[i][:], psap, [psb], [B_kst[i]])
                    S.dma(udst[mt * 128:(mt + 1) * 128, t0:t0 + NB], kst[i][:], reads=[B_kst[i]], q=POOL)
                proj_ws(wload, wb_in, KT, 0, 8, xnT, B_xnT, ev_u)
                if kvoff is None:
                    continue

                def ev_k(mt, psap, psb):
                    i = kc[0] % 4
                    kc[0] += 1
                    cp(mt, kst[i][:], psap, [psb], [B_kst[i]])
                    S.dma(kt_scr[mt, :, kvoff:kvoff + NB], kst[i][:], reads=[B_kst[i]], q=POOL)
                proj_ws(wload, wb_in, KT, K_OFF, NH, xnT, B_xnT, ev_k)
                for gi in range(3):
                    wt, wb = wload(wb_in, KT, V_OFF + gi * 512, 512)
                    for tt in range(4):
                        bi = ps_rr()
                        for k in range(KT):
                            T(lambda: nc.tensor.matmul(bank(bi), xnT[:, k, tt * 128:(tt + 1) * 128], wt[:, k, :],
                                                       start=(k == 0), stop=(k == KT - 1)), [wb, B_xnT], [PB[bi]])
                        i = kc[0] % 4
                        kc[0] += 1
                        cp(tt, kst[i][:], bank(bi), [PB[bi]], [B_kst[i]])
                        S.dma(v_scr[kvoff + tt * 128:kvoff + (tt + 1) * 128, gi * 512:(gi + 1) * 512], kst[i][:],
                              reads=[B_kst[i]], q=POOL)
            S.barrier()

        with ExitStack() as ps:
            T1re = sbt(ps, "T1re", [128, NG, NS], BF16)
            T1im = sbt(ps, "T1im", [128, NG, NS], BF16)
            T2re = sbt(ps, "T2re", [128, NG, 128], BF16)
            T2im = sbt(ps, "T2im", [128, NG, 128], BF16)
            G1 = sbt(ps, "G1", [128, NG], F32)
            G2 = sbt(ps, "G2", [128, NG], F32)
            Bmat = sbt(ps, "Bmat", [128, 8, 8, 128], BF16)
            Cm1 = sbt(ps, "Cm1", [128, NG, 16], BF16)
            Cm2 = sbt(ps, "Cm2", [128, NG, 16], BF16)
            BTAB = Buf("ssmtab")

            def cmul(o_re, o_im, x_re, x_im, y_re, y_im, t1, t2, bufs):
                f = lambda fn: S.op(DVE, fn, bufs, bufs)
                E = nc.vector
                f(lambda: E.tensor_tensor(t1, x_re, y_re, ALU.mult))
                f(lambda: E.tensor_tensor(t2, x_im, y_im, ALU.mult))
                f(lambda: E.tensor_tensor(t2, t1, t2, ALU.subtract))
                f(lambda: E.tensor_tensor(t1, x_re, y_im, ALU.mult))
                f(lambda: E.tensor_tensor(o_im, x_im, y_re, ALU.mult))
                f(lambda: E.tensor_tensor(o_im, o_im, t1, ALU.add))
                f(lambda: E.tensor_copy(out=o_re, in_=t2))

            def abar_of(ar, ai, ldt, shape, stack, tag):
                bufs = [BTAB]
                mk = lambda nm: sbt(stack, tag + nm, shape, F32)
                dt_, lr, th, mag, cs, sn, t1, t2 = (mk(n)[:] for n in ("dt", "lr", "th", "mag", "cs", "sn", "t1", "t2"))
                o = {k: mk(k)[:] for k in ("abr", "abi", "air", "aii", "fr", "fi")}
                f = lambda fn: S.op(DVE, fn, bufs, bufs)
                fa = lambda fn: S.op(ACT, fn, bufs, bufs)
                fa(lambda: nc.scalar.activation(dt_, ldt, AF.Exp))
                f(lambda: nc.vector.tensor_tensor(lr, ar, dt_, ALU.mult))
                f(lambda: nc.vector.tensor_tensor(th, ai, dt_, ALU.mult))
                f(lambda: nc.vector.tensor_copy(out=t2, in_=th))
                for jj in range(1, 8):
                    f(lambda: nc.vector.tensor_scalar(t1, th, (2 * jj - 1) * math.pi, -2.0 * math.pi, ALU.is_gt, ALU.mult))
                    f(lambda: nc.vector.tensor_tensor(t2, t2, t1, ALU.add))
                fa(lambda: nc.scalar.activation(sn, t2, AF.Sin))
                f(lambda: nc.vector.tensor_scalar(t1, t2, -1.0, None, ALU.mult))
                f(lambda: nc.vector.tensor_tensor(t1, t1, t2, ALU.max))
                f(lambda: nc.vector.tensor_scalar(t1, t1, -1.0, math.pi / 2, ALU.mult, ALU.add))
                fa(lambda: nc.scalar.activation(cs, t1, AF.Sin))
                fa(lambda: nc.scalar.activation(mag, lr, AF.Exp))
                f(lambda: nc.vector.tensor_tensor(o["abr"], mag, cs, ALU.mult))
                f(lambda: nc.vector.tensor_tensor(o["abi"], mag, sn, ALU.mult))
                fa(lambda: nc.scalar.activation(mag, lr, AF.Exp, scale=-1.0))
                f(lambda: nc.vector.tensor_tensor(o["air"], mag, cs, ALU.mult))
                f(lambda: nc.vector.tensor_tensor(o["aii"], mag, sn, ALU.mult))
                f(lambda: nc.vector.tensor_scalar(o["aii"], o["aii"], -1.0, None, ALU.mult))
                f(lambda: nc.vector.tensor_tensor(t1, ar, ar, ALU.mult))
                f(lambda: nc.vector.tensor_tensor(t2, ai, ai, ALU.mult))
                f(lambda: nc.vector.tensor_tensor(t1, t1, t2, ALU.add))
                f(lambda: nc.vector.reciprocal(t1, t1))
                f(lambda: nc.vector.tensor_scalar(cs, o["abr"], -1.0, None, ALU.add))
                f(lambda: nc.vector.tensor_tensor(t2, cs, ar, ALU.mult))
                f(lambda: nc.vector.tensor_tensor(mag, o["abi"], ai, ALU.mult))
                f(lambda: nc.vector.tensor_tensor(t2, t2, mag, ALU.add))
                f(lambda: nc.vector.tensor_tensor(o["fr"], t2, t1, ALU.mult))
                f(lambda: nc.vector.tensor_tensor(t2, o["abi"], ar, ALU.mult))
                f(lambda: nc.vector.tensor_tensor(mag, cs, ai, ALU.mult))
                f(lambda: nc.vector.tensor_tensor(t2, t2, mag, ALU.subtract))
                f(lambda: nc.vector.tensor_tensor(o["fi"], t2, t1, ALU.mult))
                return o

            def gen_ssm_tables(dr):
                rev = (dr == 1)
                bufs = [BTAB]
                f = lambda fn: S.op(DVE, fn, bufs, bufs)
                with ExitStack() as p2:
                    arT = sbt(p2, "arT", [128, NG], F32)
                    aiT = sbt(p2, "aiT", [128, NG], F32)
                    ldT = sbt(p2, "ldT", [128, NG], F32)
                    for half in (0, 64):
                        S.dma(arT[half:half + 64, :], a_re[0, dr].rearrange("g n -> n g"), writes=bufs)
                        S.dma(aiT[half:half + 64, :], a_im[0, dr].rearrange("g n -> n g"), writes=bufs)
                    S.dma(ldT[:], log_dt[0, dr].partition_broadcast(128), writes=bufs)
                    st = abar_of(arT[:], aiT[:], ldT[:], [128, NG], p2, "s_")
                    GB = 16
                    Are = sbt(p2, "Are", [128, GB, 128], F32)
                    Aim = sbt(p2, "Aim", [128, GB, 128], F32)
                    Nre = sbt(p2, "Nre", [128, GB, 128], F32)
                    Nim = sbt(p2, "Nim", [128, GB, 128], F32)
                    tA = sbt(p2, "tA", [128, GB, 64], F32)
                    tB = sbt(p2, "tB", [128, GB, 64], F32)
                    cur = [sbt(p2, "cur%d" % i, [128, GB], F32) for i in range(4)]
                    tc1 = sbt(p2, "tc1", [128, GB], F32)
                    tc2 = sbt(p2, "tc2", [128, GB], F32)

                    def sl(lo, hi):
                        return slice(128 - hi, 128 - lo) if rev else slice(lo, hi)
                    for gb in range(NG // GB):
                        gs = slice(gb * GB, (gb + 1) * GB)
                        for (Pre, Pim, b_re_, b_im_, one) in ((Are, Aim, st["abr"], st["abi"], True),
                                                               (Nre, Nim, st["air"], st["aii"], False)):
                            if one:
                                f(lambda: nc.vector.memset(Pre[:, :, sl(0, 1)], 1.0))
                                f(lambda: nc.vector.memset(Pim[:, :, sl(0, 1)], 0.0))
                            else:
                                f(lambda: nc.vector.tensor_copy(out=Pre[:, :, sl(0, 1)], in_=st["fr"][:, gs].unsqueeze(2)))
                                f(lambda: nc.vector.tensor_copy(out=Pim[:, :, sl(0, 1)], in_=st["fi"][:, gs].unsqueeze(2)))
                            f(lambda: nc.vector.tensor_copy(out=cur[0][:], in_=b_re_[:, gs]))
                            f(lambda: nc.vector.tensor_copy(out=cur[1][:], in_=b_im_[:, gs]))
                            cr, ci, nr, ni = cur[0], cur[1], cur[2], cur[3]
                            w = 1
                            while w < 128:
                                bc = lambda t: t[:].unsqueeze(2).broadcast_to([128, GB, w])
                                cmul(Pre[:, :, sl(w, 2 * w)], Pim[:, :, sl(w, 2 * w)], Pre[:, :, sl(0, w)], Pim[:, :, sl(0, w)],
                                     bc(cr), bc(ci), tA[:, :, 0:w], tB[:, :, 0:w], bufs)
                                if 2 * w < 128:
                                    cmul(nr[:], ni[:], cr[:], ci[:], cr[:], ci[:], tc1[:], tc2[:], bufs)
                                    cr, ci, nr, ni = nr, ni, cr, ci
                                w *= 2
                        f(lambda: nc.vector.tensor_copy(out=T2re[:, gs, :], in_=Are[:]))
                        f(lambda: nc.vector.tensor_copy(out=T2im[:, gs, :], in_=Aim[:]))
                        e127 = 0 if rev else 127
                        cmul(G1[:, gs], G2[:, gs], Are[:, :, e127], Aim[:, :, e127], st["abr"][:, gs], st["abi"][:, gs],
                             tc1[:], tc2[:], bufs)
                        for src_t, dst_t in ((Nre, T1re), (Nim, T1im)):
                            for q4 in range(GB // 8):
                                for gg in range(8):
                                    g_l = q4 * 8 + gg
                                    T(lambda: nc.tensor.transpose(PS2[0][:, gg * 64:(gg + 1) * 64], src_t[0:64, g_l, :],
                                                                  ident_f[0:64, 0:64]), bufs + [BC], [PB[0]])
                                g0 = gb * GB + q4 * 8
                                A(lambda: nc.scalar.copy(dst_t[:, g0:g0 + 8, :].rearrange("p g n -> p (g n)"),
                                                         PS2[0][:, 0:512]), [PB[0]], bufs)
                    f(lambda: nc.vector.tensor_scalar(G2[:], G2[:], sgn[:, 0:1], None, ALU.mult))
                with ExitStack() as p2:
                    bl = sbt(p2, "bl", [64, 2, NG, 16], F32)
                    S.dma(bl[:, 0], b_re[0, dr].rearrange("g n c -> n g c"), writes=bufs)
                    S.dma(bl[:, 1], b_im[0, dr].rearrange("g n c -> n g c"), writes=bufs)
                    bcomp = sbt(p2, "bcomp", [128, 8, 128], F32)
                    for kt in range(8):
                        for ri in range(2):
                            T(lambda: nc.tensor.transpose(PS2[0][:, ri * 64:(ri + 1) * 64],
                                                          bl[:, ri, kt * 8:(kt + 1) * 8, :].rearrange("p g c -> p (g c)"),
                                                          ident_f[0:64, 0:64]), bufs + [BC], [PB[0]])
                        A(lambda: nc.scalar.copy(bcomp[:, kt, :], PS2[0][:, 0:128]), [PB[0]], bufs)
                    for g8 in range(8):
                        f(lambda: nc.vector.tensor_scalar(Bmat[:, :, g8, :], bcomp[:], gmask[:, g8:g8 + 1], None, ALU.mult))
                    cl = sbt(p2, "cl", [128, 8, 2, NS], F32)
                    for cm, (top, bot) in ((Cm1, (c_re, c_im)), (Cm2, (c_im, c_re))):
                        S.dma(cl[:, :, 0, :], top[0, dr].rearrange("(k g) c n -> (g c) k n", g=8), writes=bufs)
                        S.dma(cl[:, :, 1, :], bot[0, dr].rearrange("(k g) c n -> (g c) k n", g=8), writes=bufs)
                        for kt in range(8):
                            T(lambda: nc.tensor.transpose(PS2[kt // 4][:, (kt % 4) * 128:(kt % 4 + 1) * 128],
                                                          cl[:, kt].rearrange("p r n -> p (r n)"), ident_f[:]),
                              bufs + [BC], [PB[0], PB[2]])
                        for hf in range(2):
                            A(lambda: nc.scalar.copy(cm[:, hf * 32:(hf + 1) * 32, :].rearrange("p g c -> p (g c)"),
                                                     PS2[hf][:, 0:512]), [PB[0], PB[2]], bufs)
                    f(lambda: nc.vector.tensor_scalar(Cm1[64:128], Cm1[64:128], -1.0, None, ALU.mult))
                    f(lambda: nc.vector.tensor_scalar(Cm2[:], Cm2[:], -1.0, None, ALU.mult))
                S.barrier()

            uTb = [sbt(ps, "uT%d" % i, [128, 8, NB], BF16) for i in range(3)]
            B_uTb = [Buf() for _ in range(3)]
            yaccs = [sbt(ps, "yacc%d" % i, [128, 8, NB], F32) for i in range(2)]
            B_yaccs = [Buf("yacc0"), Buf("yacc1")]
            NMB = 4
            M1 = [sbt(ps, "M1_%d" % i, [128, 4, 128], BF16) for i in range(NMB)]
            M2 = [sbt(ps, "M2_%d" % i, [128, 4, 128], BF16) for i in range(NMB)]
            H1 = [sbt(ps, "H1_%d" % i, [128, 4, 128], BF16) for i in range(NMB)]
            H2 = [sbt(ps, "H2_%d" % i, [128, 4, 128], BF16) for i in range(NMB)]
            Wp = [sbt(ps, "Wp_%d" % i, [128, 4, 128], BF16) for i in range(NMB)]
            B_Wp = [Buf() for _ in range(NMB)]
            B_M = [Buf() for _ in range(NMB)]
            B_H = [Buf() for _ in range(NMB)]
            B_H2 = [Buf() for _ in range(NMB)]
            ccar = [sbt(ps, "ccar%d" % i, [128, NG], F32) for i in range(2)]
            xcar = sbt(ps, "xcar", [128, NG], F32)
            tcar = sbt(ps, "tcar", [128, NG], F32)
            ytok = sbt(ps, "ytok", [128, SSMW], F32)
            B_car, B_ytok = Buf("car"), Buf("ytok")
            flg = sbt(ps, "flg", [128, 2], F32)
            if CTX:
                S.dma(flg[:], flags_in, writes=[BC])

            import os as _os
            NWARM = int(_os.environ.get("NWARM", "0"))

            def ssm_run(dr, chunks):
                tri, ntri = (tri_f, ntri_f) if dr == 0 else (tri_b, ntri_b)
                last = 127 if dr == 0 else 0
                units = [(ci, j) for ci in range(len(chunks)) for j in range(16)]
                NU = len(units)
                cprev = ccar[0]

                def Bu(t):
                    ci, j = units[t]
                    ck = chunks[ci]
                    if j == 0 and ck.get("pre"):
                        ck["pre"]()
                    kt, hf = j // 2, j % 2
                    tok = slice(ck["c"] * 128, (ck["c"] + 1) * 128)
                    T(lambda: nc.tensor.matmul(bank(t % 2), ck["uT"][:, kt, tok],
                                               Bmat[:, kt, hf * 4:(hf + 1) * 4, :].rearrange("p g x -> p (g x)"),
                                               start=True, stop=True), [ck["B_uT"], BTAB], [PB[t % 2]])

                def st1(t):
                    ci, j = units[t]
                    gs = slice(j * 4, (j + 1) * 4)
                    mi = t % NMB
                    pv = bank(t % 2).rearrange("p (g r n) -> p g r n", g=4, r=2)
                    V(lambda: nc.vector.tensor_tensor(M1[mi][:].rearrange("p g (r n) -> p g r n", r=2), pv,
                                                      T1re[:, gs, :].unsqueeze(2).broadcast_to([128, 4, 2, NS]), ALU.mult),
                      [PB[t % 2], BTAB], [B_M[mi]])
                    V(lambda: nc.vector.tensor_tensor(M2[mi][:].rearrange("p g (r n) -> p g r n", r=2), pv,
                                                      T1im[:, gs, :].unsqueeze(2).broadcast_to([128, 4, 2, NS]), ALU.mult),
                      [PB[t % 2], BTAB], [B_M[mi]])

                def csum(t):
                    ci, j = units[t]
                    ck = chunks[ci]
                    mi = t % NMB
                    if ck["summary"]:
                        for g4 in range(4):
                            g = j * 4 + g4
                            o = PS2[3][:, 512 + g:512 + g + 1]
                            T(lambda: nc.tensor.matmul(o, M1[mi][:, g4, :], tri_f[:, 127:128], start=True, stop=False,
                                                       skip_group_check=True), [B_M[mi], BC], [PB[7]])
                            T(lambda: nc.tensor.matmul(o[0:64, :], M2[mi][:, g4, 64:128], ntri_f[:, 127:128], start=False,
                                                       stop=False, skip_group_check=True), [B_M[mi], BC], [PB[7]])
                            T(lambda: nc.tensor.matmul(o[64:128, :], M2[mi][:, g4, 0:64], tri_f[:, 127:128], start=False,
                                                       stop=True, skip_group_check=True), [B_M[mi], BC], [PB[7]])
                        return
                    wbk = 2 + t % 2
                    for g4 in range(4):
                        o = bank(wbk)[:, g4 * 128:(g4 + 1) * 128]
                        T(lambda: nc.tensor.matmul(o, M1[mi][:, g4, :], tri[:], start=True, stop=False,
                                                   skip_group_check=True), [B_M[mi], BC], [PB[wbk]])
                        T(lambda: nc.tensor.matmul(o[0:64, :], M2[mi][:, g4, 64:128], ntri[:], start=False, stop=False,
                                                   skip_group_check=True), [B_M[mi], BC], [PB[wbk]])
                        T(lambda: nc.tensor.matmul(o[64:128, :], M2[mi][:, g4, 0:64], tri[:], start=False, stop=True,
                                                   skip_group_check=True), [B_M[mi], BC], [PB[wbk]])

                def carry_update(ck):
                    T(lambda: nc.tensor.matmul(PS2[3][:, 640:640 + NG], perm_f[:], xcar[:], start=True, stop=True,
                                               skip_group_check=True), [B_car, BC], [PB[7]])
                    V(lambda: nc.vector.tensor_tensor(tcar[:], G2[:], PS2[3][:, 640:640 + NG], ALU.mult),
                      [PB[7], BTAB, B_car], [B_car])
                    V(lambda: nc.vector.tensor_tensor(ccar[1][:], G1[:], xcar[:], ALU.mult), [B_car, BTAB], [B_car])
                    if ck.get("scale") is not None:
                        V(lambda: nc.vector.tensor_tensor(ccar[1][:], ccar[1][:], tcar[:], ALU.add), [B_car], [B_car])
                        V(lambda: nc.vector.tensor_scalar(ccar[0][:], ccar[1][:], ck["scale"], None, ALU.mult),
                          [B_car, BC], [B_car])
                    else:
                        V(lambda: nc.vector.tensor_tensor(ccar[0][:], ccar[1][:], tcar[:], ALU.add), [B_car], [B_car])

                def st2(t):
                    ci, j = units[t]
                    ck = chunks[ci]
                    mi = t % NMB
                    if ck["summary"]:
                        if j == 15:
                            if ck["first"]:
                                V(lambda: nc.vector.tensor_copy(out=xcar[:], in_=PS2[3][:, 512:512 + NG]), [PB[7]], [B_car])
                            else:
                                V(lambda: nc.vector.tensor_tensor(xcar[:], PS2[3][:, 512:512 + NG], cprev[:], ALU.add),
                                  [PB[7], B_car], [B_car])
                            carry_update(ck)
                        return
                    wbk = 2 + t % 2
                    gs = slice(j * 4, (j + 1) * 4)
                    wv = bank(wbk).rearrange("p (g t) -> p g t", g=4)
                    if ck["first"]:
                        V(lambda: nc.vector.tensor_copy(out=xcar[:, gs], in_=wv[:, :, last]), [PB[wbk]], [B_car])
                    else:
                        V(lambda: nc.vector.tensor_tensor(xcar[:, gs], wv[:, :, last], cprev[:, gs], ALU.add),
                          [PB[wbk], B_car], [B_car])
                    for g4 in range(4):
                        g = j * 4 + g4
                        wsl = bank(wbk)[:, g4 * 128:(g4 + 1) * 128]
                        if ck["first"]:
                            A(lambda: nc.scalar.copy(Wp[mi][:, g4, :], wsl), [PB[wbk]], [B_Wp[mi]])
                        else:
                            A(lambda: nc.scalar.activation(Wp[mi][:, g4, :], wsl, AF.Identity, bias=cprev[:, g:g + 1]),
                              [PB[wbk], B_car], [B_Wp[mi]])
                    if j == 15:
                        carry_update(ck)

                def st2b(t):
                    ci, j = units[t]
                    ck = chunks[ci]
                    if ck["summary"]:
                        return
                    mi = t % NMB
                    gs = slice(j * 4, (j + 1) * 4)
                    V(lambda: nc.vector.tensor_tensor(H1[mi][:], Wp[mi][:], T2re[:, gs, :], ALU.mult),
                      [B_Wp[mi], BTAB], [B_H[mi]])
                    G(lambda: nc.gpsimd.tensor_tensor(H2[mi][:], Wp[mi][:], T2im[:, gs, :], ALU.mult),
                      [B_Wp[mi], BTAB], [B_H2[mi]])

                def cproj(t):
                    ci, j = units[t]
                    ck = chunks[ci]
                    if ck["summary"]:
                        return
                    mi = t % NMB
                    for g4 in range(4):
                        g = j * 4 + g4
                        o = PS2[2][:, g * 16:(g + 1) * 16]
                        T(lambda: nc.tensor.matmul(o, H1[mi][:, g4, :], Cm1[:, g, :], start=True, stop=False,
                                                   skip_group_check=True), [B_H[mi], BTAB], [PB[4 + g // 32]])
                        T(lambda: nc.tensor.matmul(o, H2[mi][:, g4, :], Cm2[:, g, :], start=False, stop=True,
                                                   skip_group_check=True), [B_H2[mi], BTAB], [PB[4 + g // 32]])
                    if j == 15:
                        tok = slice(ck["c"] * 128, (ck["c"] + 1) * 128)
                        A(lambda: nc.scalar.copy(ytok[:], PS2[2][:]), [PB[4], PB[5]], [B_ytok])
                        for hf in range(2):
                            for k in range(4):
                                kk = hf * 4 + k
                                T(lambda: nc.tensor.transpose(bank(6)[:, k * 128:(k + 1) * 128],
                                                              ytok[:, kk * 128:(kk + 1) * 128], ident_f[:]),
                                  [B_ytok, BC], [PB[6]])
                            A(lambda: nc.scalar.copy(ck["yacc"][:, hf * 4:(hf + 1) * 4, tok],
                                                     bank(6).rearrange("p (k t) -> p k t", k=4)), [PB[6]], [ck["B_yacc"]])
                        if ck.get("post"):
                            ck["post"]()

                def warm(n):
                    for _ in range(n):
                        T(lambda: nc.tensor.matmul(bank(6), ident_b[:], T2re[:, 0:4, :].rearrange("p g t -> p (g t)"),
                                                   start=True, stop=True, skip_group_check=True), [BTAB, BC], [PB[6]])

                for t in range(-1, NU + 2):
                    if t + 1 < NU:
                        Bu(t + 1)
                        warm(NWARM)
                        st1(t + 1)
                    if 0 <= t < NU:
                        csum(t)
                    if 0 <= t - 2 < NU:
                        cproj(t - 2)
                    if 0 <= t < NU:
                        st2(t)
                    if 0 <= t - 1 < NU:
                        st2b(t - 1)

            def mk_chunks(dr, own_posts):
                chunks = []
                ub = [0]
                nctx = CTX // NB
                order = range(nctx - 1, -1, -1) if dr == 1 else range(nctx)
                corder = range(3, -1, -1) if dr == 1 else range(4)
                for ib, b in enumerate(order):
                    ui = ub[0] % 3
                    ub[0] += 1

                    def pre(ui=ui, b=b):
                        S.dma(uTb[ui][:], uc_scr[:, b * NB:(b + 1) * NB].rearrange("(k p) t -> p k t", p=128),
                              writes=[B_uTb[ui]])
                    for ic, c in enumerate(corder):
                        lastc = (ib == nctx - 1 and ic == 3)
                        chunks.append(dict(uT=uTb[ui], B_uT=B_uTb[ui], c=c, summary=True, first=(ib == 0 and ic == 0),
                                           pre=pre if ic == 0 else None,
                                           scale=(flg[:, 1:2] if dr == 1 else flg[:, 0:1]) if lastc else None))
                border = range(NBLK - 1, -1, -1) if dr == 1 else range(NBLK)
                for ib, b in enumerate(border):
                    ui = ub[0] % 3
                    ub[0] += 1
                    yi = ib % 2

                    def pre(ui=ui, b=b):
                        S.dma(uTb[ui][:], u_scr[:, b * NB:(b + 1) * NB].rearrange("(k p) t -> p k t", p=128),
                              writes=[B_uTb[ui]])
                    for ic, c in enumerate(corder):
                        post = None
                        if ic == 3:
                            post = (lambda b=b, ui=ui, yi=yi: own_posts(b, ui, yi))
                        chunks.append(dict(uT=uTb[ui], B_uT=B_uTb[ui], c=c, summary=False,
                                           first=(ib == 0 and ic == 0 and not CTX), pre=pre if ic == 0 else None,
                                           post=post, yacc=yaccs[yi], B_yacc=B_yaccs[yi]))
                return chunks

            gen_ssm_tables(1)

            def post_bwd(b, ui, yi):
                S.dma(yb_scr[:, b * NB:(b + 1) * NB].rearrange("(k p) t -> p k t", p=128), yaccs[yi][:],
                      reads=[B_yaccs[yi]], q=POOL)
            ssm_run(1, mk_chunks(1, post_bwd))
            S.barrier()

            gen_ssm_tables(0)
            wglu = sbt(ps, "wglu", [128, 8, SSMW], BF16)
            B_wglu = Buf()
            S.dma(wglu[:], wb_glu.rearrange("(k p) c -> p k c", p=128), writes=[B_wglu])
            ybt = [sbt(ps, "ybt%d" % i, [128, NB], F32) for i in range(2)]
            B_ybt = [Buf() for _ in range(2)]
            gt = [sbt(ps, "gt%d" % i, [128, NB], F32) for i in range(2)]
            B_gt = [Buf() for _ in range(2)]
            zT = sbt(ps, "zT", [128, 8, NB], BF16)
            ysT = sbt(ps, "ysT", [128, 8, NB], BF16)
            sgl = [sbt(ps, "sgl%d" % i, [128, NB], BF16) for i in range(2)]
            B_sgl = [Buf() for _ in range(2)]
            B_z, B_ys = Buf("z"), Buf("ys")
            yc = [0]

            def post_fwd(b, ui, yi):
                t0 = b * NB
                yacc, B_yacc = yaccs[yi], B_yaccs[yi]
                for k in range(8):
                    i = yc[0] % 2
                    yc[0] += 1
                    S.dma(ybt[i][:], yb_scr[k * 128:(k + 1) * 128, t0:t0 + NB], writes=[B_ybt[i]])
                    yk = yacc[:, k, :]
                    G(lambda: nc.gpsimd.tensor_tensor(yk, yk, ybt[i][:], ALU.add), [B_yacc, B_ybt[i]], [B_yacc])
                    G(lambda: nc.gpsimd.tensor_scalar(gt[0][:], uTb[ui][:, k, :], dcol[:, k:k + 1], None, ALU.mult),
                      [B_uTb[ui], BC], [B_gt[0]])
                    G(lambda: nc.gpsimd.tensor_tensor(yk, yk, gt[0][:], ALU.add), [B_yacc, B_gt[0]], [B_yacc])
                    if dbg:
                        S.dma(dbg_out["d_yb"][k * 128:(k + 1) * 128, t0:t0 + NB], ybt[i][:], reads=[B_ybt[i]])
                    G(lambda: nc.gpsimd.tensor_tensor(gt[0][:], yk, yk, ALU.mult), [B_yacc], [B_gt[0]])
                    G(lambda: nc.gpsimd.tensor_scalar(gt[0][:], gt[0][:], 0.044715, 1.0, ALU.mult, ALU.add), [B_gt[0]], [B_gt[0]])
                    G(lambda: nc.gpsimd.tensor_tensor(gt[0][:], gt[0][:], yk, ALU.mult), [B_gt[0], B_yacc], [B_gt[0]])
                    A(lambda: nc.scalar.activation(gt[1][:], gt[0][:], AF.Sigmoid, scale=GELU_C), [B_gt[0]], [B_gt[1]])
                    G(lambda: nc.gpsimd.tensor_tensor(zT[:, k, :], yk, gt[1][:], ALU.mult), [B_gt[1], B_yacc], [B_z])
                for mt in range(8):
                    for k in range(8):
                        T(lambda: nc.tensor.matmul(bank(7), wglu[:, k, mt * 128:(mt + 1) * 128], zT[:, k, :],
                                                   start=(k == 0), stop=(k == 7), skip_group_check=True),
                          [B_wglu, B_z], [PB[7]])
                    i = mt % 2
                    A(lambda: nc.scalar.activation(sgl[i][:], bank(7), AF.Sigmoid, bias=bgcol[:, mt:mt + 1]),
                      [PB[7], BC], [B_sgl[i]])
                    G(lambda: nc.gpsimd.tensor_tensor(ysT[:, mt, :], zT[:, mt, :], sgl[i][:], ALU.mult),
                      [B_z, B_sgl[i]], [B_ys])
                S.dma(ys_scr[:, t0:t0 + NB].rearrange("(k p) t -> p k t", p=128), ysT[:], reads=[B_ys], q=POOL)
                if dbg:
                    for k in range(8):
                        G(lambda: nc.gpsimd.tensor_copy(out=gt[0][:], in_=ysT[:, k, :]), [B_ys], [B_gt[0]])
                        S.dma(dbg_out["d_ys"][k * 128:(k + 1) * 128, t0:t0 + NB], gt[0][:], reads=[B_gt[0]])
            ssm_run(0, mk_chunks(0, post_fwd))
            S.barrier()

        with ExitStack() as ps:
            wload = make_wring(ps)
            rmsnorm_to_T, stats, rstd, B_st = make_norm(ps)
            gfin = sbt(ps, "gfin", [128, D], F32)
            S.dma(gfin[:], final_g.partition_broadcast(128), writes=[BC])
            NQs = (128, 128, 32)
            bt_hi = sbt(ps, "bt_hi", [128, 24 * 128], BF16)
            bt_lo = sbt(ps, "bt_lo", [128, 24 * 128], BF16)
            BT = {}
            with ExitStack() as p2:
                bst = sbt(p2, "bst", [128, 24 * 128], F32)
                V(lambda: nc.vector.memset(bst[:], 0.0), writes=[BBT])
                for gi in range(3):
                    nq = NQs[gi]
                    for j in range(4):
                        for k2 in range(2):
                            col = ((gi * 4 + j) * 2 + k2) * 128
                            src = bass.AP(bias_rep_h, (gi * 4 + j) * 128 * 384 + 255 - 128 * k2, [[383, 128], [1, nq]])
                            S.dma(bst[:, col:col + nq], src, reads=[BBT], writes=[BBT])
                            BT[(gi, j, k2)] = (bt_hi[:, col:col + nq], bt_lo[:, col:col + nq])
                V(lambda: nc.vector.tensor_copy(out=bt_hi[:], in_=bst[:]), [BBT], [BBT])
                V(lambda: nc.vector.tensor_tensor(bst[:], bst[:], bt_hi[:], ALU.subtract), [BBT], [BBT])
                V(lambda: nc.vector.tensor_copy(out=bt_lo[:], in_=bst[:]), [BBT], [BBT])
                S.barrier()
            xb = sbt(ps, "xb", [128, 4, D], F32)
            xnT = sbt(ps, "xnT", [128, KT, NB], BF16)
            ysT = sbt(ps, "ysT2", [128, 8, NB], BF16)
            qT = sbt(ps, "qT", [128, NH, NB], BF16)
            mixT = sbt(ps, "mixT", [128, KT, NB], BF16)
            yaT = sbt(ps, "yaT", [128, 4, NB], BF16)
            sg = [sbt(ps, "sg%d" % i, [128, NB], BF16) for i in range(2)]
            sgt = [sbt(ps, "sgt%d" % i, [128, NB], BF16) for i in range(2)]
            B_x, B_xnT, B_ys, B_q, B_mixT, B_ya = (Buf(n) for n in "x xnT ys q mixT ya".split())
            B_sg = [Buf() for _ in range(2)]
            B_sgt = [Buf() for _ in range(2)]
            sgc = [0]
            kwin, B_kwin = [], []
            for gi_ in range(3):
                for i_ in range(2):
                    kwin.append(sbt(ps, "kwin%d_%d" % (gi_, i_), [128, NB + 128 * DIL[gi_]], BF16))
                    B_kwin.append(Buf())
            NVW, NPT = 5, 4
            vwin = [sbt(ps, "vwin%d" % i, [128, 2, 256], BF16) for i in range(NVW)]
            B_vwin = [Buf() for _ in range(NVW)]
            kmc = [sbt(ps, "kmc%d" % i, [128, 2], F32) for i in range(NVW)]
            PT = [sbt(ps, "PT%d" % i, [128, 2, 128], BF16) for i in range(NPT)]
            B_PT = [Buf() for _ in range(NPT)]
            rden = sbt(ps, "rden", [128, NB], F32)
            B_rden = Buf()
            print("pass2b sbuf remaining before actq:", nc.sbuf_bytes_remaining)
            actq = sbt(ps, "actq", [128, 8, NB], BF16)
            relu_t = [sbt(ps, "relu%d" % i, [128, NB], BF16) for i in range(2)]
            B_relu = [Buf() for _ in range(2)]
            B_actq = Buf()
            outt = actq[:].rearrange("p k t -> p (k t)").bitcast(F32)
            B_outt = B_actq
            kwc, vwc, ptc = [0], [0], [0]

            for b in range(NBLK):
                t0 = b * NB
                S.dma(xb[:], xs[t0:t0 + NB, :].rearrange("(t p) d -> p t d", p=128), writes=[B_x])
                S.dma(ysT[:], ys_scr[:, t0:t0 + NB].rearrange("(k p) t -> p k t", p=128), writes=[B_ys])
                rmsnorm_to_T(xb, xnT, B_x, B_xnT)
                for m0 in range(0, KT, 4):
                    wtg, wbg = wload(wb_in, KT, G_OFF + m0 * 128, 512)
                    wtb, wbb = wload(wb_brs, 8, m0 * 128, 512)
                    for mi in range(4):
                        mt = m0 + mi
                        bi = ps_rr()
                        for k in range(KT):
                            T(lambda: nc.tensor.matmul(bank(bi), wtg[:, k, mi * 128:(mi + 1) * 128], xnT[:, k, :],
                                                       start=(k == 0), stop=(k == KT - 1)), [wbg, B_xnT], [PB[bi]])
                        i = sgc[0] % 2
                        sgc[0] += 1
                        A(lambda: nc.scalar.activation(sg[i][:], bank(bi), AF.Sigmoid), [PB[bi]], [B_sg[i]])
                        bj = ps_rr()
                        for k in range(8):
                            T(lambda: nc.tensor.matmul(bank(bj), wtb[:, k, mi * 128:(mi + 1) * 128], ysT[:, k, :],
                                                       start=(k == 0), stop=(k == 7)), [wbb, B_ys], [PB[bj]])
                        V(lambda: nc.vector.tensor_tensor(mixT[:, mt, :], bank(bj), sg[i][:], ALU.mult),
                          [PB[bj], B_sg[i]], [B_mixT])

                def ev_q(mt, psap, psb):
                    A(lambda: nc.scalar.activation(qT[:, mt, :], psap, AF.Copy, scale=HD ** -0.5), [psb], [B_q])
                proj_ws(wload, wb_in, KT, Q_OFF, NH, xnT, B_xnT, ev_q)

                for rnd in range(2):
                    first_mm = {0: True, 1: True}
                    items = []
                    unit_list = []
                    for gi in range(3):
                        for u in range(4 if gi == 0 else DIL[gi]):
                            unit_list.append((gi, u))
                            for jj in range(2):
                                items.append((gi, u, jj, len(unit_list) - 1))
                    uinfo = {}

                    def unit_geom(gi, u):
                        d = DIL[gi]
                        reach = 64 * d
                        if gi == 0:
                            return slice(u * 128, (u + 1) * 128), u * 128, 1, t0 + u * 128 - 64
                        return slice(u, NB, d), u, d, t0 + u - reach

                    def load_unit(ui_):
                        gi, u = unit_list[ui_]
                        nq = NQs[gi]
                        qsl, kcol0, kstep, tstart = unit_geom(gi, u)
                        if u == 0:
                            reach = 64 * DIL[gi]
                            for jj in range(2):
                                h = gi * 4 + rnd * 2 + jj
                                i = gi * 2 + jj
                                S.dma(kwin[i][:, 0:NB + 2 * reach], kt_scr[h, :, HALO + t0 - reach:HALO + t0 + NB + reach],
                                      writes=[B_kwin[i]])
                                kwmap[(gi, jj)] = i
                        vi = vwc[0] % NVW
                        vwc[0] += 1
                        c0 = gi * 512 + rnd * 256
                        r0 = HALO + tstart
                        RS = NH * HD
                        if nq == 128:
                            src = bass.AP(v_scr_h, r0 * RS + c0, [[kstep * RS, 128], [128 * kstep * RS, 2], [1, 256]])
                            S.dma(vwin[vi][:, :, :], src, writes=[B_vwin[vi]])
                            srcm = bass.AP(kmask_h, r0, [[kstep, 128], [128 * kstep, 2]])
                            S.dma(kmc[vi][:, :], srcm, writes=[B_vwin[vi]])
                        else:
                            for k2, n2 in ((0, 128), (1, nq)):
                                rr0 = r0 + k2 * 128 * kstep
                                src = bass.AP(v_scr_h, rr0 * RS + c0, [[kstep * RS, n2], [1, 256]])
                                S.dma(vwin[vi][0:n2, k2, :], src, writes=[B_vwin[vi]])
                                srcm = bass.AP(kmask_h, rr0, [[kstep, n2], [1, 1]])
                                S.dma(kmc[vi][0:n2, k2:k2 + 1], srcm, writes=[B_vwin[vi]])
                        uinfo[ui_] = vi

                    def scores(it):
                        gi, u, jj, ui_ = it
                        nq = NQs[gi]
                        nk2 = (128, nq)
                        qsl, kcol0, kstep, tstart = unit_geom(gi, u)
                        vi = uinfo[ui_]
                        j = rnd * 2 + jj
                        h = gi * 4 + j
                        ki = kwmap[(gi, jj)]
                        pi_ = ptc[0] % NPT
                        ptc[0] += 1
                        sb_i = ps_rr()
                        for k2 in range(2):
                            n2 = nk2[k2]
                            kc0 = kcol0 + k2 * 128 * kstep
                            ksl = slice(kc0, kc0 + (n2 - 1) * kstep + 1, kstep)
                            so = bank(sb_i)[0:n2, k2 * 128:k2 * 128 + nq]
                            bhi, blo = BT[(gi, j, k2)]
                            T(lambda: nc.tensor.matmul(so, kwin[ki][:, ksl], qT[:, h, qsl], start=True, stop=False,
                                                       skip_group_check=True), [B_kwin[ki], B_q], [PB[sb_i]])
                            T(lambda: nc.tensor.matmul(so, ident_b[0:n2, 0:n2], bhi[0:n2, :], start=False, stop=False,
                                                       skip_group_check=True), [BBT, BC], [PB[sb_i]])
                            T(lambda: nc.tensor.matmul(so, ident_b[0:n2, 0:n2], blo[0:n2, :], start=False, stop=True,
                                                       skip_group_check=True), [BBT, BC], [PB[sb_i]])
                            A(lambda: nc.scalar.activation(PT[pi_][0:n2, k2, 0:nq], so, AF.Exp,
                                                           bias=kmc[vi][0:n2, k2:k2 + 1]),
                              [PB[sb_i], B_vwin[vi]], [B_PT[pi_]])
                        return pi_

                    def pv(it, pi_):
                        gi, u, jj, ui_ = it
                        nq = NQs[gi]
                        nk2 = (128, nq)
                        qsl, kcol0, kstep, tstart = unit_geom(gi, u)
                        vi = uinfo[ui_]
                        ob, db = 4 + jj, 6 + jj
                        for k2 in range(2):
                            n2 = nk2[k2]
                            T(lambda: nc.tensor.matmul(bank(ob)[:, qsl], vwin[vi][0:n2, k2, jj * 128:(jj + 1) * 128],
                                                       PT[pi_][0:n2, k2, 0:nq], start=first_mm[jj], stop=False,
                                                       skip_group_check=True),
                              [B_vwin[vi], B_PT[pi_]], [PB[ob]])
                            T(lambda: nc.tensor.matmul(bank(db)[:, qsl], ones_b[0:n2, :], PT[pi_][0:n2, k2, 0:nq],
                                                       start=first_mm[jj], stop=False, skip_group_check=True),
                              [BC, B_PT[pi_]], [PB[db]])
                            first_mm[jj] = False

                    kwmap = {}
                    AHEAD = 3
                    for ui_ in range(min(AHEAD, len(unit_list))):
                        load_unit(ui_)
                    pend = []
                    for idx, it in enumerate(items):
                        if it[2] == 0 and it[3] + AHEAD < len(unit_list):
                            load_unit(it[3] + AHEAD)
                        pi_ = scores(it)
                        pend.append((it, pi_))
                        if len(pend) > 2:
                            pv(*pend.pop(0))
                    while pend:
                        pv(*pend.pop(0))
                    for jj in range(2):
                        j = rnd * 2 + jj
                        V(lambda: nc.vector.tensor_scalar(rden[:], bank(6 + jj), 1e-30, None, ALU.add), [PB[6 + jj]], [B_rden])
                        V(lambda: nc.vector.reciprocal(rden[:], rden[:]), [B_rden], [B_rden])
                        V(lambda: nc.vector.tensor_tensor(yaT[:, j, :], bank(4 + jj), rden[:], ALU.mult),
                          [PB[4 + jj], B_rden], [B_ya])
                if dbg:
                    for j in range(4):
                        V(lambda: nc.vector.tensor_copy(out=rden[:], in_=yaT[:, j, :]), [B_ya], [B_rden])
                        S.dma(dbg_out["d_ya"][j * 128:(j + 1) * 128, t0:t0 + NB], rden[:], reads=[B_rden])

                for m0 in range(0, KT, 4):
                    wtg, wbg = wload(wb_in, KT, G_OFF + D + m0 * 128, 512)
                    wtb, wbb = wload(wb_bra, 4, m0 * 128, 512)
                    for mi in range(4):
                        mt = m0 + mi
                        bi = ps_rr()
                        for k in range(KT):
                            T(lambda: nc.tensor.matmul(bank(bi), wtg[:, k, mi * 128:(mi + 1) * 128], xnT[:, k, :],
                                                       start=(k == 0), stop=(k == KT - 1)), [wbg, B_xnT], [PB[bi]])
                        i = sgc[0] % 2
                        sgc[0] += 1
                        A(lambda: nc.scalar.activation(sg[i][:], bank(bi), AF.Sigmoid), [PB[bi]], [B_sg[i]])
                        bj = ps_rr()
                        for k in range(4):
                            T(lambda: nc.tensor.matmul(bank(bj), wtb[:, k, mi * 128:(mi + 1) * 128], yaT[:, k, :],
                                                       start=(k == 0), stop=(k == 3)), [wbb, B_ya], [PB[bj]])
                        V(lambda: nc.vector.tensor_tensor(sgt[i][:], bank(bj), sg[i][:], ALU.mult),
                          [PB[bj], B_sg[i]], [B_sgt[i]])
                        G(lambda: nc.gpsimd.tensor_tensor(mixT[:, mt, :], sgt[i][:], mixT[:, mt, :], ALU.add),
                          [B_sgt[i], B_mixT], [B_mixT])

                for cg in range(4):
                    wt, wb = wload(wb_out, KT, cg * 512, 512)
                    for tt in range(4):
                        bi = ps_rr()
                        for k in range(KT):
                            T(lambda: nc.tensor.matmul(bank(bi), mixT[:, k, tt * 128:(tt + 1) * 128], wt[:, k, :],
                                                       start=(k == 0), stop=(k == KT - 1)), [wb, B_mixT], [PB[bi]])
                        xsl = xb[:, tt, cg * 512:(cg + 1) * 512]
                        V(lambda: nc.vector.tensor_tensor(xsl, bank(bi), xsl, ALU.add), [PB[bi], B_x], [B_x])
                rmsnorm_to_T(xb, xnT, B_x, B_xnT)
                for qf in range(8):
                    def ev_ff1(mt, psap, psb):
                        i = sgc[0] % 2
                        sgc[0] += 1
                        A(lambda: nc.scalar.activation(relu_t[i][:], psap, AF.Relu), [psb], [B_relu[i]])
                        V(lambda: nc.vector.scalar_tensor_tensor(actq[:, mt, :], psap, 0.0, relu_t[i][:], ALU.max, ALU.mult),
                          [psb, B_relu[i]], [B_actq])
                    proj_ws(wload, wb_ff1, KT, qf * 1024, 8, xnT, B_xnT, ev_ff1)
                    for cg in range(4):
                        wt, wb = wload(wb_ff2[qf * 1024:(qf + 1) * 1024, :], 8, cg * 512, 512)
                        for tt in range(4):
                            bi = ps_rr()
                            for k in range(8):
                                T(lambda: nc.tensor.matmul(bank(bi), actq[:, k, tt * 128:(tt + 1) * 128], wt[:, k, :],
                                                           start=(k == 0), stop=(k == 7)), [wb, B_actq], [PB[bi]])
                            xsl = xb[:, tt, cg * 512:(cg + 1) * 512]
                            V(lambda: nc.vector.tensor_tensor(xsl, bank(bi), xsl, ALU.add), [PB[bi], B_x], [B_x])
                stats(xb, B_x)
                for tt in range(4):
                    V(lambda: nc.vector.scalar_tensor_tensor(outt, xb[:, tt, :], rstd[:, tt:tt + 1], gfin[:],
                                                             ALU.mult, ALU.mult), [B_x, B_st, BC], [B_outt])
                    S.dma(ys[t0 + tt * 128:t0 + (tt + 1) * 128, :], outt, reads=[B_outt], q=POOL)
            S.barrier()
    return nc


_PROG = {}


def _get_prog(L, dbg=False, CTX=0):
    key = (L, dbg, CTX)
    if key not in _PROG:
        _PROG[key] = build_program(L, dbg, CTX)
    return _PROG[key]


def _kmask(L, nvalid):
    m = np.full((L + 2 * HALO, 1), NEG, np.float32)
    m[HALO:HALO + nvalid] = 0.0
    return m


def kernel(**inputs):
    f32 = lambda a: np.ascontiguousarray(np.asarray(a, dtype=np.float32))
    xp = f32(inputs["x_prompt"])
    xsm = f32(inputs["x_sample"])
    B, SL, _ = xp.shape
    LS = xsm.shape[1]
    assert xsm.shape[0] == 1 and LS == 2 * SL and B + 2 <= 8
    L = SL
    shared = {}
    for k in ("norm1_g", "w_in", "ssm_a_re", "ssm_a_im", "ssm_log_dt", "ssm_b_re", "ssm_b_im", "ssm_c_re",
              "ssm_c_im", "ssm_d", "w_glu", "b_glu", "w_br_ssm", "w_br_attn", "w_out", "norm2_g", "w_ff1",
              "w_ff2", "rel_bias", "final_g"):
        shared[k] = f32(inputs[k])
    shared["onehot"] = _t5_onehot()
    zeros_ctx = np.zeros((L, D), np.float32)

    def km(left, right):
        m = np.full((L + 2 * HALO, 1), NEG, np.float32)
        m[HALO:HALO + L] = 0.0
        if left:
            m[:HALO] = 0.0
        if right:
            m[HALO + L:] = 0.0
        return m

    def fl(a, b):
        f = np.zeros((128, 2), np.float32)
        f[:, 0] = a
        f[:, 1] = b
        return f
    in_maps = []
    for c in range(8):
        m = dict(shared)
        if c < B:
            m["xs"], m["xc"], m["kmask"], m["flags"] = xp[c], zeros_ctx, km(False, False), fl(0.0, 0.0)
        elif c == B:
            m["xs"], m["xc"], m["kmask"], m["flags"] = xsm[0, :L], xsm[0, L:], km(False, True), fl(0.0, 1.0)
        elif c == B + 1:
            m["xs"], m["xc"], m["kmask"], m["flags"] = xsm[0, L:], xsm[0, :L], km(True, False), fl(1.0, 0.0)
        else:
            m["xs"], m["xc"], m["kmask"], m["flags"] = zeros_ctx, zeros_ctx, km(False, False), fl(0.0, 0.0)
        in_maps.append(m)
    nc = _get_prog(L, CTX=L)
    res = run_bass_kernel_spmd(nc, in_maps, core_ids=list(range(8)))
    y_prompt = np.stack([np.asarray(res.results[c]["ys"], dtype=np.float32) for c in range(B)], axis=0)
    y_sample = np.concatenate([np.asarray(res.results[B]["ys"], dtype=np.float32),
                               np.asarray(res.results[B + 1]["ys"], dtype=np.float32)], axis=0)[None]
    return (y_prompt, y_sample)
```
